# Optimizing a Trainium2 kernel written in Bass

```python
import math
import jax, jax.numpy as jnp
from jax import lax
import numpy as np

D_MODEL = 1024
BATCH = 8
SEQ = 8192
DEPTH = 1

CTX_LEN = 256
GRID_W = 64
ROPE_THETA = 10000.0
Q_BLOCK = 128
LN_EPS = 1e-5
RMS_EPS = 1e-6
SUBLN_EPS = 1e-5

MLA_HEADS = 8
MLA_NOPE = 64
MLA_ROPE = 32
MLA_V = 64
MLA_Q_LORA = 384
MLA_KV_LORA = 256
MLA_WIDTH = MLA_HEADS * MLA_V
MLA_SCALE = (MLA_NOPE + MLA_ROPE) ** -0.5

DIFF_HEADS = 4
DIFF_HD = 64
DIFF_WIDTH = DIFF_HEADS * 2 * DIFF_HD
DIFF_SCALE = DIFF_HD ** -0.5

IN_SPLITS = (
    MLA_Q_LORA,
    MLA_KV_LORA,
    MLA_ROPE,
    MLA_WIDTH,
    DIFF_WIDTH,
    DIFF_WIDTH,
    DIFF_WIDTH,
    DIFF_WIDTH,
    2 * D_MODEL,
)
N_IN = sum(IN_SPLITS)
IN_OFFSETS = tuple(int(o) for o in np.cumsum(IN_SPLITS)[:-1])

ALPHA = (2 * DEPTH) ** 0.25
BETA = (8 * DEPTH) ** -0.25

kernel_name = "hybrid_mla_diffattn_dit_layer"


def _rms_norm(x, w, eps):
    x32 = x.astype(jnp.float32)
    y = x32 * lax.rsqrt(jnp.mean(x32 * x32, axis=-1, keepdims=True) + eps)
    return y.astype(x.dtype) * w


def _layer_norm(x, g, b):
    x32 = x.astype(jnp.float32)
    mu = jnp.mean(x32, axis=-1, keepdims=True)
    var = jnp.mean(jnp.square(x32 - mu), axis=-1, keepdims=True)
    return ((x32 - mu) * lax.rsqrt(var + LN_EPS)).astype(x.dtype) * g + b


def _rope_1d(x, pos):
    half = x.shape[-1] // 2
    inv = ROPE_THETA ** (-jnp.arange(half, dtype=jnp.float32) / half)
    ang = pos[:, None] * inv[None, :]
    cos = jnp.cos(ang).astype(x.dtype)
    sin = jnp.sin(ang).astype(x.dtype)
    x1, x2 = x[..., :half], x[..., half:]
    return jnp.concatenate([x1 * cos - x2 * sin, x1 * sin + x2 * cos], axis=-1)


def _rope_2d(x, row, col):
    half = x.shape[-1] // 2
    return jnp.concatenate([_rope_1d(x[..., :half], row), _rope_1d(x[..., half:], col)], axis=-1)


def _blockwise(fn, qs):
    b, h, s, _ = qs[0].shape
    blocks = tuple(q.reshape(b, h, s // Q_BLOCK, Q_BLOCK, q.shape[-1]).transpose(2, 0, 1, 3, 4) for q in qs)
    out = lax.map(lambda qb: fn(*qb), blocks)
    n = out.shape[0]
    return out.transpose(1, 2, 0, 3, 4).reshape(b, h, n * Q_BLOCK, out.shape[-1])


def _softmax_attend(q, k, v, scale):
    s = jnp.einsum('bhqd,bhkd->bhqk', q, k).astype(jnp.float32) * scale
    p = jax.nn.softmax(s, axis=-1).astype(v.dtype)
    return jnp.einsum('bhqk,bhkd->bhqd', p, v)


def _diff_attend(q1, q2, k1, k2, v, lam, scale):
    s1 = jnp.einsum('bhqd,bhkd->bhqk', q1, k1).astype(jnp.float32) * scale
    s2 = jnp.einsum('bhqd,bhkd->bhqk', q2, k2).astype(jnp.float32) * scale
    p = jax.nn.softmax(s1, axis=-1) - lam * jax.nn.softmax(s2, axis=-1)
    return jnp.einsum('bhqk,bhkd->bhqd', p.astype(v.dtype), v)


def _branch_inputs(p, q_norm, kv_norm, w_uq, w_ukv, row, col):
    c_q, c_kv, k_r, gate_a, dq, dk, dv, gate_d, merge = jnp.split(p, IN_OFFSETS, axis=-1)
    b, s, _ = p.shape
    q = jnp.einsum('bsr,rhd->bhsd', _rms_norm(c_q, q_norm, RMS_EPS),
                   w_uq.reshape(MLA_Q_LORA, MLA_HEADS, MLA_NOPE + MLA_ROPE))
    kv = jnp.einsum('bsr,rhd->bhsd', _rms_norm(c_kv, kv_norm, RMS_EPS),
                    w_ukv.reshape(MLA_KV_LORA, MLA_HEADS, MLA_NOPE + MLA_V))
    q_nope, q_rope = q[..., :MLA_NOPE], q[..., MLA_NOPE:]
    k_nope, v_mla = kv[..., :MLA_NOPE], kv[..., MLA_NOPE:]
    k_r = k_r[:, None]
    dq = dq.reshape(b, s, DIFF_HEADS, 2, DIFF_HD).transpose(0, 2, 3, 1, 4)
    dk = dk.reshape(b, s, DIFF_HEADS, 2, DIFF_HD).transpose(0, 2, 3, 1, 4)
    dv = dv.reshape(b, s, DIFF_HEADS, 2 * DIFF_HD).transpose(0, 2, 1, 3)
    if row is not None:
        q_rope = _rope_2d(q_rope, row, col)
        k_r = _rope_2d(k_r, row, col)
        dq = _rope_2d(dq, row, col)
        dk = _rope_2d(dk, row, col)
    q_mla = jnp.concatenate([q_nope, q_rope], axis=-1)
    k_mla = jnp.concatenate([k_nope, jnp.broadcast_to(k_r, k_nope.shape[:-1] + (MLA_ROPE,))], axis=-1)
    return (q_mla, k_mla, v_mla, dq, dk, dv), (gate_a, gate_d, merge)


def _mix(q_mla, k_mla, v_mla, dq, dk, dv, gates, lam, lam_init, diff_subln, w_oa, w_ob, w_out):
    gate_a, gate_d, merge = gates
    b, s = gate_a.shape[0], gate_a.shape[1]
    o_a = _blockwise(lambda q: _softmax_attend(q, k_mla, v_mla, MLA_SCALE), (q_mla,))
    k1, k2 = dk[:, :, 0], dk[:, :, 1]
    o_d = _blockwise(lambda q1, q2: _diff_attend(q1, q2, k1, k2, dv, lam, DIFF_SCALE),
                     (dq[:, :, 0], dq[:, :, 1]))
    o_d = _rms_norm(o_d, diff_subln, SUBLN_EPS) * (1.0 - lam_init)
    o_a = o_a.transpose(0, 2, 1, 3).reshape(b, s, MLA_WIDTH)
    o_d = o_d.transpose(0, 2, 1, 3).reshape(b, s, DIFF_WIDTH)
    y_a = (o_a * jax.nn.silu(gate_a)) @ w_oa
    y_d = (o_d * jax.nn.silu(gate_d)) @ w_ob
    m_a, m_d = merge[..., :D_MODEL], merge[..., D_MODEL:]
    return (jax.nn.sigmoid(m_a) * y_a + jax.nn.sigmoid(m_d) * y_d) @ w_out


def setup_inputs(seed: int = 0) -> dict:
    key = jax.random.key(seed)
    ks = jax.random.split(key, 20)
    nrm = lambda k, shape, s: jax.random.normal(k, shape, jnp.float32) * s
    L, D = DEPTH, D_MODEL
    return {
        "x": nrm(ks[0], (BATCH, SEQ, D), 1.0),
        "c": nrm(ks[1], (BATCH, D), 1.0),
        "ctx": nrm(ks[2], (BATCH, CTX_LEN, D), 1.0),
        "c_ctx": nrm(ks[3], (D,), 1.0),
        "w_mod": nrm(ks[4], (L, D, 3 * D), 0.5 * D ** -0.5),
        "b_mod": nrm(ks[5], (L, 3 * D), 0.01),
        "w_in": nrm(ks[6], (L, D, N_IN), D ** -0.5),
        "mla_q_norm": 1.0 + nrm(ks[7], (L, MLA_Q_LORA), 0.01),
        "mla_kv_norm": 1.0 + nrm(ks[8], (L, MLA_KV_LORA), 0.01),
        "w_uq": nrm(ks[9], (L, MLA_Q_LORA, MLA_HEADS * (MLA_NOPE + MLA_ROPE)), MLA_Q_LORA ** -0.5),
        "w_ukv": nrm(ks[10], (L, MLA_KV_LORA, MLA_HEADS * (MLA_NOPE + MLA_V)), MLA_KV_LORA ** -0.5),
        "diff_lambda": nrm(ks[11], (L, 4, DIFF_HD), 0.1),
        "diff_subln": 1.0 + nrm(ks[12], (L, 2 * DIFF_HD), 0.01),
        "w_oa": nrm(ks[13], (L, MLA_WIDTH, D), BETA * MLA_WIDTH ** -0.5),
        "w_ob": nrm(ks[14], (L, DIFF_WIDTH, D), BETA * DIFF_WIDTH ** -0.5),
        "w_out": nrm(ks[15], (L, D, D), BETA * D ** -0.5),
        "ln_g": 1.0 + nrm(ks[16], (L, D), 0.01),
        "ln_b": nrm(ks[17], (L, D), 0.01),
    }


def reference(x, c, ctx, c_ctx, w_mod, b_mod, w_in, mla_q_norm, mla_kv_norm, w_uq, w_ukv,
              diff_lambda, diff_subln, w_oa, w_ob, w_out, ln_g, ln_b):
    n = x.shape[1]
    ROWS = n // GRID_W
    row = jnp.broadcast_to(jnp.arange(ROWS, dtype=jnp.float32)[:, None], (ROWS, GRID_W)).reshape(-1)
    col = jnp.broadcast_to(jnp.arange(GRID_W, dtype=jnp.float32)[None, :], (ROWS, GRID_W)).reshape(-1)
    for l in range(DEPTH):
        shift, scale, gate = jnp.split(jax.nn.silu(c) @ w_mod[l] + b_mod[l], 3, axis=-1)
        shift_c, scale_c, gate_c = jnp.split(jax.nn.silu(c_ctx) @ w_mod[l] + b_mod[l], 3, axis=-1)
        h = x * (1.0 + scale[:, None]) + shift[:, None]
        h_c = ctx * (1.0 + scale_c) + shift_c
        lat_t, lat_g = _branch_inputs(h @ w_in[l], mla_q_norm[l], mla_kv_norm[l], w_uq[l], w_ukv[l], row, col)
        ctx_t, ctx_g = _branch_inputs(h_c @ w_in[l], mla_q_norm[l], mla_kv_norm[l], w_uq[l], w_ukv[l], None, None)
        lam_init = 0.8 - 0.6 * math.exp(-0.3 * l)
        lq1, lk1, lq2, lk2 = diff_lambda[l, 0], diff_lambda[l, 1], diff_lambda[l, 2], diff_lambda[l, 3]
        lam = (jnp.exp(jnp.sum(lq1 * lk1).astype(jnp.float32))
               - jnp.exp(jnp.sum(lq2 * lk2).astype(jnp.float32)) + lam_init)
        q_mla, k_mla, v_mla, dq, dk, dv = lat_t
        q_mla_c, k_mla_c, v_mla_c, dq_c, dk_c, dv_c = ctx_t
        k_all = jnp.concatenate([k_mla_c, k_mla], axis=2)
        v_all = jnp.concatenate([v_mla_c, v_mla], axis=2)
        dk_all = jnp.concatenate([dk_c, dk], axis=3)
        dv_all = jnp.concatenate([dv_c, dv], axis=2)
        y = _mix(q_mla, k_all, v_all, dq, dk_all, dv_all, lat_g, lam, lam_init,
                 diff_subln[l], w_oa[l], w_ob[l], w_out[l])
        if l + 1 < DEPTH:
            y_c = _mix(q_mla_c, k_mla_c, v_mla_c, dq_c, dk_c, dv_c, ctx_g, lam, lam_init,
                       diff_subln[l], w_oa[l], w_ob[l], w_out[l])
            ctx = _layer_norm(ALPHA * ctx + gate_c * y_c, ln_g[l], ln_b[l])
        x = _layer_norm(ALPHA * x + gate[:, None] * y, ln_g[l], ln_b[l])
    return x
```

```python
import math
from contextlib import ExitStack

import numpy as np
import concourse.bass as bass
import concourse.mybir as mybir
from concourse.bass_utils import run_bass_kernel_spmd

F32 = mybir.dt.float32
BF16 = mybir.dt.bfloat16
AF = mybir.ActivationFunctionType
ALU = mybir.AluOpType
AX = mybir.AxisListType

D = 1024
T = 8192
CTX = 256
NK = T + CTX
NKB = NK // 128
W = 256
NA = 3392
MLA_SCALE = 96 ** -0.5
DIFF_SCALE = 64 ** -0.5
LN_EPS = 1e-5
RMS_EPS = 1e-6
SUBLN_EPS = 1e-5
ALPHA = 2 ** 0.25
LAM_INIT = 0.8 - 0.6 * math.exp(0.0)
ROPE_THETA = 10000.0
DBG = {}


class Buf:
    __slots__ = ("name", "w", "r", "excl")

    def __init__(self, name, excl=False):
        self.name = name
        self.w = None
        self.r = {}
        self.excl = excl


class Chan:
    __slots__ = ("sem", "count", "idx")

    def __init__(self, sem, idx):
        self.sem = sem
        self.count = 0
        self.idx = idx


class Op:
    __slots__ = ("eng", "fn", "deps", "sig", "chan", "need", "seq", "tag")


def _flat(bs):
    outl = []
    for b in bs:
        if isinstance(b, (list, tuple)):
            outl.extend(_flat(b))
        else:
            outl.append(b)
    return outl


class Sched:
    CAP = 30000

    def __init__(self, nc, es, nchan=84):
        self.nc = nc
        self.es = es
        self.h = {"pe": nc.tensor, "act": nc.scalar, "dve": nc.vector, "pool": nc.gpsimd, "sp": nc.sync}
        self.ops = []
        self.bufs = []
        self.nsig = {e: 0 for e in self.h}
        self.sems = {e: [] for e in self.h}
        self.waited = {e: {} for e in self.h}
        self.chans = [Chan(es.enter_context(nc.semaphore(f"dch{i}")), i) for i in range(nchan)]
        self.chan_next = 0
        self.seq = 0
        self.last = {}
        self.n_inst = 0
        self.tag = None

    def buf(self, name="b", excl=False):
        b = Buf(name, excl)
        self.bufs.append(b)
        return b

    def bufs_n(self, n, name="b", excl=False):
        return [self.buf(f"{name}{i}", excl) for i in range(n)]

    def pbanks(self, n, name="pb"):
        return self.bufs_n(n, name, excl=True)

    def chan(self):
        c = self.chans[self.chan_next]
        self.chan_next += 1
        return c

    @staticmethod
    def _key(o):
        return ("c", o.chan.idx) if o.chan is not None else ("e", o.eng)

    def op(self, eng, fn, reads=(), writes=(), chan=None):
        o = Op()
        o.eng = eng
        o.fn = fn
        o.chan = chan
        o.sig = None
        o.need = False
        o.tag = self.tag
        self.seq += 1
        o.seq = self.seq
        deps = {}
        reads = _flat(reads)
        writes = _flat(writes)

        def add(d):
            if d is None:
                return
            k = self._key(d)
            if k not in deps or deps[k].seq < d.seq:
                deps[k] = d

        k_self = self._key(o)
        for b in reads:
            if b.excl:
                for kk, r in b.r.items():
                    if kk != k_self:
                        add(r)
            else:
                add(b.w)
        for b in writes:
            if b.excl:
                for kk, r in b.r.items():
                    if kk != k_self:
                        add(r)
            else:
                add(b.w)
                for r in b.r.values():
                    add(r)
        for b in writes:
            if b.excl:
                b.r[k_self] = o
            else:
                b.w = o
                b.r = {}
        for b in reads:
            b.r[k_self] = o
        o.deps = []
        for k, d in deps.items():
            if k == ("e", "pe") and eng == "pe" and chan is None:
                continue
            d.need = True
            o.deps.append(d)
        self.ops.append(o)
        self.last[k_self] = o
        return o

    def flush(self):
        lasts = list(self.last.values())
        for e in self.h:
            o = Op()
            o.eng = e
            o.fn = None
            o.chan = None
            o.sig = None
            o.need = False
            o.tag = None
            self.seq += 1
            o.seq = self.seq
            o.deps = [d for d in lasts if not (d.eng == "pe" and e == "pe" and d.chan is None)]
            for d in o.deps:
                d.need = True
            self.ops.append(o)
        for o in self.ops:
            if o.chan is not None:
                o.chan.count += 1
                o.sig = (("c", o.chan.idx), o.chan.sem, 16 * o.chan.count)
            elif o.need:
                k = self.nsig[o.eng]
                si = k // self.CAP
                while len(self.sems[o.eng]) <= si:
                    self.sems[o.eng].append(
                        self.es.enter_context(self.nc.semaphore(f"s_{o.eng}{len(self.sems[o.eng])}")))
                o.sig = (("e", o.eng, si), self.sems[o.eng][si], k % self.CAP + 1)
                self.nsig[o.eng] += 1
        for o in self.ops:
            h = self.h[o.eng]
            wt = self.waited[o.eng]
            for d in o.deps:
                key, sem, val = d.sig
                if wt.get(key, 0) >= val:
                    continue
                h.wait_ge(sem, val)
                wt[key] = val
                self.n_inst += 1
            if o.fn is not None:
                ins = o.fn(h)
                if o.tag is not None and DBG.get("annotate"):
                    ins.annotate(o.tag)
                self.n_inst += 1
                if o.sig is not None:
                    ins.then_inc(o.sig[1], 16 if o.chan is not None else 1)
        self.ops = []
        self.last = {}
        for b in self.bufs:
            b.w = None
            b.r = {}
        self.bufs = []
        self.chan_next = 0

    def mm(self, out, lhsT, rhs, start, stop, reads, writes):
        return self.op("pe", lambda h: h.matmul(out, lhsT, rhs, start=start, stop=stop), reads, writes)

    def tr(self, out, in_, ident, reads, writes):
        return self.op("pe", lambda h: h.transpose(out, in_, ident), reads, writes)

    def act(self, out, in_, func, reads, writes, bias=None, scale=None, accum_out=None):
        kw = {}
        if bias is not None:
            kw["bias"] = bias
        if scale is not None:
            kw["scale"] = scale
        if accum_out is not None:
            kw["accum_out"] = accum_out
        return self.op("act", lambda h: h.activation(out, in_, func, **kw), reads, writes)

    def tt(self, eng, out, in0, in1, op, reads, writes):
        return self.op(eng, lambda h: h.tensor_tensor(out, in0, in1, op), reads, writes)

    def ts(self, eng, out, in0, s1, s2, op0, op1, reads, writes):
        if s2 is None:
            return self.op(eng, lambda h: h.tensor_scalar(out, in0, s1, None, op0), reads, writes)
        return self.op(eng, lambda h: h.tensor_scalar(out, in0, s1, s2, op0, op1), reads, writes)

    def stt(self, out, in0, scalar, in1, op0, op1, reads, writes):
        return self.op("dve", lambda h: h.scalar_tensor_tensor(out, in0, scalar, in1, op0, op1), reads, writes)

    def cp(self, eng, out, in_, reads, writes):
        if eng == "act":
            return self.op("act", lambda h: h.activation(out, in_, AF.Copy), reads, writes)
        return self.op(eng, lambda h: h.tensor_copy(out, in_), reads, writes)

    def recip(self, out, in_, reads, writes):
        return self.op("dve", lambda h: h.reciprocal(out, in_), reads, writes)

    def memset(self, eng, ap, val, writes):
        return self.op(eng, lambda h: h.memset(ap, val), (), writes)

    def dma(self, out, in_, reads, writes, chan):
        return self.op("sp", lambda h: h.dma_start(out=out, in_=in_), reads, writes, chan=chan)


def build_nc(debug=False, phases="0ABC"):
    nc = bass.Bass("TRN2", target_bir_lowering=False)

    def din(name, shape):
        return nc.dram_tensor(name, shape, F32, kind="ExternalInput").ap()

    xk = din("xk", [NK, D])
    cvec = din("cvec", [128, 8, 2])
    w_mod = din("w_mod", [D, 3 * D])
    bmodT = din("bmodT", [128, 24])
    bgate = din("bgate", [1, D])
    wA = din("wA", [D, NA])
    wG = din("wG", [D, 3072])
    wuq2 = din("wuq2", [384, 1536])
    wukv = din("wukv", [256, 1024])
    qnT = din("qnT", [128, 3])
    kvnT = din("kvnT", [128, 2])
    lamb = din("lamb", [1, 256])
    sublnT = din("sublnT", [128, 1])
    w_oa = din("w_oa", [512, D])
    w_ob = din("w_ob", [512, D])
    w_out = din("w_out", [D, D])
    ln_g = din("ln_g", [1, D])
    ln_b = din("ln_b", [1, D])
    rope_tab = din("rope_tab", [128, 4, NK])
    ident_d = din("ident", [128, 128])
    out = nc.dram_tensor("out", [T, D], F32, kind="ExternalOutput").ap()

    skind = "ExternalOutput" if debug else "Internal"

    def scr(name, shape, dt):
        return nc.dram_tensor(name, shape, dt, kind=skind).ap()

    qaT = scr("qaT", [8, 96, T], BF16)
    kaT = scr("kaT", [8, 96, NK], BF16)
    vaS = scr("vaS", [8, 128, NKB * 65], BF16)
    qdT = scr("qdT", [4, 128, T], BF16)
    kdT = scr("kdT", [4, 128, NK], BF16)
    vdS = scr("vdS", [4, 128, NKB * 128], BF16)
    oaT = scr("oaT", [512, T], F32)
    saS = scr("saS", [8, T], F32)
    od1T = scr("od1T", [512, T], F32)
    od2T = scr("od2T", [512, T], F32)
    sdS = scr("sdS", [8, T], F32)
    if debug:
        dbg_mod = nc.dram_tensor("dbg_mod", [128, 16, 2], F32, kind="ExternalOutput").ap()
        dbg_gate = nc.dram_tensor("dbg_gate", [128, D], F32, kind="ExternalOutput").ap()
        dbg_misc = nc.dram_tensor("dbg_misc", [2, 128, 1], F32, kind="ExternalOutput").ap()

    with ExitStack() as es:
        S = Sched(nc, es)

        def sb(stack, name, shape, dt):
            return stack.enter_context(nc.sbuf_tensor("sb_" + name, shape, dt))

        P = es.enter_context(nc.psum_tensor("P", [128, 4096], F32))
        ident = sb(es, "ident", [128, 128], F32)
        ones_f = sb(es, "ones_f", [128, 128], F32)
        ones_b = sb(es, "ones_b", [128, 128], BF16)
        modT = sb(es, "modT", [128, 16, 2], F32)
        gate_bc = sb(es, "gate_bc", [128, D], F32)
        lng_bc = sb(es, "lng_bc", [128, D], F32)
        lnb_bc = sb(es, "lnb_bc", [128, D], F32)
        neglam = sb(es, "neglam", [128, 1], F32)
        sublnS = sb(es, "sublnS", [128, 1], F32)
        qn_t = sb(es, "qn_t", [128, 3], F32)
        kvn_t = sb(es, "kvn_t", [128, 2], F32)

        with ExitStack() as ph:
            wm = sb(ph, "wm", [128, 8, 3072], F32)
            cv = sb(ph, "cv", [128, 8, 2], F32)
            sg0 = sb(ph, "sg0", [128, 8, 2], F32)
            sc = sb(ph, "sc", [128, 8, 2], F32)
            screp = sb(ph, "screp", [128, 8, 128], F32)
            bmT = sb(ph, "bmT", [128, 24], F32)
            bg = sb(ph, "bg", [128, D], F32)
            lamt = sb(ph, "lamt", [128, 256], F32)
            lprod = sb(ph, "lprod", [128, 128], F32)
            lsum = sb(ph, "lsum", [128, 4], F32)
            subt = sb(ph, "subt", [128, 1], F32)

            b_wm, b_cv, b_sc, b_screp, b_bmT, b_bg, b_lamt, b_lprod, b_lsum, b_subt, b_sg0 = S.bufs_n(11, "p0")
            b_ident, b_onesf, b_onesb, b_modT, b_gate, b_lng, b_lnb, b_neglam, b_subln, b_qn, b_kvn = S.bufs_n(11, "c0")
            b_ps0, b_ps1, b_ps2 = S.pbanks(3, "ps")

            S.dma(ident[:], ident_d[:, :], (), [b_ident], S.chan())
            S.dma(cv[:], cvec[:, :, :], (), [b_cv], S.chan())
            S.dma(bmT[:], bmodT[:, :], (), [b_bmT], S.chan())
            S.dma(wm[:], w_mod.rearrange("(k p) n -> p k n", p=128), (), [b_wm], S.chan())
            S.dma(bg[:], bgate[0:1, :].broadcast_to([128, D]), (), [b_bg], S.chan())
            S.dma(lng_bc[:], ln_g[0:1, :].broadcast_to([128, D]), (), [b_lng], S.chan())
            S.dma(lnb_bc[:], ln_b[0:1, :].broadcast_to([128, D]), (), [b_lnb], S.chan())
            S.dma(lamt[:], lamb[0:1, :].broadcast_to([128, 256]), (), [b_lamt], S.chan())
            S.dma(subt[:], sublnT[:, :], (), [b_subt], S.chan())
            S.dma(qn_t[:], qnT[:, :], (), [b_qn], S.chan())
            S.dma(kvn_t[:], kvnT[:, :], (), [b_kvn], S.chan())
            S.memset("dve", ones_f[:], 1.0, [b_onesf])
            S.memset("pool", ones_b[:], 1.0, [b_onesb])

            S.act(sg0[:], cv[:], AF.Sigmoid, [b_cv], [b_sg0])
            S.tt("dve", sc[:], cv[:], sg0[:], ALU.mult, [b_cv, b_sg0], [b_sc])
            for k in range(8):
                S.ts("dve", screp[:, k, :], ones_f[:, :], sc[:, k, 0:1], None, ALU.mult, None,
                     [b_onesf, b_sc], [b_screp])
            for nck in range(16):
                for k in range(8):
                    S.mm(P[:, nck * 2:nck * 2 + 2], wm[:, k, nck * 128:(nck + 1) * 128], sc[:, k, :],
                         k == 0, k == 7, [b_wm, b_sc], [b_ps0])
            for wch in range(2):
                S.tt("dve", modT[:, :, wch], P[:, wch:32:2], bmT[:, 0:16], ALU.add, [b_ps0, b_bmT], [b_modT])
            S.ts("dve", modT[:, 8:16, :], modT[:, 8:16, :], 1.0, None, ALU.add, None, [b_modT], [b_modT])
            for half in range(2):
                bps = (b_ps1, b_ps2)[half]
                for k in range(8):
                    S.mm(P[:, 512 * (half + 1):512 * (half + 2)], screp[:, k, :],
                         wm[:, k, 2048 + half * 512:2048 + (half + 1) * 512], k == 0, k == 7,
                         [b_screp, b_wm], [bps])
                S.tt("dve", gate_bc[:, half * 512:(half + 1) * 512], P[:, 512 * (half + 1):512 * (half + 2)],
                     bg[:, half * 512:(half + 1) * 512], ALU.add, [bps, b_bg], [b_gate])
            S.tt("dve", lprod[:, 0:64], lamt[:, 0:64], lamt[:, 64:128], ALU.mult, [b_lamt], [b_lprod])
            S.tt("dve", lprod[:, 64:128], lamt[:, 128:192], lamt[:, 192:256], ALU.mult, [b_lamt], [b_lprod])
            S.op("dve", lambda h: h.reduce_sum(lsum[:, 0:1], lprod[:, 0:64], AX.X), [b_lprod], [b_lsum])
            S.op("dve", lambda h: h.reduce_sum(lsum[:, 1:2], lprod[:, 64:128], AX.X), [b_lprod], [b_lsum])
            S.act(lsum[:, 2:4], lsum[:, 0:2], AF.Exp, [b_lsum], [b_lsum])
            S.tt("dve", neglam[:], lsum[:, 3:4], lsum[:, 2:3], ALU.subtract, [b_lsum], [b_neglam])
            S.ts("dve", neglam[:], neglam[:], -LAM_INIT, None, ALU.add, None, [b_neglam], [b_neglam])
            S.ts("dve", sublnS[:], subt[:], 1.0 - LAM_INIT, None, ALU.mult, None, [b_subt], [b_subln])
            if debug:
                S.dma(dbg_mod[:, :, :], modT[:], [b_modT], (), S.chan())
                S.dma(dbg_gate[:, :], gate_bc[:], [b_gate], (), S.chan())
                S.dma(dbg_misc[0, :, :], neglam[:], [b_neglam], (), S.chan())
                S.dma(dbg_misc[1, :, :], sublnS[:], [b_subln], (), S.chan())
            S.flush()

        cw_stack = ExitStack()
        WG = WOA = WOB = WOUT = None
        if "A" in phases:
            phase_A(nc, S, sb, P, locals())
        es.enter_context(cw_stack)
        WG = sb(cw_stack, "WG", [128, 8, 3072], BF16)
        WOA = sb(cw_stack, "WOA", [128, 4, D], BF16)
        WOB = sb(cw_stack, "WOB", [128, 4, D], BF16)
        WOUT = sb(cw_stack, "WOUT", [128, 8, D], BF16)
        cw_loaded = [False]
        if "B" in phases:
            phase_B(nc, S, sb, P, locals())
        if "C" in phases:
            phase_C(nc, S, sb, P, locals())
    return nc


def load_weight_bf16(*a, **kw):
    for _ in load_weight_bf16_gen(*a, **kw):
        pass


def load_weight_bf16_gen(S, dst, dst_bufs, src, nrow_chunks, ncols, stg, stg_bufs, chans, engines=("dve", "act"),
                         rowscale=None, rowscale_buf=None, piece=2048, cnt=[0], colscale=None, colscale_buf=None):
    for k in range(nrow_chunks):
        c0 = 0
        while c0 < ncols:
            cw = min(piece, ncols - c0)
            i = cnt[0] % len(stg)
            cnt[0] += 1
            S.dma(stg[i][:, 0:cw], src[k * 128:(k + 1) * 128, c0:c0 + cw], (), [stg_bufs[i]], chans[i])
            ei = cnt[0] % len(engines)
            eng = engines[ei]
            dst_buf = dst_bufs[ei]
            if colscale is not None:
                S.tt(eng, dst[:, k, c0:c0 + cw], stg[i][:, 0:cw], colscale[:, c0:c0 + cw], ALU.mult,
                     [stg_bufs[i], colscale_buf], [dst_buf])
            elif rowscale is None:
                S.cp(eng, dst[:, k, c0:c0 + cw], stg[i][:, 0:cw], [stg_bufs[i]], [dst_buf])
            elif eng == "act":
                S.act(dst[:, k, c0:c0 + cw], stg[i][:, 0:cw], AF.Copy, [stg_bufs[i], rowscale_buf], [dst_buf],
                      scale=rowscale[:, k:k + 1])
            else:
                S.ts(eng, dst[:, k, c0:c0 + cw], stg[i][:, 0:cw], rowscale[:, k:k + 1], None, ALU.mult, None,
                     [stg_bufs[i], rowscale_buf], [dst_buf])
            c0 += cw
            yield


def phase_A(nc, S, sb, P, env):
    ident, ones_b, modT, qn_t, kvn_t = env["ident"], env["ones_b"], env["modT"], env["qn_t"], env["kvn_t"]
    xk, wA, wuq2, wukv, rope_tab = env["xk"], env["wA"], env["wuq2"], env["wukv"], env["rope_tab"]
    qaT, kaT, vaS, qdT, kdT, vdS = env["qaT"], env["kaT"], env["vaS"], env["qdT"], env["kdT"], env["vdS"]
    NCH = DBG.get('nchA', NK // W)
    with ExitStack() as ph:
        WA = sb(ph, "WA", [128, 8, NA], BF16)
        WUQ = sb(ph, "WUQ", [128, 3, 1536], BF16)
        WUKV = sb(ph, "WUKV", [128, 2, 1024], BF16)
        stg = [sb(ph, f"wstg{i}", [128, 2048], F32) for i in range(2)]
        xin = [sb(ph, f"xin{i}", [128, 2, D], F32) for i in range(2)]
        tab = [sb(ph, f"tab{i}", [128, 4, W], F32) for i in range(2)]
        hT = [sb(ph, f"hT{i}", [128, 8, W], BF16) for i in range(2)]
        cq = sb(ph, "cq", [128, 5, W], BF16)
        sq = sb(ph, "sq", [128, 5, W], BF16)
        NTMP = 8
        tmp = [sb(ph, f"tmp{i}", [128, W], F32) for i in range(NTMP)]
        rq = sb(ph, "rq", [128, W], F32)
        rkv = sb(ph, "rkv", [128, W], F32)
        csr = sb(ph, "csr", [128, W], F32)
        snr = sb(ph, "snr", [128, W], F32)
        rkvt = sb(ph, "rkvt", [128, 2], F32)
        QDs = [sb(ph, f"QDs{i}", [128, 4, W], BF16) for i in range(2)]
        KDs = [sb(ph, f"KDs{i}", [128, 4, W], BF16) for i in range(2)]
        VDs = [sb(ph, f"VDs{i}", [128, 4, 2, 128], BF16) for i in range(2)]
        QAs = [sb(ph, f"QAs{i}", [128, 8, W], BF16) for i in range(2)]
        KAs = [sb(ph, f"KAs{i}", [128, 8, W], BF16) for i in range(2)]
        VAs = [sb(ph, f"VAs{i}", [128, 8, 2, 65], BF16) for i in range(2)]

        b_ident, b_onesb, b_modT, b_qn, b_kvn = S.bufs_n(5, "cA")
        b_WA, b_WUQ, b_WUKV = (S.bufs_n(2, n) for n in ("wA", "wUQ", "wUKV"))
        b_stg = S.bufs_n(2, "stg")
        b_xin = S.bufs_n(2, "xin")
        b_tab = S.bufs_n(2, "tab")
        b_hT = [S.bufs_n(8, f"hT{i}_") for i in range(2)]
        b_cq = S.bufs_n(5, "cq")
        b_sq = S.bufs_n(5, "sq")
        b_tmp = S.bufs_n(NTMP, "tmp")
        b_rq, b_rkv, b_csr, b_snr, b_rkvt = S.bufs_n(5, "r")
        b_QDs = [S.bufs_n(4, f"QDs{i}_") for i in range(2)]
        b_KDs = [S.bufs_n(4, f"KDs{i}_") for i in range(2)]
        b_VDs = [S.bufs_n(2, f"VDs{i}_") for i in range(2)]
        b_QAs = [S.bufs_n(8, f"QAs{i}_") for i in range(2)]
        b_KAs = [S.bufs_n(8, f"KAs{i}_") for i in range(2)]
        b_KAr = [S.bufs_n(8, f"KAr{i}_") for i in range(2)]
        b_VAs = [S.bufs_n(2, f"VAs{i}_") for i in range(2)]
        b_VAones = S.bufs_n(2, "VAones")
        pbk = S.pbanks(8, "pbk")
        ch_stg = [S.chan() for _ in range(2)]
        ch_x = [S.chan() for _ in range(2)]
        ch_tab = [S.chan() for _ in range(2)]
        ch_st = {n: [S.chan() for _ in range(2)] for n in ("qd", "kd", "vd", "qa", "ka", "va")}

        load_weight_bf16(S, WA, b_WA, wA, 8, NA, stg, b_stg, ch_stg)
        load_weight_bf16(S, WUQ, b_WUQ, wuq2, 3, 1536, stg, b_stg, ch_stg, rowscale=qn_t, rowscale_buf=b_qn)
        load_weight_bf16(S, WUKV, b_WUKV, wukv, 2, 1024, stg, b_stg, ch_stg, rowscale=kvn_t, rowscale_buf=b_kvn)
        for i in range(2):
            S.memset("pool", VAs[i][:, :, :, 64:65], 1.0, [b_VAones[i]])

        bank_rr = [0]

        def get_bank():
            f = 4 + bank_rr[0] % 4
            bank_rr[0] += 1
            return f

        def Pb(f, half, m=128):
            return P[0:m, f * 512 + half * 256:f * 512 + (half + 1) * 256]

        tmp_rr = [0]

        def get_tmp():
            i = tmp_rr[0] % NTMP
            tmp_rr[0] += 1
            return i

        def load(ci):
            s = ci % 2
            S.dma(xin[s][:], xk[ci * W:(ci + 1) * W, :].rearrange("(j p) d -> p j d", p=128), (), [b_xin[s]], ch_x[s])

        def load_tab(ci):
            s = ci % 2
            S.dma(tab[s][:], rope_tab[:, :, ci * W:(ci + 1) * W], (), [b_tab[s]], ch_tab[s])

        def transposes(ci):
            s = ci % 2
            wch = 1 if ci == 0 else 0
            for dk in range(8):
                for j in range(2):
                    S.tr(P[:, dk * 256 + j * 128:dk * 256 + (j + 1) * 128], xin[s][:, j, dk * 128:(dk + 1) * 128],
                         ident[:], [b_xin[s], b_ident], [pbk[dk // 2]])
            for dk in range(8):
                S.act(hT[s][:, dk, :], Pb(dk // 2, dk % 2), AF.Identity, [pbk[dk // 2], b_modT], [b_hT[s][dk]],
                      bias=modT[:, dk, wch:wch + 1], scale=modT[:, 8 + dk, wch:wch + 1])

        def proj_job(s, tiles):
            f = get_bank()
            for idx, (c0, m) in enumerate(tiles):
                for k in range(8):
                    S.mm(Pb(f, idx, m), WA[:, k, c0:c0 + m], hT[s][:, k, :], k == 0, k == 7,
                         [b_hT[s][k], b_WA], [pbk[f]])
            return f

        def rope_from_bank(f, cs, sn, m, tab_bufs):
            t1 = get_tmp()
            S.tt("dve", tmp[t1][0:m, :], Pb(f, 0, m), cs, ALU.mult, [pbk[f]] + tab_bufs, [b_tmp[t1]])
            t2 = get_tmp()
            S.tt("dve", tmp[t2][0:m, :], Pb(f, 1, m), sn, ALU.mult, [pbk[f]] + tab_bufs, [b_tmp[t2]])
            return t1, t2

        def stage1(ci):
            s = ci % 2
            latent = ci > 0
            for grp in ((0, 1), (2, 3), (4,)):
                f = proj_job(s, [(m * 128, 128) for m in grp])
                for idx, m in enumerate(grp):
                    S.cp("dve", cq[:, m, :], Pb(f, idx), [pbk[f]], [b_cq[m]])
                    S.act(sq[:, m, :], Pb(f, idx), AF.Square, [pbk[f]], [b_sq[m]])
            f = proj_job(s, [(640, 96), (736, 96)])
            t1, t2 = rope_from_bank(f, tab[s][0:96, 2, :], tab[s][0:96, 3, :], 96, [b_tab[s]])
            for h in range(8):
                S.tt("pool", KAs[s][64:96, h, :], tmp[t1][64:96, :], tmp[t2][64:96, :], ALU.add,
                     [b_tmp[t1], b_tmp[t2]], [b_KAr[s][h]])
            for (c_n, c_s, dst, dbuf, need) in ((832, 1344, QDs, b_QDs, latent), (1856, 2368, KDs, b_KDs, True)):
                if not need:
                    continue
                for hh in range(4):
                    f = proj_job(s, [(c_n + hh * 128, 128), (c_s + hh * 128, 128)])
                    t1, t2 = rope_from_bank(f, tab[s][:, 0, :], tab[s][:, 1, :], 128, [b_tab[s]])
                    S.tt("pool", dst[s][:, hh, :], tmp[t1][:, :], tmp[t2][:, :], ALU.add,
                         [b_tmp[t1], b_tmp[t2]], [dbuf[s][hh]])
            for j in range(2):
                f = get_bank()
                for k in range(8):
                    S.mm(P[:, f * 512:(f + 1) * 512], hT[s][:, k, j * 128:(j + 1) * 128], WA[:, k, 2880:3392],
                         k == 0, k == 7, [b_hT[s][k], b_WA], [pbk[f]])
                S.cp("act", VDs[s][:, :, j, :], P[:, f * 512:(f + 1) * 512].rearrange("p (h d) -> p h d", h=4),
                     [pbk[f]], [b_VDs[s][j]])

        def stage2(ci):
            s = ci % 2
            latent = ci > 0
            f = get_bank()
            for c in range(3):
                S.mm(Pb(f, 0, 96), ones_b[:, 0:96], sq[:, c, :], c == 0, c == 2, [b_onesb, b_sq[c]], [pbk[f]])
            for c in range(2):
                S.mm(Pb(f, 1, 64), ones_b[:, 0:64], sq[:, 3 + c, :], c == 0, c == 1, [b_onesb, b_sq[3 + c]], [pbk[f]])
            S.act(rq[0:96, :], Pb(f, 0, 96), AF.Sqrt, [pbk[f]], [b_rq], bias=RMS_EPS, scale=1.0 / 384)
            S.act(rkv[0:64, :], Pb(f, 1, 64), AF.Sqrt, [pbk[f]], [b_rkv], bias=RMS_EPS, scale=1.0 / 256)
            S.recip(rq[0:96, :], rq[0:96, :], [b_rq], [b_rq])
            S.recip(rkv[0:64, :], rkv[0:64, :], [b_rkv], [b_rkv])
            f = get_bank()
            for j in range(2):
                for c in range(2):
                    S.mm(P[:, f * 512 + j:f * 512 + j + 1], sq[:, 3 + c, j * 128:(j + 1) * 128], ones_b[:, 0:1],
                         c == 0, c == 1, [b_onesb, b_sq[3 + c]], [pbk[f]])
            S.act(rkvt[:, :], P[:, f * 512:f * 512 + 2], AF.Sqrt, [pbk[f]], [b_rkvt], bias=RMS_EPS, scale=1.0 / 256)
            S.recip(rkvt[:, :], rkvt[:, :], [b_rkvt], [b_rkvt])
            if latent:
                S.tt("pool", csr[0:96, :], tab[s][0:96, 2, :], rq[0:96, :], ALU.mult, [b_tab[s], b_rq], [b_csr])
                S.tt("pool", snr[0:96, :], tab[s][0:96, 3, :], rq[0:96, :], ALU.mult, [b_tab[s], b_rq], [b_snr])
                for h in range(8):
                    f = get_bank()
                    for idx in range(2):
                        for c in range(3):
                            S.mm(Pb(f, idx, 96), WUQ[:, c, idx * 768 + h * 96:idx * 768 + (h + 1) * 96], cq[:, c, :],
                                 c == 0, c == 2, [b_WUQ, b_cq[c]], [pbk[f]])
                    t1 = get_tmp()
                    S.tt("dve", tmp[t1][0:96, :], Pb(f, 0, 96), csr[0:96, :], ALU.mult, [pbk[f], b_csr], [b_tmp[t1]])
                    t2 = get_tmp()
                    S.tt("dve", tmp[t2][0:96, :], Pb(f, 1, 96), snr[0:96, :], ALU.mult, [pbk[f], b_snr], [b_tmp[t2]])
                    S.tt("pool", QAs[s][0:96, h, :], tmp[t1][0:96, :], tmp[t2][0:96, :], ALU.add,
                         [b_tmp[t1], b_tmp[t2]], [b_QAs[s][h]])
            for hp in range(4):
                f = get_bank()
                for idx in range(2):
                    h = hp * 2 + idx
                    for c in range(2):
                        S.mm(Pb(f, idx, 64), WUKV[:, c, h * 64:(h + 1) * 64], cq[:, 3 + c, :], c == 0, c == 1,
                             [b_WUKV, b_cq[3 + c]], [pbk[f]])
                for idx in range(2):
                    h = hp * 2 + idx
                    S.tt("dve", KAs[s][0:64, h, :], Pb(f, idx, 64), rkv[0:64, :], ALU.mult, [pbk[f], b_rkv],
                         [b_KAs[s][h]])
            for j in range(2):
                f = get_bank()
                for c in range(2):
                    S.mm(P[:, f * 512:(f + 1) * 512], cq[:, 3 + c, j * 128:(j + 1) * 128], WUKV[:, c, 512:1024],
                         c == 0, c == 1, [b_cq[3 + c], b_WUKV], [pbk[f]])
                S.act(VAs[s][:, :, j, 0:64], P[:, f * 512:(f + 1) * 512].rearrange("p (h d) -> p h d", h=8), AF.Copy,
                      [pbk[f], b_rkvt], [b_VAs[s][j]], scale=rkvt[:, j:j + 1])

        def stores(ci):
            s = ci % 2
            k0 = ci * W
            if ci > 0:
                t0 = k0 - CTX
                S.dma(qdT[:, :, t0:t0 + W].rearrange("h p w -> p h w"), QDs[s][:], b_QDs[s], (), ch_st["qd"][s])
                S.dma(qaT[:, :, t0:t0 + W].rearrange("h p w -> p h w"), QAs[s][0:96, :, :], b_QAs[s], (), ch_st["qa"][s])
            S.dma(kdT[:, :, k0:k0 + W].rearrange("h p w -> p h w"), KDs[s][:], b_KDs[s], (), ch_st["kd"][s])
            S.dma(kaT[:, :, k0:k0 + W].rearrange("h p w -> p h w"), KAs[s][0:96, :, :], b_KAs[s] + b_KAr[s], (),
                  ch_st["ka"][s])
            kb0 = k0 // 128
            S.dma(vdS[:, :, kb0 * 128:(kb0 + 2) * 128].rearrange("h p w -> p h w"),
                  VDs[s][:].rearrange("p h j d -> p h (j d)"), b_VDs[s], (), ch_st["vd"][s])
            S.dma(vaS[:, :, kb0 * 65:(kb0 + 2) * 65].rearrange("h p w -> p h w"),
                  VAs[s][:].rearrange("p h j d -> p h (j d)"), b_VAs[s] + [b_VAones[s]], (), ch_st["va"][s])

        load(0)
        load_tab(0)
        if NCH > 1:
            load(1)
            load_tab(1)
        transposes(0)
        for ci in range(NCH):
            if ci + 2 < NCH:
                load(ci + 2)
            stage1(ci)
            if ci + 1 < NCH:
                transposes(ci + 1)
            stage2(ci)
            stores(ci)
            if ci + 2 < NCH:
                load_tab(ci + 2)
        S.flush()


def phase_B(nc, S, sb, P, env):
    ones_f = env["ones_f"]
    qaT, kaT, vaS, qdT, kdT, vdS = env["qaT"], env["kaT"], env["vaS"], env["qdT"], env["kdT"], env["vdS"]
    oaT, saS, od1T, od2T, sdS = env["oaT"], env["saS"], env["od1T"], env["od2T"], env["sdS"]
    NQC = T // 512

    with ExitStack() as ph:
        G = 3
        NG = NKB // G
        KA = [sb(ph, f"KA{i}", [128, NK], BF16) for i in range(2)]
        VA = [sb(ph, f"VA{i}", [128, NKB, 65], BF16) for i in range(2)]
        QC = [sb(ph, f"QC{i}", [128, 512], BF16) for i in range(3)]
        PT = [sb(ph, f"PT{i}", [128, G * 512], BF16) for i in range(3)]
        OS = [sb(ph, f"OS{i}", [128, 512], F32) for i in range(2)]
        wstg = [sb(ph, f"bwstg{i}", [128, 1024], F32) for i in range(2)]
        b_wstg = S.bufs_n(2, "bwstg")
        ch_wstg = [S.chan() for _ in range(2)]
        b_cw = S.bufs_n(1, "cw")

        b_gate_c = S.buf("gate_c")

        def _cw_gen():
            for (dst, src, nk, ncol) in ((env["WG"], env["wG"], 8, 3072), (env["WOA"], env["w_oa"], 4, D),
                                          (env["WOB"], env["w_ob"], 4, D)):
                yield from load_weight_bf16_gen(S, dst, b_cw, src, nk, ncol, wstg, b_wstg, ch_wstg,
                                                engines=("dve",), piece=1024)
            yield from load_weight_bf16_gen(S, env["WOUT"], b_cw, env["w_out"], 8, D, wstg, b_wstg, ch_wstg,
                                            engines=("dve",), piece=1024, colscale=env["gate_bc"],
                                            colscale_buf=b_gate_c)
            env["cw_loaded"][0] = True
        cw_gen = _cw_gen()
        b_KA = S.bufs_n(2, "KA")
        b_VA = S.bufs_n(2, "VA")
        b_QC = S.bufs_n(3, "QC")
        b_PT = S.bufs_n(3, "PT")
        b_OS = S.bufs_n(2, "OS")
        b_pS = [S.pbanks(G, f"pS{i}_") for i in range(2)]
        b_pO = S.pbanks(2, "pO")
        ch_KA = [S.chan() for _ in range(2)]
        ch_VA = [S.chan() for _ in range(2)]
        ch_QC = [S.chan() for _ in range(3)]
        ch_O = [S.chan() for _ in range(2)]
        ch_Os = [S.chan() for _ in range(2)]
        POs = [P[0:65, 6 * 512:7 * 512], P[0:65, 7 * 512:8 * 512]]

        def load_kv(h):
            s = h % 2
            S.dma(KA[s][0:96, :], kaT[h, :, :], (), [b_KA[s]], ch_KA[s])
            S.dma(VA[s][:].rearrange("p k d -> p (k d)"), vaS[h, :, :], (), [b_VA[s]], ch_VA[s])

        chunks = [(h, qc) for h in range(DBG.get('nhB1', 8)) for qc in range(DBG.get('nqcB', NQC))]

        def load_q(m):
            h, qc = chunks[m]
            s = m % 3
            S.dma(QC[s][0:96, :], qaT[h, :, qc * 512:(qc + 1) * 512], (), [b_QC[s]], ch_QC[s])

        steps = [(m, g) for m in range(len(chunks)) for g in range(NG)]

        def rec_S(n):
            m, g = steps[n]
            h, qc = chunks[m]
            if g == 1 and qc == 0 and h + 1 < DBG.get('nhB1', 8):
                load_kv(h + 1)
            if g == 0:
                if m + 2 < len(chunks):
                    load_q(m + 2)
            if g == 2:
                next(cw_gen, None)
            sbk = n % 2
            for i in range(G):
                kb = g * G + i
                col = (sbk * G + i) * 512
                S.mm(P[:, col:col + 512], KA[h % 2][0:96, kb * 128:(kb + 1) * 128], QC[m % 3][0:96, :], True, True,
                     [b_KA[h % 2], b_QC[m % 3]], [b_pS[sbk][i]])

        def rec_rest(n):
            m, g = steps[n]
            h, qc = chunks[m]
            sbk = n % 2
            ps = n % 3
            col = sbk * G * 512
            S.act(PT[ps][:, :], P[:, col:col + G * 512], AF.Exp, b_pS[sbk], [b_PT[ps]], scale=MLA_SCALE)
            for i in range(G):
                kb = g * G + i
                S.mm(POs[m % 2], VA[h % 2][:, kb, :], PT[ps][:, i * 512:(i + 1) * 512], g == 0 and i == 0,
                     g == NG - 1 and i == G - 1, [b_VA[h % 2], b_PT[ps]], [b_pO[m % 2]])
            if g == NG - 1:
                o = m % 2
                S.cp("dve", OS[o][0:65, :], POs[o], [b_pO[o]], [b_OS[o]])
                S.recip(OS[o][64:65, :], OS[o][64:65, :], [b_OS[o]], [b_OS[o]])
                S.dma(oaT[h * 64:(h + 1) * 64, qc * 512:(qc + 1) * 512], OS[o][0:64, :], [b_OS[o]], (), ch_O[o])
                S.dma(saS[h:h + 1, qc * 512:(qc + 1) * 512], OS[o][64:65, :], [b_OS[o]], (), ch_Os[o])

        if steps:
            load_kv(0)
            load_q(0)
            if len(chunks) > 1:
                load_q(1)
            rec_S(0)
        for n in range(len(steps)):
            if n + 1 < len(steps):
                rec_S(n + 1)
            rec_rest(n)
        for _ in cw_gen:
            pass
        S.flush()

    with ExitStack() as ph:
        KD = [sb(ph, f"KD{i}", [128, NK], BF16) for i in range(2)]
        VD = [sb(ph, f"VD{i}", [128, NKB, 128], BF16) for i in range(2)]
        QC = [sb(ph, f"QD{i}", [128, 512], BF16) for i in range(3)]
        PT = [sb(ph, f"PD{i}", [128, 1024], BF16) for i in range(3)]
        ACC = [[sb(ph, f"ACC{i}{j}", [128, 1024], F32) for j in range(2)] for i in range(2)]
        OS = [[sb(ph, f"OD{i}{j}", [128, 512], F32) for j in range(2)] for i in range(2)]
        SS = [sb(ph, f"SS{i}", [1, 1024], F32) for i in range(2)]
        b_KD = S.bufs_n(2, "KD")
        b_VD = S.bufs_n(2, "VD")
        b_QC = S.bufs_n(3, "QD")
        b_PT = S.bufs_n(3, "PD")
        b_ACC = [S.bufs_n(2, f"ACC{i}_") for i in range(2)]
        b_OS = [S.bufs_n(2, f"OD{i}_") for i in range(2)]
        b_SS = S.bufs_n(2, "SS")
        b_pS = [S.pbanks(2, f"pSd{i}_") for i in range(2)]
        b_pO = S.pbanks(2, "pOd")
        b_pSum = S.pbanks(2, "pSum")
        b_onesf = S.buf("onesf")
        ch_KD = [S.chan() for _ in range(2)]
        ch_VD = [S.chan() for _ in range(2)]
        ch_QC = [S.chan() for _ in range(3)]
        ch_O = [[S.chan() for _ in range(2)] for _ in range(2)]
        ch_Ss = [[S.chan() for _ in range(2)] for _ in range(2)]
        PO = [P[:, 4 * 512:5 * 512], P[:, 5 * 512:6 * 512]]
        PSUMS = [P[0:1, 6 * 512:7 * 512], P[0:1, 7 * 512:8 * 512]]

        def load_kv(h):
            s = h % 2
            S.dma(KD[s][:, :], kdT[h, :, :], (), [b_KD[s]], ch_KD[s])
            S.dma(VD[s][:].rearrange("p k d -> p (k d)"), vdS[h, :, :], (), [b_VD[s]], ch_VD[s])

        chunks = [(h, qc) for h in range(DBG.get('nhB2', 4)) for qc in range(DBG.get('nqcB', NQC))]

        def load_q(m):
            h, qc = chunks[m]
            s = m % 3
            S.dma(QC[s][:, :], qdT[h, :, qc * 512:(qc + 1) * 512], (), [b_QC[s]], ch_QC[s])

        steps = [(m, kb) for m in range(len(chunks)) for kb in range(NKB)]

        def rec_S(n):
            m, kb = steps[n]
            h, qc = chunks[m]
            if kb == 1 and qc == 0 and h + 1 < DBG.get('nhB2', 4):
                load_kv(h + 1)
            if kb == 0:
                if m + 2 < len(chunks):
                    load_q(m + 2)
            sbk = n % 2
            for i in range(2):
                col = (sbk * 2 + i) * 512
                S.mm(P[:, col:col + 512], KD[h % 2][i * 64:(i + 1) * 64, kb * 128:(kb + 1) * 128],
                     QC[m % 3][i * 64:(i + 1) * 64, :], True, True, [b_KD[h % 2], b_QC[m % 3]], [b_pS[sbk][i]])

        def rec_rest(n):
            m, kb = steps[n]
            h, qc = chunks[m]
            sbk = n % 2
            ps = n % 3
            col = sbk * 1024
            a = m % 2
            par = kb % 2
            S.act(PT[ps][:, :], P[:, col:col + 1024], AF.Exp, b_pS[sbk], [b_PT[ps]], scale=DIFF_SCALE)
            for i in range(2):
                S.mm(PO[i], VD[h % 2][:, kb, :], PT[ps][:, i * 512:(i + 1) * 512], kb == 0, kb == NKB - 1,
                     [b_VD[h % 2], b_PT[ps]], [b_pO[i]])
            eng, acc_i, first = "dve", kb % 2, kb < 2
            if first:
                S.cp(eng, ACC[a][acc_i][:, :], PT[ps][:, :], [b_PT[ps]], [b_ACC[a][acc_i]])
            else:
                S.tt(eng, ACC[a][acc_i][:, :], ACC[a][acc_i][:, :], PT[ps][:, :], ALU.add,
                     [b_PT[ps], b_ACC[a][acc_i]], [b_ACC[a][acc_i]])
            if kb == NKB - 1:
                for i in range(2):
                    S.cp("dve", OS[a][i][:, :], PO[i], [b_pO[i]], [b_OS[a][i]])
                    dst = (od1T, od2T)[i]
                    S.dma(dst[h * 128:(h + 1) * 128, qc * 512:(qc + 1) * 512], OS[a][i][:, :], [b_OS[a][i]], (),
                          ch_O[a][i])
                for i in range(2):
                    for p2 in range(2):
                        S.mm(PSUMS[i], ones_f[:, 0:1], ACC[a][p2][:, i * 512:(i + 1) * 512], p2 == 0, p2 == 1,
                             [b_onesf, b_ACC[a][p2]], [b_pSum[i]])
                    S.cp("dve", SS[a][0:1, i * 512:(i + 1) * 512], PSUMS[i], [b_pSum[i]], [b_SS[a]])
                for i in range(2):
                    S.dma(sdS[2 * h + i:2 * h + i + 1, qc * 512:(qc + 1) * 512], SS[a][0:1, i * 512:(i + 1) * 512],
                          [b_SS[a]], (), ch_Ss[a][i])

        if steps:
            load_kv(0)
            load_q(0)
            if len(chunks) > 1:
                load_q(1)
            rec_S(0)
        for n in range(len(steps)):
            if n + 1 < len(steps):
                rec_S(n + 1)
            rec_rest(n)
        S.flush()


def phase_C(nc, S, sb, P, env):
    ident, ones_b, modT = env["ident"], env["ones_b"], env["modT"]
    gate_bc, lng_bc, lnb_bc, neglam, sublnS = env["gate_bc"], env["lng_bc"], env["lnb_bc"], env["neglam"], env["sublnS"]
    xk, wG, w_oa, w_ob, w_out, out = env["xk"], env["wG"], env["w_oa"], env["w_ob"], env["w_out"], env["out"]
    oaT, saS, od1T, od2T, sdS = env["oaT"], env["saS"], env["od1T"], env["od2T"], env["sdS"]
    NCH = DBG.get('nchC', T // W)
    NX = 3
    with ExitStack() as ph:
        WG, WOA, WOB, WOUT = env["WG"], env["WOA"], env["WOB"], env["WOUT"]
        xin = [sb(ph, f"cxin{i}", [128, 2, D], F32) for i in range(NX)]
        hT = [sb(ph, f"chT{i}", [128, 8, W], BF16) for i in range(2)]
        oa = sb(ph, "oa", [128, 4, W], F32)
        rab = sb(ph, "rab", [128, 4, W], F32)
        od1 = sb(ph, "od1", [128, 4, W], F32)
        od2 = sb(ph, "od2", [128, 4, W], F32)
        r1b = sb(ph, "r1b", [128, 4, W], F32)
        r2b = sb(ph, "r2b", [128, 4, W], F32)
        sil = sb(ph, "sil", [128, 8, W], F32)
        NTMP = 6
        tmp = [sb(ph, f"ctmp{i}", [128, W], F32) for i in range(NTMP)]
        dsq = sb(ph, "dsq", [128, 4, W], BF16)
        rn = sb(ph, "rn", [128, 4, W], F32)
        oga = [sb(ph, f"oga{i}", [128, 4, W], BF16) for i in range(2)]
        ogd = [sb(ph, f"ogd{i}", [128, 4, W], BF16) for i in range(2)]
        zT = sb(ph, "zT", [128, 8, W], BF16)
        tv = [sb(ph, f"tv{i}", [128, D], F32) for i in range(2)]
        ost = [sb(ph, f"ost{i}", [128, D], F32) for i in range(2)]
        stat = sb(ph, "stat", [128, 2, 6], F32)
        mv = sb(ph, "mv", [128, 4], F32)
        stg = tv
        eps_t = sb(ph, "eps_t", [128, 1], F32)
        b_eps = S.buf("eps")
        S.memset("pool", eps_t[:], SUBLN_EPS, [b_eps])
        rsb = ost

        b_ident, b_onesb, b_modT, b_gate, b_lng, b_lnb, b_neglam, b_subln = S.bufs_n(8, "cC")
        b_WG, b_WOA, b_WOB, b_WOUT = (S.bufs_n(2, n) for n in ("wG", "wOA", "wOB", "wOUT"))
        b_xin = S.bufs_n(NX, "cxin")
        b_hT = [S.bufs_n(8, f"chT{i}_") for i in range(2)]
        b_oa, b_rab, b_od1, b_od2, b_r1b, b_r2b = S.bufs_n(6, "o")
        b_sil = S.bufs_n(8, "sil")
        b_tmp = S.bufs_n(NTMP, "ctmp")
        b_dsq = S.buf("dsq")
        b_rn = S.bufs_n(4, "rn")
        b_oga = S.bufs_n(2, "oga")
        b_ogd = S.bufs_n(2, "ogd")
        b_zT = S.bufs_n(8, "zT")
        b_tv = S.bufs_n(2, "tv")
        b_stg = b_tv
        b_ost = S.bufs_n(2, "ost")
        b_stat, b_mv = S.bufs_n(2, "st")
        pbk = S.pbanks(8, "cpbk")
        ch_stg = [S.chan() for _ in range(2)]
        ch_x = [S.chan() for _ in range(NX)]
        ch_in = {n: S.chan() for n in ("oa", "rab0", "rab1", "od1", "od2", "r1b", "r2b")}
        ch_out = [S.chan() for _ in range(2)]

        if not env["cw_loaded"][0]:
            load_weight_bf16(S, WG, b_WG, wG, 8, 3072, stg, b_stg, ch_stg, piece=1024)
            load_weight_bf16(S, WOA, b_WOA, w_oa, 4, D, stg, b_stg, ch_stg, piece=1024)
            load_weight_bf16(S, WOB, b_WOB, w_ob, 4, D, stg, b_stg, ch_stg, piece=1024)
            load_weight_bf16(S, WOUT, b_WOUT, w_out, 8, D, stg, b_stg, ch_stg, piece=1024, engines=("dve",),
                             colscale=gate_bc, colscale_buf=b_gate)

        bank_rr = [0]
        tmp_rr = [0]

        def get_bank():
            f = 4 + bank_rr[0] % 4
            bank_rr[0] += 1
            return f

        def get_tmp():
            i = tmp_rr[0] % NTMP
            tmp_rr[0] += 1
            return i

        def Pb(f, half, m=128):
            return P[0:m, f * 512 + half * 256:f * 512 + (half + 1) * 256]

        def bc_src(t, row0, rstride, nrow, t0, nparts):
            ncol = t.shape[1]
            return bass.AP(t.tensor, row0 * ncol + t0, [[0, nparts], [rstride * ncol, nrow], [1, W]])

        def load_x(ci):
            s = ci % NX
            t0 = ci * W
            S.dma(xin[s][:], xk[CTX + t0:CTX + t0 + W, :].rearrange("(j p) d -> p j d", p=128), (), [b_xin[s]], ch_x[s])

        def load_o(ci):
            t0 = ci * W
            S.dma(oa[:], oaT[:, t0:t0 + W].rearrange("(c p) w -> p c w", p=128), (), [b_oa], ch_in["oa"])
            S.dma(od1[:], od1T[:, t0:t0 + W].rearrange("(c p) w -> p c w", p=128), (), [b_od1], ch_in["od1"])
            S.dma(od2[:], od2T[:, t0:t0 + W].rearrange("(c p) w -> p c w", p=128), (), [b_od2], ch_in["od2"])
            S.dma(rab[0:64, :, :], bc_src(saS, 0, 2, 4, t0, 64), (), [b_rab], ch_in["rab0"])
            S.dma(rab[64:128, :, :], bc_src(saS, 1, 2, 4, t0, 64), (), [b_rab], ch_in["rab1"])
            S.dma(r1b[:], bc_src(sdS, 0, 2, 4, t0, 128), [b_rd], [b_r1b], ch_in["r1b"])
            S.dma(r2b[:], bc_src(sdS, 1, 2, 4, t0, 128), [b_rd], [b_r2b], ch_in["r2b"])

        def flat(t):
            return t[:].rearrange("p c w -> p (c w)")

        def stage_X(ci, seg):
            sx = ci % NX
            s = ci % 2
            S.tag = f"X{seg}c{ci}"
            if seg == -1:
                stage_X0(ci, sx, s)
            elif seg == 0:
                stage_X1(ci, sx, s)
            elif seg == 1:
                stage_X2(ci, sx, s)
            else:
                stage_X3(ci, sx, s)

        def stage_X0(ci, sx, s):
            for dk in range(8):
                for j in range(2):
                    S.tr(P[:, dk * 256 + j * 128:dk * 256 + (j + 1) * 128], xin[sx][:, j, dk * 128:(dk + 1) * 128],
                         ident[:], [b_xin[sx], b_ident], [pbk[dk // 2]])
            for dk in range(8):
                S.act(hT[s][:, dk, :], Pb(dk // 2, dk % 2), AF.Identity, [pbk[dk // 2], b_modT], [b_hT[s][dk]],
                      bias=modT[:, dk, 0:1], scale=modT[:, 8 + dk, 0:1])
        def stage_X1(ci, sx, s):
            for mp in range(4):
                f = get_bank()
                for idx in range(2):
                    m = mp * 2 + idx
                    for k in range(8):
                        S.mm(Pb(f, idx), WG[:, k, m * 128:(m + 1) * 128], hT[s][:, k, :], k == 0, k == 7,
                             [b_WG, b_hT[s][k]], [pbk[f]])
                for idx in range(2):
                    m = mp * 2 + idx
                    t1 = get_tmp()
                    S.act(tmp[t1][:, :], Pb(f, idx), AF.Sigmoid, [pbk[f]], [b_tmp[t1]])
                    S.tt("dve", sil[:, m, :], Pb(f, idx), tmp[t1][:, :], ALU.mult, [pbk[f], b_tmp[t1]], [b_sil[m]])
        def stage_X2(ci, sx, s):
            S.tt("pool", flat(oa), flat(oa), flat(rab), ALU.mult, [b_oa, b_rab], [b_oa])
            S.tt("dve", flat(oga[s]), flat(oa), sil[:, 0:4, :].rearrange("p c w -> p (c w)"), ALU.mult,
                 [b_oa] + b_sil[0:4], [b_oga[s]])
            S.tt("pool", flat(od1), flat(od1), flat(r1b), ALU.mult, [b_od1, b_r1b], [b_od1])
            S.tt("pool", flat(od2), flat(od2), flat(r2b), ALU.mult, [b_od2, b_r2b], [b_od2])
            S.stt(flat(od1), flat(od2), neglam[:, 0:1], flat(od1), ALU.mult, ALU.add, [b_od2, b_od1, b_neglam], [b_od1])
            S.tt("pool", flat(dsq), flat(od1), flat(od1), ALU.mult, [b_od1], [b_dsq])

        def stage_X3(ci, sx, s):
            for cp_ in range(2):
                f = get_bank()
                for idx in range(2):
                    c = cp_ * 2 + idx
                    S.mm(Pb(f, idx), ones_b[:, :], dsq[:, c, :], True, True, [b_onesb, b_dsq], [pbk[f]])
                for idx in range(2):
                    c = cp_ * 2 + idx
                    S.act(rn[:, c, :], Pb(f, idx), AF.Ln, [pbk[f], b_eps], [b_rn[c]], bias=eps_t[:, 0:1], scale=1.0 / 128)
            S.act(flat(rn), flat(rn), AF.Exp, b_rn, b_rn, scale=-0.5)
            S.tt("pool", flat(od2), flat(od1), flat(rn), ALU.mult, [b_od1] + b_rn, [b_od2])
            S.stt(flat(ogd[s]), flat(od2), sublnS[:, 0:1], sil[:, 4:8, :].rearrange("p c w -> p (c w)"), ALU.mult,
                  ALU.mult, [b_od2, b_subln] + b_sil[4:8], [b_ogd[s]])

        def stage_Y(ci, seg):
            sx = ci % NX
            s = ci % 2
            t0 = ci * W
            S.tag = f"Y{seg}c{ci}"
            if seg < 2:
                stage_Y1(ci, s, range(seg * 4, seg * 4 + 4))
            else:
                stage_Y2(ci, sx, t0, seg - 2)

        def stage_Y1(ci, s, ms):
            for m in ms:
                fa = get_bank()
                for c in range(4):
                    S.mm(Pb(fa, 0), WOA[:, c, m * 128:(m + 1) * 128], oga[s][:, c, :], c == 0, c == 3,
                         [b_WOA, b_oga[s]], [pbk[fa]])
                for c in range(4):
                    S.mm(Pb(fa, 1), WOB[:, c, m * 128:(m + 1) * 128], ogd[s][:, c, :], c == 0, c == 3,
                         [b_WOB, b_ogd[s]], [pbk[fa]])
                fm = get_bank()
                for idx in range(2):
                    for k in range(8):
                        S.mm(Pb(fm, idx), WG[:, k, 1024 * (idx + 1) + m * 128:1024 * (idx + 1) + (m + 1) * 128],
                             hT[s][:, k, :], k == 0, k == 7, [b_WG, b_hT[s][k]], [pbk[fm]])
                t1 = get_tmp()
                S.act(tmp[t1][:, :], Pb(fm, 0), AF.Sigmoid, [pbk[fm]], [b_tmp[t1]])
                t2 = get_tmp()
                S.act(tmp[t2][:, :], Pb(fm, 1), AF.Sigmoid, [pbk[fm]], [b_tmp[t2]])
                S.tt("dve", tmp[t1][:, :], Pb(fa, 0), tmp[t1][:, :], ALU.mult, [pbk[fa], b_tmp[t1]], [b_tmp[t1]])
                S.tt("dve", tmp[t2][:, :], Pb(fa, 1), tmp[t2][:, :], ALU.mult, [pbk[fa], b_tmp[t2]], [b_tmp[t2]])
                S.tt("pool", zT[:, m, :], tmp[t1][:, :], tmp[t2][:, :], ALU.add, [b_tmp[t1], b_tmp[t2]], [b_zT[m]])
        def stage_Y2(ci, sx, t0, j):
            if True:
                o = j
                for half in range(2):
                    f = get_bank()
                    for m in range(8):
                        S.mm(P[:, f * 512:(f + 1) * 512], zT[:, m, j * 128:(j + 1) * 128],
                             WOUT[:, m, half * 512:(half + 1) * 512], m == 0, m == 7, [b_zT[m], b_WOUT],
                             [pbk[f]])
                    S.stt(tv[o][:, half * 512:(half + 1) * 512], xin[sx][:, j, half * 512:(half + 1) * 512], ALPHA,
                          P[:, f * 512:(f + 1) * 512], ALU.mult, ALU.add, [b_xin[sx], pbk[f]], [b_tv[o]])
                S.op("dve", lambda h, o=o: h.bn_stats(stat[:, 0, :], tv[o][:, 0:512]), [b_tv[o]], [b_stat])
                S.op("dve", lambda h, o=o: h.bn_stats(stat[:, 1, :], tv[o][:, 512:1024]), [b_tv[o]], [b_stat])
                S.op("dve", lambda h: h.bn_aggr(mv[:, 0:2], stat[:, :, :]), [b_stat], [b_mv])
                S.act(mv[:, 2:3], mv[:, 1:2], AF.Sqrt, [b_mv], [b_mv], bias=LN_EPS, scale=1.0)
                S.recip(mv[:, 2:3], mv[:, 2:3], [b_mv], [b_mv])
                S.ts("dve", mv[:, 3:4], mv[:, 0:1], -1.0, mv[:, 2:3], ALU.mult, ALU.mult, [b_mv], [b_mv])
                S.act(tv[o][:, :], tv[o][:, :], AF.Identity, [b_tv[o], b_mv], [b_tv[o]], bias=mv[:, 3:4], scale=mv[:, 2:3])
                S.tt("pool", tv[o][:, :], tv[o][:, :], lng_bc[:, :], ALU.mult, [b_tv[o], b_lng], [b_tv[o]])
                S.tt("pool", ost[o][:, :], tv[o][:, :], lnb_bc[:, :], ALU.add, [b_tv[o], b_lnb], [b_ost[o]])
                S.dma(out[t0 + j * 128:t0 + (j + 1) * 128, :], ost[o][:, :], [b_ost[o]], (), ch_out[o])

        b_rd = S.buf("rdram")
        b_rst = b_ost
        ch_rl = [S.chan() for _ in range(2)]
        ch_rs = [S.chan() for _ in range(2)]
        for pc in range(DBG.get('nqcB', T // 512)):
            i = pc % 2
            S.dma(rsb[i][0:8, 0:512], sdS[:, pc * 512:(pc + 1) * 512], (), [b_rst[i]], ch_rl[i])
            S.recip(rsb[i][0:8, 0:512], rsb[i][0:8, 0:512], [b_rst[i]], [b_rst[i]])
            S.dma(sdS[:, pc * 512:(pc + 1) * 512], rsb[i][0:8, 0:512], [b_rst[i]], [b_rd], ch_rs[i])
        load_x(0)
        if NCH > 1:
            load_x(1)
        load_o(0)
        for seg in range(-1, 3):
            stage_X(0, seg)
        for ci in range(NCH):
            if ci + 2 < NCH:
                load_x(ci + 2)
            nxt = ci + 1 < NCH
            if nxt:
                load_o(ci + 1)
                stage_X(ci + 1, -1)
            stage_Y(ci, 0)
            if nxt:
                stage_X(ci + 1, 0)
                stage_X(ci + 1, 1)
            stage_Y(ci, 1)
            if nxt:
                stage_X(ci + 1, 2)
            stage_Y(ci, 2)
            stage_Y(ci, 3)
        S.flush()


def _perm(n):
    q = n // 4
    p = np.concatenate([np.arange(0, q), np.arange(2 * q, 3 * q), np.arange(q, 2 * q), np.arange(3 * q, 4 * q)])
    ps = np.concatenate([p[n // 2:], p[:n // 2]])
    return p, ps


def _rope_table():
    tab = np.zeros((128, 4, NK), np.float32)
    tab[:, 0, :] = 1.0
    tab[:, 2, :] = 1.0
    t = np.arange(T)
    row = (t // 64).astype(np.float32)
    col = (t % 64).astype(np.float32)

    def cs(n):
        q = n // 4
        inv = (ROPE_THETA ** (-np.arange(q, dtype=np.float32) / np.float32(q))).astype(np.float32)
        c = np.zeros((n, T), np.float32)
        s = np.zeros((n, T), np.float32)
        for r in range(n):
            j = r % (n // 2)
            ang = (row * inv[j]) if j < q else (col * inv[j - q])
            ang = ang.astype(np.float32)
            c[r] = np.cos(ang)
            s[r] = -np.sin(ang) if r < n // 2 else np.sin(ang)
        return c, s

    c64, s64 = cs(64)
    c32, s32 = cs(32)
    tab[:, 0, CTX:] = np.concatenate([c64, c64], 0)
    tab[:, 1, CTX:] = np.concatenate([s64, s64], 0)
    tab[64:96, 2, CTX:] = c32
    tab[64:96, 3, CTX:] = s32
    return tab


def _prep_shared(w_mod, b_mod, w_in, mla_q_norm, mla_kv_norm, w_uq, w_ukv, diff_lambda, diff_subln, w_oa, w_ob,
                 w_out, ln_g, ln_b):
    f = lambda a: np.ascontiguousarray(a, dtype=np.float32)
    w_in = w_in[0]
    p32, p32s = _perm(32)
    p64, p64s = _perm(64)
    cols = [w_in[:, 0:640]]
    cols.append(w_in[:, 576:640])
    cols.append(w_in[:, 640:672][:, p32])
    cols.append(w_in[:, 576:640])
    cols.append(w_in[:, 640:672][:, p32s])
    for base in (1184, 1696):
        for pp in (p64, p64s):
            for blk in range(8):
                cols.append(w_in[:, base + blk * 64:base + (blk + 1) * 64][:, pp])
    cols.append(w_in[:, 2208:2720])
    wA = np.concatenate(cols, axis=1)
    assert wA.shape == (D, NA)
    wG = np.concatenate([w_in[:, 672:1184], w_in[:, 2720:3232], w_in[:, 3232:5280]], axis=1)
    uq = w_uq[0].reshape(384, 8, 96)
    uq_n = np.concatenate([uq[:, :, 0:64], uq[:, :, 64:96][:, :, p32]], axis=2).reshape(384, 768)
    uq_s = np.concatenate([uq[:, :, 0:64], uq[:, :, 64:96][:, :, p32s]], axis=2).reshape(384, 768)
    wuq2 = np.concatenate([uq_n, uq_s], axis=1)
    ukv = w_ukv[0].reshape(256, 8, 128)
    wukv = np.concatenate([ukv[:, :, 0:64].reshape(256, 512), ukv[:, :, 64:128].reshape(256, 512)], axis=1)
    return {
        "w_mod": f(w_mod[0]),
        "bmodT": f(b_mod[0].reshape(24, 128).T),
        "bgate": f(b_mod[0][2048:3072].reshape(1, D)),
        "wA": f(wA),
        "wG": f(wG),
        "wuq2": f(wuq2),
        "wukv": f(wukv),
        "qnT": f(mla_q_norm[0].reshape(3, 128).T),
        "kvnT": f(mla_kv_norm[0].reshape(2, 128).T),
        "lamb": f(diff_lambda[0].reshape(1, 256)),
        "sublnT": f(diff_subln[0].reshape(128, 1)),
        "w_oa": f(w_oa[0]),
        "w_ob": f(w_ob[0]),
        "w_out": f(w_out[0]),
        "ln_g": f(ln_g[0].reshape(1, D)),
        "ln_b": f(ln_b[0].reshape(1, D)),
        "rope_tab": _rope_table(),
        "ident": np.eye(128, dtype=np.float32),
    }


def make_in_maps(x, c, ctx, c_ctx, **w):
    x, c, ctx, c_ctx = (np.asarray(a, dtype=np.float32) for a in (x, c, ctx, c_ctx))
    shared = _prep_shared(**{k: np.asarray(v, dtype=np.float32) for k, v in w.items()})
    in_maps = []
    for b in range(x.shape[0]):
        m = dict(shared)
        m["xk"] = np.ascontiguousarray(np.concatenate([ctx[b], x[b]], axis=0))
        cv = np.stack([c[b].reshape(8, 128).T, c_ctx.reshape(8, 128).T], axis=2)
        m["cvec"] = np.ascontiguousarray(cv, dtype=np.float32)
        in_maps.append(m)
    return in_maps


_NC_CACHE = {}


def kernel(x, c, ctx, c_ctx, w_mod, b_mod, w_in, mla_q_norm, mla_kv_norm, w_uq, w_ukv, diff_lambda, diff_subln,
           w_oa, w_ob, w_out, ln_g, ln_b):
    in_maps = make_in_maps(x, c, ctx, c_ctx, w_mod=w_mod, b_mod=b_mod, w_in=w_in, mla_q_norm=mla_q_norm,
                           mla_kv_norm=mla_kv_norm, w_uq=w_uq, w_ukv=w_ukv, diff_lambda=diff_lambda,
                           diff_subln=diff_subln, w_oa=w_oa, w_ob=w_ob, w_out=w_out, ln_g=ln_g, ln_b=ln_b)
    if "nc" not in _NC_CACHE:
        _NC_CACHE["nc"] = build_nc()
    nc = _NC_CACHE["nc"]
    n = len(in_maps)
    res = run_bass_kernel_spmd(nc, in_maps, core_ids=list(range(n)))
    return np.stack([np.asarray(r["out"], dtype=np.float32) for r in res.results], axis=0)
```

```python
import math
from contextlib import ExitStack

import numpy as np
import concourse.bass as bass
import concourse.mybir as mybir
from concourse.bass_utils import run_bass_kernel_spmd

F32 = mybir.dt.float32
BF16 = mybir.dt.bfloat16
AF = mybir.ActivationFunctionType
ALU = mybir.AluOpType
AX = mybir.AxisListType

D = 1024
T = 8192
CTX = 256
NK = T + CTX
NKB = NK // 128
W = 256
NA = 3392
MLA_SCALE = 96 ** -0.5
DIFF_SCALE = 64 ** -0.5
LN_EPS = 1e-5
RMS_EPS = 1e-6
SUBLN_EPS = 1e-5
ALPHA = 2 ** 0.25
LAM_INIT = 0.8 - 0.6 * math.exp(0.0)
ROPE_THETA = 10000.0
DBG = {}


class Buf:
    __slots__ = ("name", "w", "r", "excl")

    def __init__(self, name, excl=False):
        self.name = name
        self.w = None
        self.r = {}
        self.excl = excl


class Chan:
    __slots__ = ("sem", "count", "idx")

    def __init__(self, sem, idx):
        self.sem = sem
        self.count = 0
        self.idx = idx


class Op:
    __slots__ = ("eng", "fn", "deps", "sig", "chan", "need", "seq", "tag")


def _flat(bs):
    outl = []
    for b in bs:
        if isinstance(b, (list, tuple)):
            outl.extend(_flat(b))
        else:
            outl.append(b)
    return outl


class Sched:
    CAP = 30000

    def __init__(self, nc, es, nchan=84):
        self.nc = nc
        self.es = es
        self.h = {"pe": nc.tensor, "act": nc.scalar, "dve": nc.vector, "pool": nc.gpsimd, "sp": nc.sync}
        self.ops = []
        self.bufs = []
        self.nsig = {e: 0 for e in self.h}
        self.sems = {e: [] for e in self.h}
        self.waited = {e: {} for e in self.h}
        self.chans = [Chan(es.enter_context(nc.semaphore(f"dch{i}")), i) for i in range(nchan)]
        self.chan_next = 0
        self.seq = 0
        self.last = {}
        self.n_inst = 0
        self.tag = None

    def buf(self, name="b", excl=False):
        b = Buf(name, excl)
        self.bufs.append(b)
        return b

    def bufs_n(self, n, name="b", excl=False):
        return [self.buf(f"{name}{i}", excl) for i in range(n)]

    def pbanks(self, n, name="pb"):
        return self.bufs_n(n, name, excl=True)

    def chan(self):
        c = self.chans[self.chan_next]
        self.chan_next += 1
        return c

    @staticmethod
    def _key(o):
        return ("c", o.chan.idx) if o.chan is not None else ("e", o.eng)

    def op(self, eng, fn, reads=(), writes=(), chan=None):
        o = Op()
        o.eng = eng
        o.fn = fn
        o.chan = chan
        o.sig = None
        o.need = False
        o.tag = self.tag
        self.seq += 1
        o.seq = self.seq
        deps = {}
        reads = _flat(reads)
        writes = _flat(writes)

        def add(d):
            if d is None:
                return
            k = self._key(d)
            if k not in deps or deps[k].seq < d.seq:
                deps[k] = d

        k_self = self._key(o)
        for b in reads:
            if b.excl:
                for kk, r in b.r.items():
                    if kk != k_self:
                        add(r)
            else:
                add(b.w)
        for b in writes:
            if b.excl:
                for kk, r in b.r.items():
                    if kk != k_self:
                        add(r)
            else:
                add(b.w)
                for r in b.r.values():
                    add(r)
        for b in writes:
            if b.excl:
                b.r[k_self] = o
            else:
                b.w = o
                b.r = {}
        for b in reads:
            b.r[k_self] = o
        o.deps = []
        for k, d in deps.items():
            if k == ("e", "pe") and eng == "pe" and chan is None:
                continue
            d.need = True
            o.deps.append(d)
        self.ops.append(o)
        self.last[k_self] = o
        return o

    def flush(self):
        lasts = list(self.last.values())
        for e in self.h:
            o = Op()
            o.eng = e
            o.fn = None
            o.chan = None
            o.sig = None
            o.need = False
            o.tag = None
            self.seq += 1
            o.seq = self.seq
            o.deps = [d for d in lasts if not (d.eng == "pe" and e == "pe" and d.chan is None)]
            for d in o.deps:
                d.need = True
            self.ops.append(o)
        for o in self.ops:
            if o.chan is not None:
                o.chan.count += 1
                o.sig = (("c", o.chan.idx), o.chan.sem, 16 * o.chan.count)
            elif o.need:
                k = self.nsig[o.eng]
                si = k // self.CAP
                while len(self.sems[o.eng]) <= si:
                    self.sems[o.eng].append(
                        self.es.enter_context(self.nc.semaphore(f"s_{o.eng}{len(self.sems[o.eng])}")))
                o.sig = (("e", o.eng, si), self.sems[o.eng][si], k % self.CAP + 1)
                self.nsig[o.eng] += 1
        for o in self.ops:
            h = self.h[o.eng]
            wt = self.waited[o.eng]
            for d in o.deps:
                key, sem, val = d.sig
                if wt.get(key, 0) >= val:
                    continue
                h.wait_ge(sem, val)
                wt[key] = val
                self.n_inst += 1
            if o.fn is not None:
                ins = o.fn(h)
                if o.tag is not None and DBG.get("annotate"):
                    ins.annotate(o.tag)
                self.n_inst += 1
                if o.sig is not None:
                    ins.then_inc(o.sig[1], 16 if o.chan is not None else 1)
        self.ops = []
        self.last = {}
        for b in self.bufs:
            b.w = None
            b.r = {}
        self.bufs = []
        self.chan_next = 0

    def mm(self, out, lhsT, rhs, start, stop, reads, writes):
        return self.op("pe", lambda h: h.matmul(out, lhsT, rhs, start=start, stop=stop), reads, writes)

    def tr(self, out, in_, ident, reads, writes):
        return self.op("pe", lambda h: h.transpose(out, in_, ident), reads, writes)

    def act(self, out, in_, func, reads, writes, bias=None, scale=None, accum_out=None):
        kw = {}
        if bias is not None:
            kw["bias"] = bias
        if scale is not None:
            kw["scale"] = scale
        if accum_out is not None:
            kw["accum_out"] = accum_out
        return self.op("act", lambda h: h.activation(out, in_, func, **kw), reads, writes)

    def tt(self, eng, out, in0, in1, op, reads, writes):
        return self.op(eng, lambda h: h.tensor_tensor(out, in0, in1, op), reads, writes)

    def ts(self, eng, out, in0, s1, s2, op0, op1, reads, writes):
        if s2 is None:
            return self.op(eng, lambda h: h.tensor_scalar(out, in0, s1, None, op0), reads, writes)
        return self.op(eng, lambda h: h.tensor_scalar(out, in0, s1, s2, op0, op1), reads, writes)

    def stt(self, out, in0, scalar, in1, op0, op1, reads, writes):
        return self.op("dve", lambda h: h.scalar_tensor_tensor(out, in0, scalar, in1, op0, op1), reads, writes)

    def cp(self, eng, out, in_, reads, writes):
        if eng == "act":
            return self.op("act", lambda h: h.activation(out, in_, AF.Copy), reads, writes)
        return self.op(eng, lambda h: h.tensor_copy(out, in_), reads, writes)

    def recip(self, out, in_, reads, writes):
        return self.op("dve", lambda h: h.reciprocal(out, in_), reads, writes)

    def memset(self, eng, ap, val, writes):
        return self.op(eng, lambda h: h.memset(ap, val), (), writes)

    def dma(self, out, in_, reads, writes, chan):
        return self.op("sp", lambda h: h.dma_start(out=out, in_=in_), reads, writes, chan=chan)


def build_nc(debug=False, phases="0ABC"):
    nc = bass.Bass("TRN2", target_bir_lowering=False)

    def din(name, shape):
        return nc.dram_tensor(name, shape, F32, kind="ExternalInput").ap()

    xk = din("xk", [NK, D])
    cvec = din("cvec", [128, 8, 2])
    w_mod = din("w_mod", [D, 3 * D])
    bmodT = din("bmodT", [128, 24])
    bgate = din("bgate", [1, D])
    wA = din("wA", [D, NA])
    wG = din("wG", [D, 3072])
    wuq2 = din("wuq2", [384, 1536])
    wukv = din("wukv", [256, 1024])
    qnT = din("qnT", [128, 3])
    kvnT = din("kvnT", [128, 2])
    lamb = din("lamb", [1, 256])
    sublnT = din("sublnT", [128, 1])
    w_oa = din("w_oa", [512, D])
    w_ob = din("w_ob", [512, D])
    w_out = din("w_out", [D, D])
    ln_g = din("ln_g", [1, D])
    ln_b = din("ln_b", [1, D])
    rope_tab = din("rope_tab", [128, 4, NK])
    ident_d = din("ident", [128, 128])
    out = nc.dram_tensor("out", [T, D], F32, kind="ExternalOutput").ap()

    skind = "ExternalOutput" if debug else "Internal"

    def scr(name, shape, dt):
        return nc.dram_tensor(name, shape, dt, kind=skind).ap()

    qaT = scr("qaT", [8, 96, T], BF16)
    kaT = scr("kaT", [8, 96, NK], BF16)
    vaS = scr("vaS", [8, 128, NKB * 65], BF16)
    qdT = scr("qdT", [4, 128, T], BF16)
    kdT = scr("kdT", [4, 128, NK], BF16)
    vdS = scr("vdS", [4, 128, NKB * 128], BF16)
    oaT = scr("oaT", [512, T], F32)
    saS = scr("saS", [8, T], F32)
    od1T = scr("od1T", [512, T], F32)
    od2T = scr("od2T", [512, T], F32)
    sdS = scr("sdS", [8, T], F32)
    if debug:
        dbg_mod = nc.dram_tensor("dbg_mod", [128, 16, 2], F32, kind="ExternalOutput").ap()
        dbg_gate = nc.dram_tensor("dbg_gate", [128, D], F32, kind="ExternalOutput").ap()
        dbg_misc = nc.dram_tensor("dbg_misc", [2, 128, 1], F32, kind="ExternalOutput").ap()

    with ExitStack() as es:
        S = Sched(nc, es)

        def sb(stack, name, shape, dt):
            return stack.enter_context(nc.sbuf_tensor("sb_" + name, shape, dt))

        P = es.enter_context(nc.psum_tensor("P", [128, 4096], F32))
        ident = sb(es, "ident", [128, 128], F32)
        ones_f = sb(es, "ones_f", [128, 128], F32)
        ones_b = sb(es, "ones_b", [128, 128], BF16)
        modT = sb(es, "modT", [128, 16, 2], F32)
        gate_bc = sb(es, "gate_bc", [128, D], F32)
        lng_bc = sb(es, "lng_bc", [128, D], F32)
        lnb_bc = sb(es, "lnb_bc", [128, D], F32)
        neglam = sb(es, "neglam", [128, 1], F32)
        sublnS = sb(es, "sublnS", [128, 1], F32)
        qn_t = sb(es, "qn_t", [128, 3], F32)
        kvn_t = sb(es, "kvn_t", [128, 2], F32)

        with ExitStack() as ph:
            wm = sb(ph, "wm", [128, 8, 3072], F32)
            cv = sb(ph, "cv", [128, 8, 2], F32)
            sg0 = sb(ph, "sg0", [128, 8, 2], F32)
            sc = sb(ph, "sc", [128, 8, 2], F32)
            screp = sb(ph, "screp", [128, 8, 128], F32)
            bmT = sb(ph, "bmT", [128, 24], F32)
            bg = sb(ph, "bg", [128, D], F32)
            lamt = sb(ph, "lamt", [128, 256], F32)
            lprod = sb(ph, "lprod", [128, 128], F32)
            lsum = sb(ph, "lsum", [128, 4], F32)
            subt = sb(ph, "subt", [128, 1], F32)

            b_wm, b_cv, b_sc, b_screp, b_bmT, b_bg, b_lamt, b_lprod, b_lsum, b_subt, b_sg0 = S.bufs_n(11, "p0")
            b_ident, b_onesf, b_onesb, b_modT, b_gate, b_lng, b_lnb, b_neglam, b_subln, b_qn, b_kvn = S.bufs_n(11, "c0")
            b_ps0, b_ps1, b_ps2 = S.pbanks(3, "ps")

            S.dma(ident[:], ident_d[:, :], (), [b_ident], S.chan())
            S.dma(cv[:], cvec[:, :, :], (), [b_cv], S.chan())
            S.dma(bmT[:], bmodT[:, :], (), [b_bmT], S.chan())
            S.dma(wm[:], w_mod.rearrange("(k p) n -> p k n", p=128), (), [b_wm], S.chan())
            S.dma(bg[:], bgate[0:1, :].broadcast_to([128, D]), (), [b_bg], S.chan())
            S.dma(lng_bc[:], ln_g[0:1, :].broadcast_to([128, D]), (), [b_lng], S.chan())
            S.dma(lnb_bc[:], ln_b[0:1, :].broadcast_to([128, D]), (), [b_lnb], S.chan())
            S.dma(lamt[:], lamb[0:1, :].broadcast_to([128, 256]), (), [b_lamt], S.chan())
            S.dma(subt[:], sublnT[:, :], (), [b_subt], S.chan())
            S.dma(qn_t[:], qnT[:, :], (), [b_qn], S.chan())
            S.dma(kvn_t[:], kvnT[:, :], (), [b_kvn], S.chan())
            S.memset("dve", ones_f[:], 1.0, [b_onesf])
            S.memset("pool", ones_b[:], 1.0, [b_onesb])

            S.act(sg0[:], cv[:], AF.Sigmoid, [b_cv], [b_sg0])
            S.tt("dve", sc[:], cv[:], sg0[:], ALU.mult, [b_cv, b_sg0], [b_sc])
            for k in range(8):
                S.ts("dve", screp[:, k, :], ones_f[:, :], sc[:, k, 0:1], None, ALU.mult, None,
                     [b_onesf, b_sc], [b_screp])
            for nck in range(16):
                for k in range(8):
                    S.mm(P[:, nck * 2:nck * 2 + 2], wm[:, k, nck * 128:(nck + 1) * 128], sc[:, k, :],
                         k == 0, k == 7, [b_wm, b_sc], [b_ps0])
            for wch in range(2):
                S.tt("dve", modT[:, :, wch], P[:, wch:32:2], bmT[:, 0:16], ALU.add, [b_ps0, b_bmT], [b_modT])
            S.ts("dve", modT[:, 8:16, :], modT[:, 8:16, :], 1.0, None, ALU.add, None, [b_modT], [b_modT])
            for half in range(2):
                bps = (b_ps1, b_ps2)[half]
                for k in range(8):
                    S.mm(P[:, 512 * (half + 1):512 * (half + 2)], screp[:, k, :],
                         wm[:, k, 2048 + half * 512:2048 + (half + 1) * 512], k == 0, k == 7,
                         [b_screp, b_wm], [bps])
                S.tt("dve", gate_bc[:, half * 512:(half + 1) * 512], P[:, 512 * (half + 1):512 * (half + 2)],
                     bg[:, half * 512:(half + 1) * 512], ALU.add, [bps, b_bg], [b_gate])
            S.tt("dve", lprod[:, 0:64], lamt[:, 0:64], lamt[:, 64:128], ALU.mult, [b_lamt], [b_lprod])
            S.tt("dve", lprod[:, 64:128], lamt[:, 128:192], lamt[:, 192:256], ALU.mult, [b_lamt], [b_lprod])
            S.op("dve", lambda h: h.reduce_sum(lsum[:, 0:1], lprod[:, 0:64], AX.X), [b_lprod], [b_lsum])
            S.op("dve", lambda h: h.reduce_sum(lsum[:, 1:2], lprod[:, 64:128], AX.X), [b_lprod], [b_lsum])
            S.act(lsum[:, 2:4], lsum[:, 0:2], AF.Exp, [b_lsum], [b_lsum])
            S.tt("dve", neglam[:], lsum[:, 3:4], lsum[:, 2:3], ALU.subtract, [b_lsum], [b_neglam])
            S.ts("dve", neglam[:], neglam[:], -LAM_INIT, None, ALU.add, None, [b_neglam], [b_neglam])
            S.ts("dve", sublnS[:], subt[:], 1.0 - LAM_INIT, None, ALU.mult, None, [b_subt], [b_subln])
            if debug:
                S.dma(dbg_mod[:, :, :], modT[:], [b_modT], (), S.chan())
                S.dma(dbg_gate[:, :], gate_bc[:], [b_gate], (), S.chan())
                S.dma(dbg_misc[0, :, :], neglam[:], [b_neglam], (), S.chan())
                S.dma(dbg_misc[1, :, :], sublnS[:], [b_subln], (), S.chan())
            S.flush()

        cw_stack = ExitStack()
        WG = WOA = WOB = WOUT = None
        if "A" in phases:
            phase_A(nc, S, sb, P, locals())
        es.enter_context(cw_stack)
        WG = sb(cw_stack, "WG", [128, 8, 3072], BF16)
        WOA = sb(cw_stack, "WOA", [128, 4, D], BF16)
        WOB = sb(cw_stack, "WOB", [128, 4, D], BF16)
        WOUT = sb(cw_stack, "WOUT", [128, 8, D], BF16)
        cw_loaded = [False]
        if "B" in phases:
            phase_B(nc, S, sb, P, locals())
        if "C" in phases:
            phase_C(nc, S, sb, P, locals())
    return nc


def load_weight_bf16(*a, **kw):
    for _ in load_weight_bf16_gen(*a, **kw):
        pass


def load_weight_bf16_gen(S, dst, dst_bufs, src, nrow_chunks, ncols, stg, stg_bufs, chans, engines=("dve", "act"),
                         rowscale=None, rowscale_buf=None, piece=2048, cnt=[0], colscale=None, colscale_buf=None):
    for k in range(nrow_chunks):
        c0 = 0
        while c0 < ncols:
            cw = min(piece, ncols - c0)
            i = cnt[0] % len(stg)
            cnt[0] += 1
            S.dma(stg[i][:, 0:cw], src[k * 128:(k + 1) * 128, c0:c0 + cw], (), [stg_bufs[i]], chans[i])
            ei = cnt[0] % len(engines)
            eng = engines[ei]
            dst_buf = dst_bufs[ei]
            if colscale is not None:
                S.tt(eng, dst[:, k, c0:c0 + cw], stg[i][:, 0:cw], colscale[:, c0:c0 + cw], ALU.mult,
                     [stg_bufs[i], colscale_buf], [dst_buf])
            elif rowscale is None:
                S.cp(eng, dst[:, k, c0:c0 + cw], stg[i][:, 0:cw], [stg_bufs[i]], [dst_buf])
            elif eng == "act":
                S.act(dst[:, k, c0:c0 + cw], stg[i][:, 0:cw], AF.Copy, [stg_bufs[i], rowscale_buf], [dst_buf],
                      scale=rowscale[:, k:k + 1])
            else:
                S.ts(eng, dst[:, k, c0:c0 + cw], stg[i][:, 0:cw], rowscale[:, k:k + 1], None, ALU.mult, None,
                     [stg_bufs[i], rowscale_buf], [dst_buf])
            c0 += cw
            yield


def phase_A(nc, S, sb, P, env):
    ident, ones_b, modT, qn_t, kvn_t = env["ident"], env["ones_b"], env["modT"], env["qn_t"], env["kvn_t"]
    xk, wA, wuq2, wukv, rope_tab = env["xk"], env["wA"], env["wuq2"], env["wukv"], env["rope_tab"]
    qaT, kaT, vaS, qdT, kdT, vdS = env["qaT"], env["kaT"], env["vaS"], env["qdT"], env["kdT"], env["vdS"]
    NCH = DBG.get('nchA', NK // W)
    with ExitStack() as ph:
        WA = sb(ph, "WA", [128, 8, NA], BF16)
        WUQ = sb(ph, "WUQ", [128, 3, 1536], BF16)
        WUKV = sb(ph, "WUKV", [128, 2, 1024], BF16)
        stg = [sb(ph, f"wstg{i}", [128, 2048], F32) for i in range(2)]
        xin = [sb(ph, f"xin{i}", [128, 2, D], F32) for i in range(2)]
        tab = [sb(ph, f"tab{i}", [128, 4, W], F32) for i in range(2)]
        hT = [sb(ph, f"hT{i}", [128, 8, W], BF16) for i in range(2)]
        cq = sb(ph, "cq", [128, 5, W], BF16)
        sq = sb(ph, "sq", [128, 5, W], BF16)
        NTMP = 8
        tmp = [sb(ph, f"tmp{i}", [128, W], F32) for i in range(NTMP)]
        rq = sb(ph, "rq", [128, W], F32)
        rkv = sb(ph, "rkv", [128, W], F32)
        csr = sb(ph, "csr", [128, W], F32)
        snr = sb(ph, "snr", [128, W], F32)
        rkvt = sb(ph, "rkvt", [128, 2], F32)
        QDs = [sb(ph, f"QDs{i}", [128, 4, W], BF16) for i in range(2)]
        KDs = [sb(ph, f"KDs{i}", [128, 4, W], BF16) for i in range(2)]
        VDs = [sb(ph, f"VDs{i}", [128, 4, 2, 128], BF16) for i in range(2)]
        QAs = [sb(ph, f"QAs{i}", [128, 8, W], BF16) for i in range(2)]
        KAs = [sb(ph, f"KAs{i}", [128, 8, W], BF16) for i in range(2)]
        VAs = [sb(ph, f"VAs{i}", [128, 8, 2, 65], BF16) for i in range(2)]

        b_ident, b_onesb, b_modT, b_qn, b_kvn = S.bufs_n(5, "cA")
        b_WA, b_WUQ, b_WUKV = (S.bufs_n(2, n) for n in ("wA", "wUQ", "wUKV"))
        b_stg = S.bufs_n(2, "stg")
        b_xin = S.bufs_n(2, "xin")
        b_tab = S.bufs_n(2, "tab")
        b_hT = [S.bufs_n(8, f"hT{i}_") for i in range(2)]
        b_cq = S.bufs_n(5, "cq")
        b_sq = S.bufs_n(5, "sq")
        b_tmp = S.bufs_n(NTMP, "tmp")
        b_rq, b_rkv, b_csr, b_snr, b_rkvt = S.bufs_n(5, "r")
        b_QDs = [S.bufs_n(4, f"QDs{i}_") for i in range(2)]
        b_KDs = [S.bufs_n(4, f"KDs{i}_") for i in range(2)]
        b_VDs = [S.bufs_n(2, f"VDs{i}_") for i in range(2)]
        b_QAs = [S.bufs_n(8, f"QAs{i}_") for i in range(2)]
        b_KAs = [S.bufs_n(8, f"KAs{i}_") for i in range(2)]
        b_KAr = [S.bufs_n(8, f"KAr{i}_") for i in range(2)]
        b_VAs = [S.bufs_n(2, f"VAs{i}_") for i in range(2)]
        b_VAones = S.bufs_n(2, "VAones")
        pbk = S.pbanks(8, "pbk")
        ch_stg = [S.chan() for _ in range(2)]
        ch_x = [S.chan() for _ in range(2)]
        ch_tab = [S.chan() for _ in range(2)]
        ch_st = {n: [S.chan() for _ in range(2)] for n in ("qd", "kd", "vd", "qa", "ka", "va")}

        load_weight_bf16(S, WA, b_WA, wA, 8, NA, stg, b_stg, ch_stg)
        load_weight_bf16(S, WUQ, b_WUQ, wuq2, 3, 1536, stg, b_stg, ch_stg, rowscale=qn_t, rowscale_buf=b_qn)
        load_weight_bf16(S, WUKV, b_WUKV, wukv, 2, 1024, stg, b_stg, ch_stg, rowscale=kvn_t, rowscale_buf=b_kvn)
        for i in range(2):
            S.memset("pool", VAs[i][:, :, :, 64:65], 1.0, [b_VAones[i]])

        bank_rr = [0]

        def get_bank():
            f = 4 + bank_rr[0] % 4
            bank_rr[0] += 1
            return f

        def Pb(f, half, m=128):
            return P[0:m, f * 512 + half * 256:f * 512 + (half + 1) * 256]

        tmp_rr = [0]

        def get_tmp():
            i = tmp_rr[0] % NTMP
            tmp_rr[0] += 1
            return i

        def load(ci):
            s = ci % 2
            S.dma(xin[s][:], xk[ci * W:(ci + 1) * W, :].rearrange("(j p) d -> p j d", p=128), (), [b_xin[s]], ch_x[s])

        def load_tab(ci):
            s = ci % 2
            S.dma(tab[s][:], rope_tab[:, :, ci * W:(ci + 1) * W], (), [b_tab[s]], ch_tab[s])

        def transposes(ci):
            s = ci % 2
            wch = 1 if ci == 0 else 0
            for dk in range(8):
                for j in range(2):
                    S.tr(P[:, dk * 256 + j * 128:dk * 256 + (j + 1) * 128], xin[s][:, j, dk * 128:(dk + 1) * 128],
                         ident[:], [b_xin[s], b_ident], [pbk[dk // 2]])
            for dk in range(8):
                S.act(hT[s][:, dk, :], Pb(dk // 2, dk % 2), AF.Identity, [pbk[dk // 2], b_modT], [b_hT[s][dk]],
                      bias=modT[:, dk, wch:wch + 1], scale=modT[:, 8 + dk, wch:wch + 1])

        def proj_job(s, tiles):
            f = get_bank()
            for idx, (c0, m) in enumerate(tiles):
                for k in range(8):
                    S.mm(Pb(f, idx, m), WA[:, k, c0:c0 + m], hT[s][:, k, :], k == 0, k == 7,
                         [b_hT[s][k], b_WA], [pbk[f]])
            return f

        def rope_from_bank(f, cs, sn, m, tab_bufs):
            t1 = get_tmp()
            S.tt("dve", tmp[t1][0:m, :], Pb(f, 0, m), cs, ALU.mult, [pbk[f]] + tab_bufs, [b_tmp[t1]])
            t2 = get_tmp()
            S.tt("dve", tmp[t2][0:m, :], Pb(f, 1, m), sn, ALU.mult, [pbk[f]] + tab_bufs, [b_tmp[t2]])
            return t1, t2

        def stage1(ci):
            s = ci % 2
            latent = ci > 0
            for grp in ((0, 1), (2, 3), (4,)):
                f = proj_job(s, [(m * 128, 128) for m in grp])
                for idx, m in enumerate(grp):
                    S.cp("dve", cq[:, m, :], Pb(f, idx), [pbk[f]], [b_cq[m]])
                    S.act(sq[:, m, :], Pb(f, idx), AF.Square, [pbk[f]], [b_sq[m]])
            f = proj_job(s, [(640, 96), (736, 96)])
            t1, t2 = rope_from_bank(f, tab[s][0:96, 2, :], tab[s][0:96, 3, :], 96, [b_tab[s]])
            for h in range(8):
                S.tt("pool", KAs[s][64:96, h, :], tmp[t1][64:96, :], tmp[t2][64:96, :], ALU.add,
                     [b_tmp[t1], b_tmp[t2]], [b_KAr[s][h]])
            for (c_n, c_s, dst, dbuf, need) in ((832, 1344, QDs, b_QDs, latent), (1856, 2368, KDs, b_KDs, True)):
                if not need:
                    continue
                for hh in range(4):
                    f = proj_job(s, [(c_n + hh * 128, 128), (c_s + hh * 128, 128)])
                    t1, t2 = rope_from_bank(f, tab[s][:, 0, :], tab[s][:, 1, :], 128, [b_tab[s]])
                    S.tt("pool", dst[s][:, hh, :], tmp[t1][:, :], tmp[t2][:, :], ALU.add,
                         [b_tmp[t1], b_tmp[t2]], [dbuf[s][hh]])
            for j in range(2):
                f = get_bank()
                for k in range(8):
                    S.mm(P[:, f * 512:(f + 1) * 512], hT[s][:, k, j * 128:(j + 1) * 128], WA[:, k, 2880:3392],
                         k == 0, k == 7, [b_hT[s][k], b_WA], [pbk[f]])
                S.cp("act", VDs[s][:, :, j, :], P[:, f * 512:(f + 1) * 512].rearrange("p (h d) -> p h d", h=4),
                     [pbk[f]], [b_VDs[s][j]])

        def stage2(ci):
            s = ci % 2
            latent = ci > 0
            f = get_bank()
            for c in range(3):
                S.mm(Pb(f, 0, 96), ones_b[:, 0:96], sq[:, c, :], c == 0, c == 2, [b_onesb, b_sq[c]], [pbk[f]])
            for c in range(2):
                S.mm(Pb(f, 1, 64), ones_b[:, 0:64], sq[:, 3 + c, :], c == 0, c == 1, [b_onesb, b_sq[3 + c]], [pbk[f]])
            S.act(rq[0:96, :], Pb(f, 0, 96), AF.Sqrt, [pbk[f]], [b_rq], bias=RMS_EPS, scale=1.0 / 384)
            S.act(rkv[0:64, :], Pb(f, 1, 64), AF.Sqrt, [pbk[f]], [b_rkv], bias=RMS_EPS, scale=1.0 / 256)
            S.recip(rq[0:96, :], rq[0:96, :], [b_rq], [b_rq])
            S.recip(rkv[0:64, :], rkv[0:64, :], [b_rkv], [b_rkv])
            f = get_bank()
            for j in range(2):
                for c in range(2):
                    S.mm(P[:, f * 512 + j:f * 512 + j + 1], sq[:, 3 + c, j * 128:(j + 1) * 128], ones_b[:, 0:1],
                         c == 0, c == 1, [b_onesb, b_sq[3 + c]], [pbk[f]])
            S.act(rkvt[:, :], P[:, f * 512:f * 512 + 2], AF.Sqrt, [pbk[f]], [b_rkvt], bias=RMS_EPS, scale=1.0 / 256)
            S.recip(rkvt[:, :], rkvt[:, :], [b_rkvt], [b_rkvt])
            if latent:
                S.tt("pool", csr[0:96, :], tab[s][0:96, 2, :], rq[0:96, :], ALU.mult, [b_tab[s], b_rq], [b_csr])
                S.tt("pool", snr[0:96, :], tab[s][0:96, 3, :], rq[0:96, :], ALU.mult, [b_tab[s], b_rq], [b_snr])
                for h in range(8):
                    f = get_bank()
                    for idx in range(2):
                        for c in range(3):
                            S.mm(Pb(f, idx, 96), WUQ[:, c, idx * 768 + h * 96:idx * 768 + (h + 1) * 96], cq[:, c, :],
                                 c == 0, c == 2, [b_WUQ, b_cq[c]], [pbk[f]])
                    t1 = get_tmp()
                    S.tt("dve", tmp[t1][0:96, :], Pb(f, 0, 96), csr[0:96, :], ALU.mult, [pbk[f], b_csr], [b_tmp[t1]])
                    t2 = get_tmp()
                    S.tt("dve", tmp[t2][0:96, :], Pb(f, 1, 96), snr[0:96, :], ALU.mult, [pbk[f], b_snr], [b_tmp[t2]])
                    S.tt("pool", QAs[s][0:96, h, :], tmp[t1][0:96, :], tmp[t2][0:96, :], ALU.add,
                         [b_tmp[t1], b_tmp[t2]], [b_QAs[s][h]])
            for hp in range(4):
                f = get_bank()
                for idx in range(2):
                    h = hp * 2 + idx
                    for c in range(2):
                        S.mm(Pb(f, idx, 64), WUKV[:, c, h * 64:(h + 1) * 64], cq[:, 3 + c, :], c == 0, c == 1,
                             [b_WUKV, b_cq[3 + c]], [pbk[f]])
                for idx in range(2):
                    h = hp * 2 + idx
                    S.tt("dve", KAs[s][0:64, h, :], Pb(f, idx, 64), rkv[0:64, :], ALU.mult, [pbk[f], b_rkv],
                         [b_KAs[s][h]])
            for j in range(2):
                f = get_bank()
                for c in range(2):
                    S.mm(P[:, f * 512:(f + 1) * 512], cq[:, 3 + c, j * 128:(j + 1) * 128], WUKV[:, c, 512:1024],
                         c == 0, c == 1, [b_cq[3 + c], b_WUKV], [pbk[f]])
                S.act(VAs[s][:, :, j, 0:64], P[:, f * 512:(f + 1) * 512].rearrange("p (h d) -> p h d", h=8), AF.Copy,
                      [pbk[f], b_rkvt], [b_VAs[s][j]], scale=rkvt[:, j:j + 1])

        def stores(ci):
            s = ci % 2
            k0 = ci * W
            if ci > 0:
                t0 = k0 - CTX
                S.dma(qdT[:, :, t0:t0 + W].rearrange("h p w -> p h w"), QDs[s][:], b_QDs[s], (), ch_st["qd"][s])
                S.dma(qaT[:, :, t0:t0 + W].rearrange("h p w -> p h w"), QAs[s][0:96, :, :], b_QAs[s], (), ch_st["qa"][s])
            S.dma(kdT[:, :, k0:k0 + W].rearrange("h p w -> p h w"), KDs[s][:], b_KDs[s], (), ch_st["kd"][s])
            S.dma(kaT[:, :, k0:k0 + W].rearrange("h p w -> p h w"), KAs[s][0:96, :, :], b_KAs[s] + b_KAr[s], (),
                  ch_st["ka"][s])
            kb0 = k0 // 128
            S.dma(vdS[:, :, kb0 * 128:(kb0 + 2) * 128].rearrange("h p w -> p h w"),
                  VDs[s][:].rearrange("p h j d -> p h (j d)"), b_VDs[s], (), ch_st["vd"][s])
            S.dma(vaS[:, :, kb0 * 65:(kb0 + 2) * 65].rearrange("h p w -> p h w"),
                  VAs[s][:].rearrange("p h j d -> p h (j d)"), b_VAs[s] + [b_VAones[s]], (), ch_st["va"][s])

        load(0)
        load_tab(0)
        if NCH > 1:
            load(1)
            load_tab(1)
        transposes(0)
        for ci in range(NCH):
            if ci + 2 < NCH:
                load(ci + 2)
            stage1(ci)
            if ci + 1 < NCH:
                transposes(ci + 1)
            stage2(ci)
            stores(ci)
            if ci + 2 < NCH:
                load_tab(ci + 2)
        S.flush()


def phase_B(nc, S, sb, P, env):
    ones_f = env["ones_f"]
    qaT, kaT, vaS, qdT, kdT, vdS = env["qaT"], env["kaT"], env["vaS"], env["qdT"], env["kdT"], env["vdS"]
    oaT, saS, od1T, od2T, sdS = env["oaT"], env["saS"], env["od1T"], env["od2T"], env["sdS"]
    NQC = T // 512

    with ExitStack() as ph:
        G = 3
        NG = NKB // G
        KA = [sb(ph, f"KA{i}", [128, NK], BF16) for i in range(2)]
        VA = [sb(ph, f"VA{i}", [128, NKB, 65], BF16) for i in range(2)]
        QC = [sb(ph, f"QC{i}", [128, 512], BF16) for i in range(3)]
        PT = [sb(ph, f"PT{i}", [128, G * 512], BF16) for i in range(3)]
        OS = [sb(ph, f"OS{i}", [128, 512], F32) for i in range(2)]
        wstg = [sb(ph, f"bwstg{i}", [128, 1024], F32) for i in range(2)]
        b_wstg = S.bufs_n(2, "bwstg")
        ch_wstg = [S.chan() for _ in range(2)]
        b_cw = S.bufs_n(1, "cw")

        b_gate_c = S.buf("gate_c")

        def _cw_gen():
            for (dst, src, nk, ncol) in ((env["WG"], env["wG"], 8, 3072), (env["WOA"], env["w_oa"], 4, D),
                                          (env["WOB"], env["w_ob"], 4, D)):
                yield from load_weight_bf16_gen(S, dst, b_cw, src, nk, ncol, wstg, b_wstg, ch_wstg,
                                                engines=("dve",), piece=1024)
            yield from load_weight_bf16_gen(S, env["WOUT"], b_cw, env["w_out"], 8, D, wstg, b_wstg, ch_wstg,
                                            engines=("dve",), piece=1024, colscale=env["gate_bc"],
                                            colscale_buf=b_gate_c)
            env["cw_loaded"][0] = True
        cw_gen = _cw_gen()
        b_KA = S.bufs_n(2, "KA")
        b_VA = S.bufs_n(2, "VA")
        b_QC = S.bufs_n(3, "QC")
        b_PT = S.bufs_n(3, "PT")
        b_OS = S.bufs_n(2, "OS")
        b_pS = [S.pbanks(G, f"pS{i}_") for i in range(2)]
        b_pO = S.pbanks(2, "pO")
        ch_KA = [S.chan() for _ in range(2)]
        ch_VA = [S.chan() for _ in range(2)]
        ch_QC = [S.chan() for _ in range(3)]
        ch_O = [S.chan() for _ in range(2)]
        ch_Os = [S.chan() for _ in range(2)]
        POs = [P[0:65, 6 * 512:7 * 512], P[0:65, 7 * 512:8 * 512]]

        def load_kv(h):
            s = h % 2
            S.dma(KA[s][0:96, :], kaT[h, :, :], (), [b_KA[s]], ch_KA[s])
            S.dma(VA[s][:].rearrange("p k d -> p (k d)"), vaS[h, :, :], (), [b_VA[s]], ch_VA[s])

        chunks = [(h, qc) for h in range(DBG.get('nhB1', 8)) for qc in range(DBG.get('nqcB', NQC))]

        def load_q(m):
            h, qc = chunks[m]
            s = m % 3
            S.dma(QC[s][0:96, :], qaT[h, :, qc * 512:(qc + 1) * 512], (), [b_QC[s]], ch_QC[s])

        steps = [(m, g) for m in range(len(chunks)) for g in range(NG)]

        def rec_S(n):
            m, g = steps[n]
            h, qc = chunks[m]
            if g == 1 and qc == 0 and h + 1 < DBG.get('nhB1', 8):
                load_kv(h + 1)
            if g == 0:
                if m + 2 < len(chunks):
                    load_q(m + 2)
            if g == 2:
                next(cw_gen, None)
            sbk = n % 2
            for i in range(G):
                kb = g * G + i
                col = (sbk * G + i) * 512
                S.mm(P[:, col:col + 512], KA[h % 2][0:96, kb * 128:(kb + 1) * 128], QC[m % 3][0:96, :], True, True,
                     [b_KA[h % 2], b_QC[m % 3]], [b_pS[sbk][i]])

        def rec_rest(n):
            m, g = steps[n]
            h, qc = chunks[m]
            sbk = n % 2
            ps = n % 3
            col = sbk * G * 512
            S.act(PT[ps][:, :], P[:, col:col + G * 512], AF.Exp, b_pS[sbk], [b_PT[ps]], scale=MLA_SCALE)
            for i in range(G):
                kb = g * G + i
                S.mm(POs[m % 2], VA[h % 2][:, kb, :], PT[ps][:, i * 512:(i + 1) * 512], g == 0 and i == 0,
                     g == NG - 1 and i == G - 1, [b_VA[h % 2], b_PT[ps]], [b_pO[m % 2]])
            if g == NG - 1:
                o = m % 2
                S.cp("dve", OS[o][0:65, :], POs[o], [b_pO[o]], [b_OS[o]])
                S.recip(OS[o][64:65, :], OS[o][64:65, :], [b_OS[o]], [b_OS[o]])
                S.dma(oaT[h * 64:(h + 1) * 64, qc * 512:(qc + 1) * 512], OS[o][0:64, :], [b_OS[o]], (), ch_O[o])
                S.dma(saS[h:h + 1, qc * 512:(qc + 1) * 512], OS[o][64:65, :], [b_OS[o]], (), ch_Os[o])

        if steps:
            load_kv(0)
            load_q(0)
            if len(chunks) > 1:
                load_q(1)
            rec_S(0)
        for n in range(len(steps)):
            if n + 1 < len(steps):
                rec_S(n + 1)
            rec_rest(n)
        for _ in cw_gen:
            pass
        S.flush()

    with ExitStack() as ph:
        KD = [sb(ph, f"KD{i}", [128, NK], BF16) for i in range(2)]
        VD = [sb(ph, f"VD{i}", [128, NKB, 128], BF16) for i in range(2)]
        QC = [sb(ph, f"QD{i}", [128, 512], BF16) for i in range(3)]
        PT = [sb(ph, f"PD{i}", [128, 1024], BF16) for i in range(3)]
        ACC = [[sb(ph, f"ACC{i}{j}", [128, 1024], F32) for j in range(2)] for i in range(2)]
        OS = [[sb(ph, f"OD{i}{j}", [128, 512], F32) for j in range(2)] for i in range(2)]
        SS = [sb(ph, f"SS{i}", [1, 1024], F32) for i in range(2)]
        b_KD = S.bufs_n(2, "KD")
        b_VD = S.bufs_n(2, "VD")
        b_QC = S.bufs_n(3, "QD")
        b_PT = S.bufs_n(3, "PD")
        b_ACC = [S.bufs_n(2, f"ACC{i}_") for i in range(2)]
        b_OS = [S.bufs_n(2, f"OD{i}_") for i in range(2)]
        b_SS = S.bufs_n(2, "SS")
        b_pS = [S.pbanks(2, f"pSd{i}_") for i in range(2)]
        b_pO = S.pbanks(2, "pOd")
        b_pSum = S.pbanks(2, "pSum")
        b_onesf = S.buf("onesf")
        ch_KD = [S.chan() for _ in range(2)]
        ch_VD = [S.chan() for _ in range(2)]
        ch_QC = [S.chan() for _ in range(3)]
        ch_O = [[S.chan() for _ in range(2)] for _ in range(2)]
        ch_Ss = [[S.chan() for _ in range(2)] for _ in range(2)]
        PO = [P[:, 4 * 512:5 * 512], P[:, 5 * 512:6 * 512]]
        PSUMS = [P[0:1, 6 * 512:7 * 512], P[0:1, 7 * 512:8 * 512]]

        def load_kv(h):
            s = h % 2
            S.dma(KD[s][:, :], kdT[h, :, :], (), [b_KD[s]], ch_KD[s])
            S.dma(VD[s][:].rearrange("p k d -> p (k d)"), vdS[h, :, :], (), [b_VD[s]], ch_VD[s])

        chunks = [(h, qc) for h in range(DBG.get('nhB2', 4)) for qc in range(DBG.get('nqcB', NQC))]

        def load_q(m):
            h, qc = chunks[m]
            s = m % 3
            S.dma(QC[s][:, :], qdT[h, :, qc * 512:(qc + 1) * 512], (), [b_QC[s]], ch_QC[s])

        steps = [(m, kb) for m in range(len(chunks)) for kb in range(NKB)]

        def rec_S(n):
            m, kb = steps[n]
            h, qc = chunks[m]
            if kb == 1 and qc == 0 and h + 1 < DBG.get('nhB2', 4):
                load_kv(h + 1)
            if kb == 0:
                if m + 2 < len(chunks):
                    load_q(m + 2)
            sbk = n % 2
            for i in range(2):
                col = (sbk * 2 + i) * 512
                S.mm(P[:, col:col + 512], KD[h % 2][i * 64:(i + 1) * 64, kb * 128:(kb + 1) * 128],
                     QC[m % 3][i * 64:(i + 1) * 64, :], True, True, [b_KD[h % 2], b_QC[m % 3]], [b_pS[sbk][i]])

        def rec_rest(n):
            m, kb = steps[n]
            h, qc = chunks[m]
            sbk = n % 2
            ps = n % 3
            col = sbk * 1024
            a = m % 2
            par = kb % 2
            S.act(PT[ps][:, :], P[:, col:col + 1024], AF.Exp, b_pS[sbk], [b_PT[ps]], scale=DIFF_SCALE)
            for i in range(2):
                S.mm(PO[i], VD[h % 2][:, kb, :], PT[ps][:, i * 512:(i + 1) * 512], kb == 0, kb == NKB - 1,
                     [b_VD[h % 2], b_PT[ps]], [b_pO[i]])
            eng, acc_i, first = "dve", kb % 2, kb < 2
            if first:
                S.cp(eng, ACC[a][acc_i][:, :], PT[ps][:, :], [b_PT[ps]], [b_ACC[a][acc_i]])
            else:
                S.tt(eng, ACC[a][acc_i][:, :], ACC[a][acc_i][:, :], PT[ps][:, :], ALU.add,
                     [b_PT[ps], b_ACC[a][acc_i]], [b_ACC[a][acc_i]])
            if kb == NKB - 1:
                for i in range(2):
                    S.cp("dve", OS[a][i][:, :], PO[i], [b_pO[i]], [b_OS[a][i]])
                    dst = (od1T, od2T)[i]
                    S.dma(dst[h * 128:(h + 1) * 128, qc * 512:(qc + 1) * 512], OS[a][i][:, :], [b_OS[a][i]], (),
                          ch_O[a][i])
                for i in range(2):
                    for p2 in range(2):
                        S.mm(PSUMS[i], ones_f[:, 0:1], ACC[a][p2][:, i * 512:(i + 1) * 512], p2 == 0, p2 == 1,
                             [b_onesf, b_ACC[a][p2]], [b_pSum[i]])
                    S.cp("dve", SS[a][0:1, i * 512:(i + 1) * 512], PSUMS[i], [b_pSum[i]], [b_SS[a]])
                for i in range(2):
                    S.dma(sdS[2 * h + i:2 * h + i + 1, qc * 512:(qc + 1) * 512], SS[a][0:1, i * 512:(i + 1) * 512],
                          [b_SS[a]], (), ch_Ss[a][i])

        if steps:
            load_kv(0)
            load_q(0)
            if len(chunks) > 1:
                load_q(1)
            rec_S(0)
        for n in range(len(steps)):
            if n + 1 < len(steps):
                rec_S(n + 1)
            rec_rest(n)
        S.flush()


def phase_C(nc, S, sb, P, env):
    ident, ones_b, modT = env["ident"], env["ones_b"], env["modT"]
    gate_bc, lng_bc, lnb_bc, neglam, sublnS = env["gate_bc"], env["lng_bc"], env["lnb_bc"], env["neglam"], env["sublnS"]
    xk, wG, w_oa, w_ob, w_out, out = env["xk"], env["wG"], env["w_oa"], env["w_ob"], env["w_out"], env["out"]
    oaT, saS, od1T, od2T, sdS = env["oaT"], env["saS"], env["od1T"], env["od2T"], env["sdS"]
    NCH = DBG.get('nchC', T // W)
    NX = 3
    with ExitStack() as ph:
        WG, WOA, WOB, WOUT = env["WG"], env["WOA"], env["WOB"], env["WOUT"]
        xin = [sb(ph, f"cxin{i}", [128, 2, D], F32) for i in range(NX)]
        hT = [sb(ph, f"chT{i}", [128, 8, W], BF16) for i in range(2)]
        oa = sb(ph, "oa", [128, 4, W], F32)
        rab = sb(ph, "rab", [128, 4, W], F32)
        od1 = sb(ph, "od1", [128, 4, W], F32)
        od2 = sb(ph, "od2", [128, 4, W], F32)
        r1b = sb(ph, "r1b", [128, 4, W], F32)
        r2b = sb(ph, "r2b", [128, 4, W], F32)
        sil = sb(ph, "sil", [128, 8, W], F32)
        NTMP = 6
        tmp = [sb(ph, f"ctmp{i}", [128, W], F32) for i in range(NTMP)]
        dsq = sb(ph, "dsq", [128, 4, W], BF16)
        rn = sb(ph, "rn", [128, 4, W], F32)
        oga = [sb(ph, f"oga{i}", [128, 4, W], BF16) for i in range(2)]
        ogd = [sb(ph, f"ogd{i}", [128, 4, W], BF16) for i in range(2)]
        zT = sb(ph, "zT", [128, 8, W], BF16)
        tv = [sb(ph, f"tv{i}", [128, D], F32) for i in range(2)]
        ost = [sb(ph, f"ost{i}", [128, D], F32) for i in range(2)]
        stat = sb(ph, "stat", [128, 2, 6], F32)
        mv = sb(ph, "mv", [128, 4], F32)
        stg = tv
        eps_t = sb(ph, "eps_t", [128, 1], F32)
        b_eps = S.buf("eps")
        S.memset("pool", eps_t[:], SUBLN_EPS, [b_eps])
        rsb = ost

        b_ident, b_onesb, b_modT, b_gate, b_lng, b_lnb, b_neglam, b_subln = S.bufs_n(8, "cC")
        b_WG, b_WOA, b_WOB, b_WOUT = (S.bufs_n(2, n) for n in ("wG", "wOA", "wOB", "wOUT"))
        b_xin = S.bufs_n(NX, "cxin")
        b_hT = [S.bufs_n(8, f"chT{i}_") for i in range(2)]
        b_oa, b_rab, b_od1, b_od2, b_r1b, b_r2b = S.bufs_n(6, "o")
        b_sil = S.bufs_n(8, "sil")
        b_tmp = S.bufs_n(NTMP, "ctmp")
        b_dsq = S.buf("dsq")
        b_rn = S.bufs_n(4, "rn")
        b_oga = S.bufs_n(2, "oga")
        b_ogd = S.bufs_n(2, "ogd")
        b_zT = S.bufs_n(8, "zT")
        b_tv = S.bufs_n(2, "tv")
        b_stg = b_tv
        b_ost = S.bufs_n(2, "ost")
        b_stat, b_mv = S.bufs_n(2, "st")
        pbk = S.pbanks(8, "cpbk")
        ch_stg = [S.chan() for _ in range(2)]
        ch_x = [S.chan() for _ in range(NX)]
        ch_in = {n: S.chan() for n in ("oa", "rab0", "rab1", "od1", "od2", "r1b", "r2b")}
        ch_out = [S.chan() for _ in range(2)]

        if not env["cw_loaded"][0]:
            load_weight_bf16(S, WG, b_WG, wG, 8, 3072, stg, b_stg, ch_stg, piece=1024)
            load_weight_bf16(S, WOA, b_WOA, w_oa, 4, D, stg, b_stg, ch_stg, piece=1024)
            load_weight_bf16(S, WOB, b_WOB, w_ob, 4, D, stg, b_stg, ch_stg, piece=1024)
            load_weight_bf16(S, WOUT, b_WOUT, w_out, 8, D, stg, b_stg, ch_stg, piece=1024, engines=("dve",),
                             colscale=gate_bc, colscale_buf=b_gate)

        bank_rr = [0]
        tmp_rr = [0]

        def get_bank():
            f = 4 + bank_rr[0] % 4
            bank_rr[0] += 1
            return f

        def get_tmp():
            i = tmp_rr[0] % NTMP
            tmp_rr[0] += 1
            return i

        def Pb(f, half, m=128):
            return P[0:m, f * 512 + half * 256:f * 512 + (half + 1) * 256]

        def bc_src(t, row0, rstride, nrow, t0, nparts):
            ncol = t.shape[1]
            return bass.AP(t.tensor, row0 * ncol + t0, [[0, nparts], [rstride * ncol, nrow], [1, W]])

        def load_x(ci):
            s = ci % NX
            t0 = ci * W
            S.dma(xin[s][:], xk[CTX + t0:CTX + t0 + W, :].rearrange("(j p) d -> p j d", p=128), (), [b_xin[s]], ch_x[s])

        def load_o(ci):
            t0 = ci * W
            S.dma(oa[:], oaT[:, t0:t0 + W].rearrange("(c p) w -> p c w", p=128), (), [b_oa], ch_in["oa"])
            S.dma(od1[:], od1T[:, t0:t0 + W].rearrange("(c p) w -> p c w", p=128), (), [b_od1], ch_in["od1"])
            S.dma(od2[:], od2T[:, t0:t0 + W].rearrange("(c p) w -> p c w", p=128), (), [b_od2], ch_in["od2"])
            S.dma(rab[0:64, :, :], bc_src(saS, 0, 2, 4, t0, 64), (), [b_rab], ch_in["rab0"])
            S.dma(rab[64:128, :, :], bc_src(saS, 1, 2, 4, t0, 64), (), [b_rab], ch_in["rab1"])
            S.dma(r1b[:], bc_src(sdS, 0, 2, 4, t0, 128), [b_rd], [b_r1b], ch_in["r1b"])
            S.dma(r2b[:], bc_src(sdS, 1, 2, 4, t0, 128), [b_rd], [b_r2b], ch_in["r2b"])

        def flat(t):
            return t[:].rearrange("p c w -> p (c w)")

        def stage_X(ci, seg):
            sx = ci % NX
            s = ci % 2
            S.tag = f"X{seg}c{ci}"
            if seg == -1:
                stage_X0(ci, sx, s)
            elif seg == 0:
                stage_X1(ci, sx, s)
            elif seg == 1:
                stage_X2(ci, sx, s)
            else:
                stage_X3(ci, sx, s)

        def stage_X0(ci, sx, s):
            for dk in range(8):
                for j in range(2):
                    S.tr(P[:, dk * 256 + j * 128:dk * 256 + (j + 1) * 128], xin[sx][:, j, dk * 128:(dk + 1) * 128],
                         ident[:], [b_xin[sx], b_ident], [pbk[dk // 2]])
            for dk in range(8):
                S.act(hT[s][:, dk, :], Pb(dk // 2, dk % 2), AF.Identity, [pbk[dk // 2], b_modT], [b_hT[s][dk]],
                      bias=modT[:, dk, 0:1], scale=modT[:, 8 + dk, 0:1])
        def stage_X1(ci, sx, s):
            for mp in range(4):
                f = get_bank()
                for idx in range(2):
                    m = mp * 2 + idx
                    for k in range(8):
                        S.mm(Pb(f, idx), WG[:, k, m * 128:(m + 1) * 128], hT[s][:, k, :], k == 0, k == 7,
                             [b_WG, b_hT[s][k]], [pbk[f]])
                for idx in range(2):
                    m = mp * 2 + idx
                    t1 = get_tmp()
                    S.act(tmp[t1][:, :], Pb(f, idx), AF.Sigmoid, [pbk[f]], [b_tmp[t1]])
                    S.tt("dve", sil[:, m, :], Pb(f, idx), tmp[t1][:, :], ALU.mult, [pbk[f], b_tmp[t1]], [b_sil[m]])
        def stage_X2(ci, sx, s):
            for _ in gen_X2(ci, s):
                pass

        def gen_X2(ci, s):
            tg = f"X1c{ci}"
            S.tag = tg
            S.tt("pool", flat(oa), flat(oa), flat(rab), ALU.mult, [b_oa, b_rab], [b_oa])
            S.tt("pool", flat(od1), flat(od1), flat(r1b), ALU.mult, [b_od1, b_r1b], [b_od1])
            yield
            S.tag = tg
            S.tt("dve", flat(oga[s]), flat(oa), sil[:, 0:4, :].rearrange("p c w -> p (c w)"), ALU.mult,
                 [b_oa] + b_sil[0:4], [b_oga[s]])
            S.tt("pool", flat(od2), flat(od2), flat(r2b), ALU.mult, [b_od2, b_r2b], [b_od2])
            yield
            S.tag = tg
            S.stt(flat(od1), flat(od2), neglam[:, 0:1], flat(od1), ALU.mult, ALU.add, [b_od2, b_od1, b_neglam], [b_od1])
            yield
            S.tag = tg
            S.tt("pool", flat(dsq), flat(od1), flat(od1), ALU.mult, [b_od1], [b_dsq])
            yield

        def stage_X3(ci, sx, s):
            for cp_ in range(2):
                f = get_bank()
                for idx in range(2):
                    c = cp_ * 2 + idx
                    S.mm(Pb(f, idx), ones_b[:, :], dsq[:, c, :], True, True, [b_onesb, b_dsq], [pbk[f]])
                for idx in range(2):
                    c = cp_ * 2 + idx
                    S.act(rn[:, c, :], Pb(f, idx), AF.Ln, [pbk[f], b_eps], [b_rn[c]], bias=eps_t[:, 0:1], scale=1.0 / 128)
            S.act(flat(rn), flat(rn), AF.Exp, b_rn, b_rn, scale=-0.5)
            S.tt("pool", flat(od2), flat(od1), flat(rn), ALU.mult, [b_od1] + b_rn, [b_od2])
            S.stt(flat(ogd[s]), flat(od2), sublnS[:, 0:1], sil[:, 4:8, :].rearrange("p c w -> p (c w)"), ALU.mult,
                  ALU.mult, [b_od2, b_subln] + b_sil[4:8], [b_ogd[s]])

        def stage_Y(ci, seg, filler=None):
            sx = ci % NX
            s = ci % 2
            t0 = ci * W
            S.tag = f"Y{seg}c{ci}"
            if seg < 2:
                stage_Y1(ci, s, range(seg * 4, seg * 4 + 4), filler, f"Y{seg}c{ci}")
            else:
                stage_Y2(ci, sx, t0, seg - 2)

        def stage_Y1(ci, s, ms, filler=None, tg=None):
            for m in ms:
                if filler is not None and m != ms[0]:
                    next(filler, None)
                S.tag = tg
                fa = get_bank()
                for c in range(4):
                    S.mm(Pb(fa, 0), WOA[:, c, m * 128:(m + 1) * 128], oga[s][:, c, :], c == 0, c == 3,
                         [b_WOA, b_oga[s]], [pbk[fa]])
                for c in range(4):
                    S.mm(Pb(fa, 1), WOB[:, c, m * 128:(m + 1) * 128], ogd[s][:, c, :], c == 0, c == 3,
                         [b_WOB, b_ogd[s]], [pbk[fa]])
                fm = get_bank()
                for idx in range(2):
                    for k in range(8):
                        S.mm(Pb(fm, idx), WG[:, k, 1024 * (idx + 1) + m * 128:1024 * (idx + 1) + (m + 1) * 128],
                             hT[s][:, k, :], k == 0, k == 7, [b_WG, b_hT[s][k]], [pbk[fm]])
                t1 = get_tmp()
                S.act(tmp[t1][:, :], Pb(fm, 0), AF.Sigmoid, [pbk[fm]], [b_tmp[t1]])
                t2 = get_tmp()
                S.act(tmp[t2][:, :], Pb(fm, 1), AF.Sigmoid, [pbk[fm]], [b_tmp[t2]])
                S.tt("dve", tmp[t1][:, :], Pb(fa, 0), tmp[t1][:, :], ALU.mult, [pbk[fa], b_tmp[t1]], [b_tmp[t1]])
                S.tt("dve", tmp[t2][:, :], Pb(fa, 1), tmp[t2][:, :], ALU.mult, [pbk[fa], b_tmp[t2]], [b_tmp[t2]])
                S.tt("pool", zT[:, m, :], tmp[t1][:, :], tmp[t2][:, :], ALU.add, [b_tmp[t1], b_tmp[t2]], [b_zT[m]])
        def stage_Y2(ci, sx, t0, j):
            if True:
                o = j
                for half in range(2):
                    f = get_bank()
                    for m in range(8):
                        S.mm(P[:, f * 512:(f + 1) * 512], zT[:, m, j * 128:(j + 1) * 128],
                             WOUT[:, m, half * 512:(half + 1) * 512], m == 0, m == 7, [b_zT[m], b_WOUT],
                             [pbk[f]])
                    S.stt(tv[o][:, half * 512:(half + 1) * 512], xin[sx][:, j, half * 512:(half + 1) * 512], ALPHA,
                          P[:, f * 512:(f + 1) * 512], ALU.mult, ALU.add, [b_xin[sx], pbk[f]], [b_tv[o]])
                S.op("dve", lambda h, o=o: h.bn_stats(stat[:, 0, :], tv[o][:, 0:512]), [b_tv[o]], [b_stat])
                S.op("dve", lambda h, o=o: h.bn_stats(stat[:, 1, :], tv[o][:, 512:1024]), [b_tv[o]], [b_stat])
                S.op("dve", lambda h: h.bn_aggr(mv[:, 0:2], stat[:, :, :]), [b_stat], [b_mv])
                S.act(mv[:, 2:3], mv[:, 1:2], AF.Sqrt, [b_mv], [b_mv], bias=LN_EPS, scale=1.0)
                S.recip(mv[:, 2:3], mv[:, 2:3], [b_mv], [b_mv])
                S.ts("dve", mv[:, 3:4], mv[:, 0:1], -1.0, mv[:, 2:3], ALU.mult, ALU.mult, [b_mv], [b_mv])
                S.act(tv[o][:, :], tv[o][:, :], AF.Identity, [b_tv[o], b_mv], [b_tv[o]], bias=mv[:, 3:4], scale=mv[:, 2:3])
                S.tt("pool", tv[o][:, :], tv[o][:, :], lng_bc[:, :], ALU.mult, [b_tv[o], b_lng], [b_tv[o]])
                S.tt("pool", ost[o][:, :], tv[o][:, :], lnb_bc[:, :], ALU.add, [b_tv[o], b_lnb], [b_ost[o]])
                S.dma(out[t0 + j * 128:t0 + (j + 1) * 128, :], ost[o][:, :], [b_ost[o]], (), ch_out[o])

        b_rd = S.buf("rdram")
        b_rst = b_ost
        ch_rl = [S.chan() for _ in range(2)]
        ch_rs = [S.chan() for _ in range(2)]
        for pc in range(DBG.get('nqcB', T // 512)):
            i = pc % 2
            S.dma(rsb[i][0:8, 0:512], sdS[:, pc * 512:(pc + 1) * 512], (), [b_rst[i]], ch_rl[i])
            S.recip(rsb[i][0:8, 0:512], rsb[i][0:8, 0:512], [b_rst[i]], [b_rst[i]])
            S.dma(sdS[:, pc * 512:(pc + 1) * 512], rsb[i][0:8, 0:512], [b_rst[i]], [b_rd], ch_rs[i])
        load_x(0)
        if NCH > 1:
            load_x(1)
        load_o(0)
        for seg in range(-1, 3):
            stage_X(0, seg)
        for ci in range(NCH):
            if ci + 2 < NCH:
                load_x(ci + 2)
            nxt = ci + 1 < NCH
            if nxt:
                load_o(ci + 1)
                stage_X(ci + 1, -1)
            stage_Y(ci, 0)
            filler = None
            if nxt:
                stage_X(ci + 1, 0)
                filler = gen_X2(ci + 1, (ci + 1) % 2)
                next(filler, None)
            stage_Y(ci, 1, filler)
            if nxt:
                for _ in filler:
                    pass
                stage_X(ci + 1, 2)
            stage_Y(ci, 2)
            stage_Y(ci, 3)
        S.flush()


def _perm(n):
    q = n // 4
    p = np.concatenate([np.arange(0, q), np.arange(2 * q, 3 * q), np.arange(q, 2 * q), np.arange(3 * q, 4 * q)])
    ps = np.concatenate([p[n // 2:], p[:n // 2]])
    return p, ps


def _rope_table():
    tab = np.zeros((128, 4, NK), np.float32)
    tab[:, 0, :] = 1.0
    tab[:, 2, :] = 1.0
    t = np.arange(T)
    row = (t // 64).astype(np.float32)
    col = (t % 64).astype(np.float32)

    def cs(n):
        q = n // 4
        inv = (ROPE_THETA ** (-np.arange(q, dtype=np.float32) / np.float32(q))).astype(np.float32)
        c = np.zeros((n, T), np.float32)
        s = np.zeros((n, T), np.float32)
        for r in range(n):
            j = r % (n // 2)
            ang = (row * inv[j]) if j < q else (col * inv[j - q])
            ang = ang.astype(np.float32)
            c[r] = np.cos(ang)
            s[r] = -np.sin(ang) if r < n // 2 else np.sin(ang)
        return c, s

    c64, s64 = cs(64)
    c32, s32 = cs(32)
    tab[:, 0, CTX:] = np.concatenate([c64, c64], 0)
    tab[:, 1, CTX:] = np.concatenate([s64, s64], 0)
    tab[64:96, 2, CTX:] = c32
    tab[64:96, 3, CTX:] = s32
    return tab


def _prep_shared(w_mod, b_mod, w_in, mla_q_norm, mla_kv_norm, w_uq, w_ukv, diff_lambda, diff_subln, w_oa, w_ob,
                 w_out, ln_g, ln_b):
    f = lambda a: np.ascontiguousarray(a, dtype=np.float32)
    w_in = w_in[0]
    p32, p32s = _perm(32)
    p64, p64s = _perm(64)
    cols = [w_in[:, 0:640]]
    cols.append(w_in[:, 576:640])
    cols.append(w_in[:, 640:672][:, p32])
    cols.append(w_in[:, 576:640])
    cols.append(w_in[:, 640:672][:, p32s])
    for base in (1184, 1696):
        for pp in (p64, p64s):
            for blk in range(8):
                cols.append(w_in[:, base + blk * 64:base + (blk + 1) * 64][:, pp])
    cols.append(w_in[:, 2208:2720])
    wA = np.concatenate(cols, axis=1)
    assert wA.shape == (D, NA)
    wG = np.concatenate([w_in[:, 672:1184], w_in[:, 2720:3232], w_in[:, 3232:5280]], axis=1)
    uq = w_uq[0].reshape(384, 8, 96)
    uq_n = np.concatenate([uq[:, :, 0:64], uq[:, :, 64:96][:, :, p32]], axis=2).reshape(384, 768)
    uq_s = np.concatenate([uq[:, :, 0:64], uq[:, :, 64:96][:, :, p32s]], axis=2).reshape(384, 768)
    wuq2 = np.concatenate([uq_n, uq_s], axis=1)
    ukv = w_ukv[0].reshape(256, 8, 128)
    wukv = np.concatenate([ukv[:, :, 0:64].reshape(256, 512), ukv[:, :, 64:128].reshape(256, 512)], axis=1)
    return {
        "w_mod": f(w_mod[0]),
        "bmodT": f(b_mod[0].reshape(24, 128).T),
        "bgate": f(b_mod[0][2048:3072].reshape(1, D)),
        "wA": f(wA),
        "wG": f(wG),
        "wuq2": f(wuq2),
        "wukv": f(wukv),
        "qnT": f(mla_q_norm[0].reshape(3, 128).T),
        "kvnT": f(mla_kv_norm[0].reshape(2, 128).T),
        "lamb": f(diff_lambda[0].reshape(1, 256)),
        "sublnT": f(diff_subln[0].reshape(128, 1)),
        "w_oa": f(w_oa[0]),
        "w_ob": f(w_ob[0]),
        "w_out": f(w_out[0]),
        "ln_g": f(ln_g[0].reshape(1, D)),
        "ln_b": f(ln_b[0].reshape(1, D)),
        "rope_tab": _rope_table(),
        "ident": np.eye(128, dtype=np.float32),
    }


def make_in_maps(x, c, ctx, c_ctx, **w):
    x, c, ctx, c_ctx = (np.asarray(a, dtype=np.float32) for a in (x, c, ctx, c_ctx))
    shared = _prep_shared(**{k: np.asarray(v, dtype=np.float32) for k, v in w.items()})
    in_maps = []
    for b in range(x.shape[0]):
        m = dict(shared)
        m["xk"] = np.ascontiguousarray(np.concatenate([ctx[b], x[b]], axis=0))
        cv = np.stack([c[b].reshape(8, 128).T, c_ctx.reshape(8, 128).T], axis=2)
        m["cvec"] = np.ascontiguousarray(cv, dtype=np.float32)
        in_maps.append(m)
    return in_maps


_NC_CACHE = {}


def kernel(x, c, ctx, c_ctx, w_mod, b_mod, w_in, mla_q_norm, mla_kv_norm, w_uq, w_ukv, diff_lambda, diff_subln,
           w_oa, w_ob, w_out, ln_g, ln_b):
    in_maps = make_in_maps(x, c, ctx, c_ctx, w_mod=w_mod, b_mod=b_mod, w_in=w_in, mla_q_norm=mla_q_norm,
                           mla_kv_norm=mla_kv_norm, w_uq=w_uq, w_ukv=w_ukv, diff_lambda=diff_lambda,
                           diff_subln=diff_subln, w_oa=w_oa, w_ob=w_ob, w_out=w_out, ln_g=ln_g, ln_b=ln_b)
    if "nc" not in _NC_CACHE:
        _NC_CACHE["nc"] = build_nc()
    nc = _NC_CACHE["nc"]
    n = len(in_maps)
    res = run_bass_kernel_spmd(nc, in_maps, core_ids=list(range(n)))
    return np.stack([np.asarray(r["out"], dtype=np.float32) for r in res.results], axis=0)
```

```python
import math
from contextlib import ExitStack

import numpy as np
import concourse.bass as bass
import concourse.mybir as mybir
from concourse.bass_utils import run_bass_kernel_spmd

F32 = mybir.dt.float32
BF16 = mybir.dt.bfloat16
AF = mybir.ActivationFunctionType
ALU = mybir.AluOpType
AX = mybir.AxisListType

D = 1024
T = 8192
CTX = 256
NK = T + CTX
NKB = NK // 128
W = 256
NA = 3392
MLA_SCALE = 96 ** -0.5
DIFF_SCALE = 64 ** -0.5
LN_EPS = 1e-5
RMS_EPS = 1e-6
SUBLN_EPS = 1e-5
ALPHA = 2 ** 0.25
LAM_INIT = 0.8 - 0.6 * math.exp(0.0)
ROPE_THETA = 10000.0
DBG = {}


class Buf:
    __slots__ = ("name", "w", "r", "excl")

    def __init__(self, name, excl=False):
        self.name = name
        self.w = None
        self.r = {}
        self.excl = excl


class Chan:
    __slots__ = ("sem", "count", "idx")

    def __init__(self, sem, idx):
        self.sem = sem
        self.count = 0
        self.idx = idx


class Op:
    __slots__ = ("eng", "fn", "deps", "sig", "chan", "need", "seq", "tag")


def _flat(bs):
    outl = []
    for b in bs:
        if isinstance(b, (list, tuple)):
            outl.extend(_flat(b))
        else:
            outl.append(b)
    return outl


class Sched:
    CAP = 30000

    def __init__(self, nc, es, nchan=84):
        self.nc = nc
        self.es = es
        self.h = {"pe": nc.tensor, "act": nc.scalar, "dve": nc.vector, "pool": nc.gpsimd, "sp": nc.sync}
        self.ops = []
        self.bufs = []
        self.nsig = {e: 0 for e in self.h}
        self.sems = {e: [] for e in self.h}
        self.waited = {e: {} for e in self.h}
        self.chans = [Chan(es.enter_context(nc.semaphore(f"dch{i}")), i) for i in range(nchan)]
        self.chan_next = 0
        self.seq = 0
        self.last = {}
        self.n_inst = 0
        self.tag = None

    def buf(self, name="b", excl=False):
        b = Buf(name, excl)
        self.bufs.append(b)
        return b

    def bufs_n(self, n, name="b", excl=False):
        return [self.buf(f"{name}{i}", excl) for i in range(n)]

    def pbanks(self, n, name="pb"):
        return self.bufs_n(n, name, excl=True)

    def chan(self):
        c = self.chans[self.chan_next]
        self.chan_next += 1
        return c

    @staticmethod
    def _key(o):
        return ("c", o.chan.idx) if o.chan is not None else ("e", o.eng)

    def op(self, eng, fn, reads=(), writes=(), chan=None):
        o = Op()
        o.eng = eng
        o.fn = fn
        o.chan = chan
        o.sig = None
        o.need = False
        o.tag = self.tag
        self.seq += 1
        o.seq = self.seq
        deps = {}
        reads = _flat(reads)
        writes = _flat(writes)

        def add(d):
            if d is None:
                return
            k = self._key(d)
            if k not in deps or deps[k].seq < d.seq:
                deps[k] = d

        k_self = self._key(o)
        for b in reads:
            if b.excl:
                for kk, r in b.r.items():
                    if kk != k_self:
                        add(r)
            else:
                add(b.w)
        for b in writes:
            if b.excl:
                for kk, r in b.r.items():
                    if kk != k_self:
                        add(r)
            else:
                add(b.w)
                for r in b.r.values():
                    add(r)
        for b in writes:
            if b.excl:
                b.r[k_self] = o
            else:
                b.w = o
                b.r = {}
        for b in reads:
            b.r[k_self] = o
        o.deps = []
        for k, d in deps.items():
            if k == ("e", "pe") and eng == "pe" and chan is None:
                continue
            d.need = True
            o.deps.append(d)
        self.ops.append(o)
        self.last[k_self] = o
        return o

    def flush(self):
        lasts = list(self.last.values())
        for e in self.h:
            o = Op()
            o.eng = e
            o.fn = None
            o.chan = None
            o.sig = None
            o.need = False
            o.tag = None
            self.seq += 1
            o.seq = self.seq
            o.deps = [d for d in lasts if not (d.eng == "pe" and e == "pe" and d.chan is None)]
            for d in o.deps:
                d.need = True
            self.ops.append(o)
        for o in self.ops:
            if o.chan is not None:
                o.chan.count += 1
                o.sig = (("c", o.chan.idx), o.chan.sem, 16 * o.chan.count)
            elif o.need:
                k = self.nsig[o.eng]
                si = k // self.CAP
                while len(self.sems[o.eng]) <= si:
                    self.sems[o.eng].append(
                        self.es.enter_context(self.nc.semaphore(f"s_{o.eng}{len(self.sems[o.eng])}")))
                o.sig = (("e", o.eng, si), self.sems[o.eng][si], k % self.CAP + 1)
                self.nsig[o.eng] += 1
        for o in self.ops:
            h = self.h[o.eng]
            wt = self.waited[o.eng]
            for d in o.deps:
                key, sem, val = d.sig
                if wt.get(key, 0) >= val:
                    continue
                h.wait_ge(sem, val)
                wt[key] = val
                self.n_inst += 1
            if o.fn is not None:
                ins = o.fn(h)
                if o.tag is not None and DBG.get("annotate"):
                    ins.annotate(o.tag)
                self.n_inst += 1
                if o.sig is not None:
                    ins.then_inc(o.sig[1], 16 if o.chan is not None else 1)
        self.ops = []
        self.last = {}
        for b in self.bufs:
            b.w = None
            b.r = {}
        self.bufs = []
        self.chan_next = 0

    def mm(self, out, lhsT, rhs, start, stop, reads, writes):
        return self.op("pe", lambda h: h.matmul(out, lhsT, rhs, start=start, stop=stop), reads, writes)

    def tr(self, out, in_, ident, reads, writes):
        return self.op("pe", lambda h: h.transpose(out, in_, ident), reads, writes)

    def act(self, out, in_, func, reads, writes, bias=None, scale=None, accum_out=None):
        kw = {}
        if bias is not None:
            kw["bias"] = bias
        if scale is not None:
            kw["scale"] = scale
        if accum_out is not None:
            kw["accum_out"] = accum_out
        return self.op("act", lambda h: h.activation(out, in_, func, **kw), reads, writes)

    def tt(self, eng, out, in0, in1, op, reads, writes):
        return self.op(eng, lambda h: h.tensor_tensor(out, in0, in1, op), reads, writes)

    def ts(self, eng, out, in0, s1, s2, op0, op1, reads, writes):
        if s2 is None:
            return self.op(eng, lambda h: h.tensor_scalar(out, in0, s1, None, op0), reads, writes)
        return self.op(eng, lambda h: h.tensor_scalar(out, in0, s1, s2, op0, op1), reads, writes)

    def stt(self, out, in0, scalar, in1, op0, op1, reads, writes):
        return self.op("dve", lambda h: h.scalar_tensor_tensor(out, in0, scalar, in1, op0, op1), reads, writes)

    def cp(self, eng, out, in_, reads, writes):
        if eng == "act":
            return self.op("act", lambda h: h.activation(out, in_, AF.Copy), reads, writes)
        return self.op(eng, lambda h: h.tensor_copy(out, in_), reads, writes)

    def recip(self, out, in_, reads, writes):
        return self.op("dve", lambda h: h.reciprocal(out, in_), reads, writes)

    def memset(self, eng, ap, val, writes):
        return self.op(eng, lambda h: h.memset(ap, val), (), writes)

    def dma(self, out, in_, reads, writes, chan):
        return self.op("sp", lambda h: h.dma_start(out=out, in_=in_), reads, writes, chan=chan)


def build_nc(debug=False, phases="0ABC"):
    nc = bass.Bass("TRN2", target_bir_lowering=False)

    def din(name, shape):
        return nc.dram_tensor(name, shape, F32, kind="ExternalInput").ap()

    xk = din("xk", [NK, D])
    cvec = din("cvec", [128, 8, 2])
    w_mod = din("w_mod", [D, 3 * D])
    bmodT = din("bmodT", [128, 24])
    bgate = din("bgate", [1, D])
    wA = din("wA", [D, NA])
    wG = din("wG", [D, 3072])
    wuq2 = din("wuq2", [384, 1536])
    wukv = din("wukv", [256, 1024])
    qnT = din("qnT", [128, 3])
    kvnT = din("kvnT", [128, 2])
    lamb = din("lamb", [1, 256])
    sublnT = din("sublnT", [128, 1])
    w_oa = din("w_oa", [512, D])
    w_ob = din("w_ob", [512, D])
    w_out = din("w_out", [D, D])
    ln_g = din("ln_g", [1, D])
    ln_b = din("ln_b", [1, D])
    rope_tab = din("rope_tab", [128, 4, NK])
    ident_d = din("ident", [128, 128])
    out = nc.dram_tensor("out", [T, D], F32, kind="ExternalOutput").ap()

    skind = "ExternalOutput" if debug else "Internal"

    def scr(name, shape, dt):
        return nc.dram_tensor(name, shape, dt, kind=skind).ap()

    qaT = scr("qaT", [8, 96, T], BF16)
    kaT = scr("kaT", [8, 96, NK], BF16)
    vaS = scr("vaS", [8, 128, NKB * 65], BF16)
    qdT = scr("qdT", [4, 128, T], BF16)
    kdT = scr("kdT", [4, 128, NK], BF16)
    vdS = scr("vdS", [4, 128, NKB * 128], BF16)
    oaT = scr("oaT", [512, T], F32)
    saS = scr("saS", [8, T], F32)
    od1T = scr("od1T", [512, T], F32)
    od2T = scr("od2T", [512, T], F32)
    sdS = scr("sdS", [8, T], F32)
    if debug:
        dbg_mod = nc.dram_tensor("dbg_mod", [128, 16, 2], F32, kind="ExternalOutput").ap()
        dbg_gate = nc.dram_tensor("dbg_gate", [128, D], F32, kind="ExternalOutput").ap()
        dbg_misc = nc.dram_tensor("dbg_misc", [2, 128, 1], F32, kind="ExternalOutput").ap()

    with ExitStack() as es:
        S = Sched(nc, es)

        def sb(stack, name, shape, dt):
            return stack.enter_context(nc.sbuf_tensor("sb_" + name, shape, dt))

        P = es.enter_context(nc.psum_tensor("P", [128, 4096], F32))
        ident = sb(es, "ident", [128, 128], F32)
        ones_f = sb(es, "ones_f", [128, 128], F32)
        ones_b = sb(es, "ones_b", [128, 128], BF16)
        modT = sb(es, "modT", [128, 16, 2], F32)
        gate_bc = sb(es, "gate_bc", [128, D], F32)
        lng_bc = sb(es, "lng_bc", [128, D], F32)
        lnb_bc = sb(es, "lnb_bc", [128, D], F32)
        neglam = sb(es, "neglam", [128, 1], F32)
        sublnS = sb(es, "sublnS", [128, 1], F32)
        qn_t = sb(es, "qn_t", [128, 3], F32)
        kvn_t = sb(es, "kvn_t", [128, 2], F32)

        aw_stack = ExitStack()
        es.enter_context(aw_stack)
        do_A = "A" in phases
        if do_A:
            WA = sb(aw_stack, "WA", [128, 8, NA], BF16)
            WUQ = sb(aw_stack, "WUQ", [128, 3, 1536], BF16)
            WUKV = sb(aw_stack, "WUKV", [128, 2, 1024], BF16)
            awstg = [sb(aw_stack, f"awstg{i}", [128, 2048], F32) for i in range(2)]
        with ExitStack() as ph:
            wm = sb(ph, "wm", [128, 8, 3072], F32)
            cv = sb(ph, "cv", [128, 8, 2], F32)
            sg0 = sb(ph, "sg0", [128, 8, 2], F32)
            sc = sb(ph, "sc", [128, 8, 2], F32)
            screp = sb(ph, "screp", [128, 8, 128], F32)
            bmT = sb(ph, "bmT", [128, 24], F32)
            bg = sb(ph, "bg", [128, D], F32)
            lamt = sb(ph, "lamt", [128, 256], F32)
            lprod = sb(ph, "lprod", [128, 128], F32)
            lsum = sb(ph, "lsum", [128, 4], F32)
            subt = sb(ph, "subt", [128, 1], F32)

            b_wm, b_cv, b_sc, b_screp, b_bmT, b_bg, b_lamt, b_lprod, b_lsum, b_subt, b_sg0 = S.bufs_n(11, "p0")
            b_ident, b_onesf, b_onesb, b_modT, b_gate, b_lng, b_lnb, b_neglam, b_subln, b_qn, b_kvn = S.bufs_n(11, "c0")
            b_ps0, b_ps1, b_ps2 = S.pbanks(3, "ps")

            S.dma(ident[:], ident_d[:, :], (), [b_ident], S.chan())
            S.dma(cv[:], cvec[:, :, :], (), [b_cv], S.chan())
            S.dma(bmT[:], bmodT[:, :], (), [b_bmT], S.chan())
            S.dma(wm[:], w_mod.rearrange("(k p) n -> p k n", p=128), (), [b_wm], S.chan())
            S.dma(bg[:], bgate[0:1, :].broadcast_to([128, D]), (), [b_bg], S.chan())
            S.dma(lng_bc[:], ln_g[0:1, :].broadcast_to([128, D]), (), [b_lng], S.chan())
            S.dma(lnb_bc[:], ln_b[0:1, :].broadcast_to([128, D]), (), [b_lnb], S.chan())
            S.dma(lamt[:], lamb[0:1, :].broadcast_to([128, 256]), (), [b_lamt], S.chan())
            S.dma(subt[:], sublnT[:, :], (), [b_subt], S.chan())
            S.dma(qn_t[:], qnT[:, :], (), [b_qn], S.chan())
            S.dma(kvn_t[:], kvnT[:, :], (), [b_kvn], S.chan())
            S.memset("dve", ones_f[:], 1.0, [b_onesf])
            S.memset("pool", ones_b[:], 1.0, [b_onesb])
            if do_A:
                b_aw = S.bufs_n(2, "aw")
                b_awstg = S.bufs_n(2, "awstg")
                ch_awstg = [S.chan() for _ in range(2)]
                load_weight_bf16(S, WA, b_aw, wA, 8, NA, awstg, b_awstg, ch_awstg)
                load_weight_bf16(S, WUQ, b_aw, wuq2, 3, 1536, awstg, b_awstg, ch_awstg, rowscale=qn_t, rowscale_buf=b_qn)
                load_weight_bf16(S, WUKV, b_aw, wukv, 2, 1024, awstg, b_awstg, ch_awstg, rowscale=kvn_t,
                                 rowscale_buf=b_kvn)

            S.act(sg0[:], cv[:], AF.Sigmoid, [b_cv], [b_sg0])
            S.tt("dve", sc[:], cv[:], sg0[:], ALU.mult, [b_cv, b_sg0], [b_sc])
            for k in range(8):
                S.ts("dve", screp[:, k, :], ones_f[:, :], sc[:, k, 0:1], None, ALU.mult, None,
                     [b_onesf, b_sc], [b_screp])
            for nck in range(16):
                for k in range(8):
                    S.mm(P[:, nck * 2:nck * 2 + 2], wm[:, k, nck * 128:(nck + 1) * 128], sc[:, k, :],
                         k == 0, k == 7, [b_wm, b_sc], [b_ps0])
            for wch in range(2):
                S.tt("dve", modT[:, :, wch], P[:, wch:32:2], bmT[:, 0:16], ALU.add, [b_ps0, b_bmT], [b_modT])
            S.ts("dve", modT[:, 8:16, :], modT[:, 8:16, :], 1.0, None, ALU.add, None, [b_modT], [b_modT])
            for half in range(2):
                bps = (b_ps1, b_ps2)[half]
                for k in range(8):
                    S.mm(P[:, 512 * (half + 1):512 * (half + 2)], screp[:, k, :],
                         wm[:, k, 2048 + half * 512:2048 + (half + 1) * 512], k == 0, k == 7,
                         [b_screp, b_wm], [bps])
                S.tt("dve", gate_bc[:, half * 512:(half + 1) * 512], P[:, 512 * (half + 1):512 * (half + 2)],
                     bg[:, half * 512:(half + 1) * 512], ALU.add, [bps, b_bg], [b_gate])
            S.tt("dve", lprod[:, 0:64], lamt[:, 0:64], lamt[:, 64:128], ALU.mult, [b_lamt], [b_lprod])
            S.tt("dve", lprod[:, 64:128], lamt[:, 128:192], lamt[:, 192:256], ALU.mult, [b_lamt], [b_lprod])
            S.op("dve", lambda h: h.reduce_sum(lsum[:, 0:1], lprod[:, 0:64], AX.X), [b_lprod], [b_lsum])
            S.op("dve", lambda h: h.reduce_sum(lsum[:, 1:2], lprod[:, 64:128], AX.X), [b_lprod], [b_lsum])
            S.act(lsum[:, 2:4], lsum[:, 0:2], AF.Exp, [b_lsum], [b_lsum])
            S.tt("dve", neglam[:], lsum[:, 3:4], lsum[:, 2:3], ALU.subtract, [b_lsum], [b_neglam])
            S.ts("dve", neglam[:], neglam[:], -LAM_INIT, None, ALU.add, None, [b_neglam], [b_neglam])
            S.ts("dve", sublnS[:], subt[:], 1.0 - LAM_INIT, None, ALU.mult, None, [b_subt], [b_subln])
            if debug:
                S.dma(dbg_mod[:, :, :], modT[:], [b_modT], (), S.chan())
                S.dma(dbg_gate[:, :], gate_bc[:], [b_gate], (), S.chan())
                S.dma(dbg_misc[0, :, :], neglam[:], [b_neglam], (), S.chan())
                S.dma(dbg_misc[1, :, :], sublnS[:], [b_subln], (), S.chan())
            S.flush()

        cw_stack = ExitStack()
        WG = WOA = WOB = WOUT = None
        if "A" in phases:
            phase_A(nc, S, sb, P, locals())
        aw_stack.close()
        es.enter_context(cw_stack)
        WG = sb(cw_stack, "WG", [128, 8, 3072], BF16)
        WOA = sb(cw_stack, "WOA", [128, 4, D], BF16)
        WOB = sb(cw_stack, "WOB", [128, 4, D], BF16)
        WOUT = sb(cw_stack, "WOUT", [128, 8, D], BF16)
        cw_loaded = [False]
        if "B" in phases:
            phase_B(nc, S, sb, P, locals())
        if "C" in phases:
            phase_C(nc, S, sb, P, locals())
    return nc


def load_weight_bf16(*a, **kw):
    for _ in load_weight_bf16_gen(*a, **kw):
        pass


def load_weight_bf16_gen(S, dst, dst_bufs, src, nrow_chunks, ncols, stg, stg_bufs, chans, engines=("dve", "act"),
                         rowscale=None, rowscale_buf=None, piece=2048, cnt=[0], colscale=None, colscale_buf=None):
    for k in range(nrow_chunks):
        c0 = 0
        while c0 < ncols:
            cw = min(piece, ncols - c0)
            i = cnt[0] % len(stg)
            cnt[0] += 1
            S.dma(stg[i][:, 0:cw], src[k * 128:(k + 1) * 128, c0:c0 + cw], (), [stg_bufs[i]], chans[i])
            ei = cnt[0] % len(engines)
            eng = engines[ei]
            dst_buf = dst_bufs[ei]
            if colscale is not None:
                S.tt(eng, dst[:, k, c0:c0 + cw], stg[i][:, 0:cw], colscale[:, c0:c0 + cw], ALU.mult,
                     [stg_bufs[i], colscale_buf], [dst_buf])
            elif rowscale is None:
                S.cp(eng, dst[:, k, c0:c0 + cw], stg[i][:, 0:cw], [stg_bufs[i]], [dst_buf])
            elif eng == "act":
                S.act(dst[:, k, c0:c0 + cw], stg[i][:, 0:cw], AF.Copy, [stg_bufs[i], rowscale_buf], [dst_buf],
                      scale=rowscale[:, k:k + 1])
            else:
                S.ts(eng, dst[:, k, c0:c0 + cw], stg[i][:, 0:cw], rowscale[:, k:k + 1], None, ALU.mult, None,
                     [stg_bufs[i], rowscale_buf], [dst_buf])
            c0 += cw
            yield


def phase_A(nc, S, sb, P, env):
    ident, ones_b, modT, qn_t, kvn_t = env["ident"], env["ones_b"], env["modT"], env["qn_t"], env["kvn_t"]
    xk, wA, wuq2, wukv, rope_tab = env["xk"], env["wA"], env["wuq2"], env["wukv"], env["rope_tab"]
    qaT, kaT, vaS, qdT, kdT, vdS = env["qaT"], env["kaT"], env["vaS"], env["qdT"], env["kdT"], env["vdS"]
    NCH = DBG.get('nchA', NK // W)
    with ExitStack() as ph:
        WA, WUQ, WUKV = env["WA"], env["WUQ"], env["WUKV"]
        xin = [sb(ph, f"xin{i}", [128, 2, D], F32) for i in range(2)]
        tab = [sb(ph, f"tab{i}", [128, 4, W], F32) for i in range(2)]
        hT = [sb(ph, f"hT{i}", [128, 8, W], BF16) for i in range(2)]
        cq = sb(ph, "cq", [128, 5, W], BF16)
        sq = sb(ph, "sq", [128, 5, W], BF16)
        NTMP = 8
        tmp = [sb(ph, f"tmp{i}", [128, W], F32) for i in range(NTMP)]
        rq = sb(ph, "rq", [128, W], F32)
        rkv = sb(ph, "rkv", [128, W], F32)
        csr = sb(ph, "csr", [128, W], F32)
        snr = sb(ph, "snr", [128, W], F32)
        rkvt = sb(ph, "rkvt", [128, 2], F32)
        QDs = [sb(ph, f"QDs{i}", [128, 4, W], BF16) for i in range(2)]
        KDs = [sb(ph, f"KDs{i}", [128, 4, W], BF16) for i in range(2)]
        VDs = [sb(ph, f"VDs{i}", [128, 4, 2, 128], BF16) for i in range(2)]
        QAs = [sb(ph, f"QAs{i}", [128, 8, W], BF16) for i in range(2)]
        KAs = [sb(ph, f"KAs{i}", [128, 8, W], BF16) for i in range(2)]
        VAs = [sb(ph, f"VAs{i}", [128, 8, 2, 65], BF16) for i in range(2)]

        b_ident, b_onesb, b_modT, b_qn, b_kvn = S.bufs_n(5, "cA")
        b_WA, b_WUQ, b_WUKV = (S.bufs_n(2, n) for n in ("wA", "wUQ", "wUKV"))
        b_xin = S.bufs_n(2, "xin")
        b_tab = S.bufs_n(2, "tab")
        b_hT = [S.bufs_n(8, f"hT{i}_") for i in range(2)]
        b_cq = S.bufs_n(5, "cq")
        b_sq = S.bufs_n(5, "sq")
        b_tmp = S.bufs_n(NTMP, "tmp")
        b_rq, b_rkv, b_csr, b_snr, b_rkvt = S.bufs_n(5, "r")
        b_QDs = [S.bufs_n(4, f"QDs{i}_") for i in range(2)]
        b_KDs = [S.bufs_n(4, f"KDs{i}_") for i in range(2)]
        b_VDs = [S.bufs_n(2, f"VDs{i}_") for i in range(2)]
        b_QAs = [S.bufs_n(8, f"QAs{i}_") for i in range(2)]
        b_KAs = [S.bufs_n(8, f"KAs{i}_") for i in range(2)]
        b_KAr = [S.bufs_n(8, f"KAr{i}_") for i in range(2)]
        b_VAs = [S.bufs_n(2, f"VAs{i}_") for i in range(2)]
        b_VAones = S.bufs_n(2, "VAones")
        pbk = S.pbanks(8, "pbk")
        ch_x = [S.chan() for _ in range(2)]
        ch_tab = [S.chan() for _ in range(2)]
        ch_st = {n: [S.chan() for _ in range(2)] for n in ("qd", "kd", "vd", "qa", "ka", "va")}

        for i in range(2):
            S.memset("pool", VAs[i][:, :, :, 64:65], 1.0, [b_VAones[i]])

        bank_rr = [0]

        def get_bank():
            f = 4 + bank_rr[0] % 4
            bank_rr[0] += 1
            return f

        def Pb(f, half, m=128):
            return P[0:m, f * 512 + half * 256:f * 512 + (half + 1) * 256]

        tmp_rr = [0]

        def get_tmp():
            i = tmp_rr[0] % NTMP
            tmp_rr[0] += 1
            return i

        def load(ci):
            s = ci % 2
            S.dma(xin[s][:], xk[ci * W:(ci + 1) * W, :].rearrange("(j p) d -> p j d", p=128), (), [b_xin[s]], ch_x[s])

        def load_tab(ci):
            s = ci % 2
            S.dma(tab[s][:], rope_tab[:, :, ci * W:(ci + 1) * W], (), [b_tab[s]], ch_tab[s])

        def transposes(ci):
            s = ci % 2
            wch = 1 if ci == 0 else 0
            for dk in range(8):
                for j in range(2):
                    S.tr(P[:, dk * 256 + j * 128:dk * 256 + (j + 1) * 128], xin[s][:, j, dk * 128:(dk + 1) * 128],
                         ident[:], [b_xin[s], b_ident], [pbk[dk // 2]])
            for dk in range(8):
                S.act(hT[s][:, dk, :], Pb(dk // 2, dk % 2), AF.Identity, [pbk[dk // 2], b_modT], [b_hT[s][dk]],
                      bias=modT[:, dk, wch:wch + 1], scale=modT[:, 8 + dk, wch:wch + 1])

        def proj_job(s, tiles):
            f = get_bank()
            for idx, (c0, m) in enumerate(tiles):
                for k in range(8):
                    S.mm(Pb(f, idx, m), WA[:, k, c0:c0 + m], hT[s][:, k, :], k == 0, k == 7,
                         [b_hT[s][k], b_WA], [pbk[f]])
            return f

        def rope_from_bank(f, cs, sn, m, tab_bufs):
            t1 = get_tmp()
            S.tt("dve", tmp[t1][0:m, :], Pb(f, 0, m), cs, ALU.mult, [pbk[f]] + tab_bufs, [b_tmp[t1]])
            t2 = get_tmp()
            S.tt("dve", tmp[t2][0:m, :], Pb(f, 1, m), sn, ALU.mult, [pbk[f]] + tab_bufs, [b_tmp[t2]])
            return t1, t2

        def stage1(ci):
            s = ci % 2
            latent = ci > 0
            for grp in ((0, 1), (2, 3), (4,)):
                f = proj_job(s, [(m * 128, 128) for m in grp])
                for idx, m in enumerate(grp):
                    S.cp("dve", cq[:, m, :], Pb(f, idx), [pbk[f]], [b_cq[m]])
                    S.act(sq[:, m, :], Pb(f, idx), AF.Square, [pbk[f]], [b_sq[m]])
            f = proj_job(s, [(640, 96), (736, 96)])
            t1, t2 = rope_from_bank(f, tab[s][0:96, 2, :], tab[s][0:96, 3, :], 96, [b_tab[s]])
            for h in range(8):
                S.tt("pool", KAs[s][64:96, h, :], tmp[t1][64:96, :], tmp[t2][64:96, :], ALU.add,
                     [b_tmp[t1], b_tmp[t2]], [b_KAr[s][h]])
            for (c_n, c_s, dst, dbuf, need) in ((832, 1344, QDs, b_QDs, latent), (1856, 2368, KDs, b_KDs, True)):
                if not need:
                    continue
                for hh in range(4):
                    f = proj_job(s, [(c_n + hh * 128, 128), (c_s + hh * 128, 128)])
                    t1, t2 = rope_from_bank(f, tab[s][:, 0, :], tab[s][:, 1, :], 128, [b_tab[s]])
                    S.tt("pool", dst[s][:, hh, :], tmp[t1][:, :], tmp[t2][:, :], ALU.add,
                         [b_tmp[t1], b_tmp[t2]], [dbuf[s][hh]])
            for j in range(2):
                f = get_bank()
                for k in range(8):
                    S.mm(P[:, f * 512:(f + 1) * 512], hT[s][:, k, j * 128:(j + 1) * 128], WA[:, k, 2880:3392],
                         k == 0, k == 7, [b_hT[s][k], b_WA], [pbk[f]])
                S.cp("act", VDs[s][:, :, j, :], P[:, f * 512:(f + 1) * 512].rearrange("p (h d) -> p h d", h=4),
                     [pbk[f]], [b_VDs[s][j]])

        def stage2(ci):
            s = ci % 2
            latent = ci > 0
            f = get_bank()
            for c in range(3):
                S.mm(Pb(f, 0, 96), ones_b[:, 0:96], sq[:, c, :], c == 0, c == 2, [b_onesb, b_sq[c]], [pbk[f]])
            for c in range(2):
                S.mm(Pb(f, 1, 64), ones_b[:, 0:64], sq[:, 3 + c, :], c == 0, c == 1, [b_onesb, b_sq[3 + c]], [pbk[f]])
            S.act(rq[0:96, :], Pb(f, 0, 96), AF.Sqrt, [pbk[f]], [b_rq], bias=RMS_EPS, scale=1.0 / 384)
            S.act(rkv[0:64, :], Pb(f, 1, 64), AF.Sqrt, [pbk[f]], [b_rkv], bias=RMS_EPS, scale=1.0 / 256)
            S.recip(rq[0:96, :], rq[0:96, :], [b_rq], [b_rq])
            S.recip(rkv[0:64, :], rkv[0:64, :], [b_rkv], [b_rkv])
            f = get_bank()
            for j in range(2):
                for c in range(2):
                    S.mm(P[:, f * 512 + j:f * 512 + j + 1], sq[:, 3 + c, j * 128:(j + 1) * 128], ones_b[:, 0:1],
                         c == 0, c == 1, [b_onesb, b_sq[3 + c]], [pbk[f]])
            S.act(rkvt[:, :], P[:, f * 512:f * 512 + 2], AF.Sqrt, [pbk[f]], [b_rkvt], bias=RMS_EPS, scale=1.0 / 256)
            S.recip(rkvt[:, :], rkvt[:, :], [b_rkvt], [b_rkvt])
            if latent:
                S.tt("pool", csr[0:96, :], tab[s][0:96, 2, :], rq[0:96, :], ALU.mult, [b_tab[s], b_rq], [b_csr])
                S.tt("pool", snr[0:96, :], tab[s][0:96, 3, :], rq[0:96, :], ALU.mult, [b_tab[s], b_rq], [b_snr])
                for h in range(8):
                    f = get_bank()
                    for idx in range(2):
                        for c in range(3):
                            S.mm(Pb(f, idx, 96), WUQ[:, c, idx * 768 + h * 96:idx * 768 + (h + 1) * 96], cq[:, c, :],
                                 c == 0, c == 2, [b_WUQ, b_cq[c]], [pbk[f]])
                    t1 = get_tmp()
                    S.tt("dve", tmp[t1][0:96, :], Pb(f, 0, 96), csr[0:96, :], ALU.mult, [pbk[f], b_csr], [b_tmp[t1]])
                    t2 = get_tmp()
                    S.tt("dve", tmp[t2][0:96, :], Pb(f, 1, 96), snr[0:96, :], ALU.mult, [pbk[f], b_snr], [b_tmp[t2]])
                    S.tt("pool", QAs[s][0:96, h, :], tmp[t1][0:96, :], tmp[t2][0:96, :], ALU.add,
                         [b_tmp[t1], b_tmp[t2]], [b_QAs[s][h]])
            for hp in range(4):
                f = get_bank()
                for idx in range(2):
                    h = hp * 2 + idx
                    for c in range(2):
                        S.mm(Pb(f, idx, 64), WUKV[:, c, h * 64:(h + 1) * 64], cq[:, 3 + c, :], c == 0, c == 1,
                             [b_WUKV, b_cq[3 + c]], [pbk[f]])
                for idx in range(2):
                    h = hp * 2 + idx
                    S.tt("dve", KAs[s][0:64, h, :], Pb(f, idx, 64), rkv[0:64, :], ALU.mult, [pbk[f], b_rkv],
                         [b_KAs[s][h]])
            for j in range(2):
                f = get_bank()
                for c in range(2):
                    S.mm(P[:, f * 512:(f + 1) * 512], cq[:, 3 + c, j * 128:(j + 1) * 128], WUKV[:, c, 512:1024],
                         c == 0, c == 1, [b_cq[3 + c], b_WUKV], [pbk[f]])
                S.act(VAs[s][:, :, j, 0:64], P[:, f * 512:(f + 1) * 512].rearrange("p (h d) -> p h d", h=8), AF.Copy,
                      [pbk[f], b_rkvt], [b_VAs[s][j]], scale=rkvt[:, j:j + 1])

        def stores(ci):
            s = ci % 2
            k0 = ci * W
            if ci > 0:
                t0 = k0 - CTX
                S.dma(qdT[:, :, t0:t0 + W].rearrange("h p w -> p h w"), QDs[s][:], b_QDs[s], (), ch_st["qd"][s])
                S.dma(qaT[:, :, t0:t0 + W].rearrange("h p w -> p h w"), QAs[s][0:96, :, :], b_QAs[s], (), ch_st["qa"][s])
            S.dma(kdT[:, :, k0:k0 + W].rearrange("h p w -> p h w"), KDs[s][:], b_KDs[s], (), ch_st["kd"][s])
            S.dma(kaT[:, :, k0:k0 + W].rearrange("h p w -> p h w"), KAs[s][0:96, :, :], b_KAs[s] + b_KAr[s], (),
                  ch_st["ka"][s])
            kb0 = k0 // 128
            S.dma(vdS[:, :, kb0 * 128:(kb0 + 2) * 128].rearrange("h p w -> p h w"),
                  VDs[s][:].rearrange("p h j d -> p h (j d)"), b_VDs[s], (), ch_st["vd"][s])
            S.dma(vaS[:, :, kb0 * 65:(kb0 + 2) * 65].rearrange("h p w -> p h w"),
                  VAs[s][:].rearrange("p h j d -> p h (j d)"), b_VAs[s] + [b_VAones[s]], (), ch_st["va"][s])

        load(0)
        load_tab(0)
        if NCH > 1:
            load(1)
            load_tab(1)
        transposes(0)
        for ci in range(NCH):
            if ci + 2 < NCH:
                load(ci + 2)
            stage1(ci)
            if ci + 1 < NCH:
                transposes(ci + 1)
            stage2(ci)
            stores(ci)
            if ci + 2 < NCH:
                load_tab(ci + 2)
        S.flush()


def phase_B(nc, S, sb, P, env):
    ones_f = env["ones_f"]
    qaT, kaT, vaS, qdT, kdT, vdS = env["qaT"], env["kaT"], env["vaS"], env["qdT"], env["kdT"], env["vdS"]
    oaT, saS, od1T, od2T, sdS = env["oaT"], env["saS"], env["od1T"], env["od2T"], env["sdS"]
    NQC = T // 512

    with ExitStack() as ph:
        G = 3
        NG = NKB // G
        KA = [sb(ph, f"KA{i}", [128, NK], BF16) for i in range(2)]
        VA = [sb(ph, f"VA{i}", [128, NKB, 65], BF16) for i in range(2)]
        QC = [sb(ph, f"QC{i}", [128, 512], BF16) for i in range(3)]
        PT = [sb(ph, f"PT{i}", [128, G * 512], BF16) for i in range(3)]
        OS = [sb(ph, f"OS{i}", [128, 512], F32) for i in range(2)]
        wstg = [sb(ph, f"bwstg{i}", [128, 1024], F32) for i in range(2)]
        b_wstg = S.bufs_n(2, "bwstg")
        ch_wstg = [S.chan() for _ in range(2)]
        b_cw = S.bufs_n(1, "cw")

        b_gate_c = S.buf("gate_c")

        def _cw_gen():
            for (dst, src, nk, ncol) in ((env["WG"], env["wG"], 8, 3072), (env["WOA"], env["w_oa"], 4, D),
                                          (env["WOB"], env["w_ob"], 4, D)):
                yield from load_weight_bf16_gen(S, dst, b_cw, src, nk, ncol, wstg, b_wstg, ch_wstg,
                                                engines=("dve",), piece=1024)
            yield from load_weight_bf16_gen(S, env["WOUT"], b_cw, env["w_out"], 8, D, wstg, b_wstg, ch_wstg,
                                            engines=("dve",), piece=1024, colscale=env["gate_bc"],
                                            colscale_buf=b_gate_c)
            env["cw_loaded"][0] = True
        cw_gen = _cw_gen()
        b_KA = S.bufs_n(2, "KA")
        b_VA = S.bufs_n(2, "VA")
        b_QC = S.bufs_n(3, "QC")
        b_PT = S.bufs_n(3, "PT")
        b_OS = S.bufs_n(2, "OS")
        b_pS = [S.pbanks(G, f"pS{i}_") for i in range(2)]
        b_pO = S.pbanks(2, "pO")
        ch_KA = [S.chan() for _ in range(2)]
        ch_VA = [S.chan() for _ in range(2)]
        ch_QC = [S.chan() for _ in range(3)]
        ch_O = [S.chan() for _ in range(2)]
        ch_Os = [S.chan() for _ in range(2)]
        POs = [P[0:65, 6 * 512:7 * 512], P[0:65, 7 * 512:8 * 512]]

        def load_kv(h):
            s = h % 2
            S.dma(KA[s][0:96, :], kaT[h, :, :], (), [b_KA[s]], ch_KA[s])
            S.dma(VA[s][:].rearrange("p k d -> p (k d)"), vaS[h, :, :], (), [b_VA[s]], ch_VA[s])

        chunks = [(h, qc) for h in range(DBG.get('nhB1', 8)) for qc in range(DBG.get('nqcB', NQC))]

        def load_q(m):
            h, qc = chunks[m]
            s = m % 3
            S.dma(QC[s][0:96, :], qaT[h, :, qc * 512:(qc + 1) * 512], (), [b_QC[s]], ch_QC[s])

        steps = [(m, g) for m in range(len(chunks)) for g in range(NG)]

        def rec_S(n):
            m, g = steps[n]
            h, qc = chunks[m]
            if g == 1 and qc == 0 and h + 1 < DBG.get('nhB1', 8):
                load_kv(h + 1)
            if g == 0:
                if m + 2 < len(chunks):
                    load_q(m + 2)
            if g == 2:
                next(cw_gen, None)
            sbk = n % 2
            for i in range(G):
                kb = g * G + i
                col = (sbk * G + i) * 512
                S.mm(P[:, col:col + 512], KA[h % 2][0:96, kb * 128:(kb + 1) * 128], QC[m % 3][0:96, :], True, True,
                     [b_KA[h % 2], b_QC[m % 3]], [b_pS[sbk][i]])

        def rec_rest(n):
            m, g = steps[n]
            h, qc = chunks[m]
            sbk = n % 2
            ps = n % 3
            col = sbk * G * 512
            S.act(PT[ps][:, :], P[:, col:col + G * 512], AF.Exp, b_pS[sbk], [b_PT[ps]], scale=MLA_SCALE)
            for i in range(G):
                kb = g * G + i
                S.mm(POs[m % 2], VA[h % 2][:, kb, :], PT[ps][:, i * 512:(i + 1) * 512], g == 0 and i == 0,
                     g == NG - 1 and i == G - 1, [b_VA[h % 2], b_PT[ps]], [b_pO[m % 2]])
            if g == NG - 1:
                o = m % 2
                S.cp("dve", OS[o][0:65, :], POs[o], [b_pO[o]], [b_OS[o]])
                S.recip(OS[o][64:65, :], OS[o][64:65, :], [b_OS[o]], [b_OS[o]])
                S.dma(oaT[h * 64:(h + 1) * 64, qc * 512:(qc + 1) * 512], OS[o][0:64, :], [b_OS[o]], (), ch_O[o])
                S.dma(saS[h:h + 1, qc * 512:(qc + 1) * 512], OS[o][64:65, :], [b_OS[o]], (), ch_Os[o])

        if steps:
            load_kv(0)
            load_q(0)
            if len(chunks) > 1:
                load_q(1)
            rec_S(0)
        for n in range(len(steps)):
            if n + 1 < len(steps):
                rec_S(n + 1)
            rec_rest(n)
        for _ in cw_gen:
            pass
        S.flush()

    with ExitStack() as ph:
        KD = [sb(ph, f"KD{i}", [128, NK], BF16) for i in range(2)]
        VD = [sb(ph, f"VD{i}", [128, NKB, 128], BF16) for i in range(2)]
        QC = [sb(ph, f"QD{i}", [128, 512], BF16) for i in range(3)]
        PT = [sb(ph, f"PD{i}", [128, 1024], BF16) for i in range(3)]
        ACC = [[sb(ph, f"ACC{i}{j}", [128, 1024], F32) for j in range(2)] for i in range(2)]
        OS = [[sb(ph, f"OD{i}{j}", [128, 512], F32) for j in range(2)] for i in range(2)]
        SS = [sb(ph, f"SS{i}", [1, 1024], F32) for i in range(2)]
        b_KD = S.bufs_n(2, "KD")
        b_VD = S.bufs_n(2, "VD")
        b_QC = S.bufs_n(3, "QD")
        b_PT = S.bufs_n(3, "PD")
        b_ACC = [S.bufs_n(2, f"ACC{i}_") for i in range(2)]
        b_OS = [S.bufs_n(2, f"OD{i}_") for i in range(2)]
        b_SS = S.bufs_n(2, "SS")
        b_pS = [S.pbanks(2, f"pSd{i}_") for i in range(2)]
        b_pO = S.pbanks(2, "pOd")
        b_pSum = S.pbanks(2, "pSum")
        b_onesf = S.buf("onesf")
        ch_KD = [S.chan() for _ in range(2)]
        ch_VD = [S.chan() for _ in range(2)]
        ch_QC = [S.chan() for _ in range(3)]
        ch_O = [[S.chan() for _ in range(2)] for _ in range(2)]
        ch_Ss = [[S.chan() for _ in range(2)] for _ in range(2)]
        PO = [P[:, 4 * 512:5 * 512], P[:, 5 * 512:6 * 512]]
        PSUMS = [P[0:1, 6 * 512:7 * 512], P[0:1, 7 * 512:8 * 512]]

        def load_kv(h):
            s = h % 2
            S.dma(KD[s][:, :], kdT[h, :, :], (), [b_KD[s]], ch_KD[s])
            S.dma(VD[s][:].rearrange("p k d -> p (k d)"), vdS[h, :, :], (), [b_VD[s]], ch_VD[s])

        chunks = [(h, qc) for h in range(DBG.get('nhB2', 4)) for qc in range(DBG.get('nqcB', NQC))]

        def load_q(m):
            h, qc = chunks[m]
            s = m % 3
            S.dma(QC[s][:, :], qdT[h, :, qc * 512:(qc + 1) * 512], (), [b_QC[s]], ch_QC[s])

        steps = [(m, kb) for m in range(len(chunks)) for kb in range(NKB)]

        def rec_S(n):
            m, kb = steps[n]
            h, qc = chunks[m]
            if kb == 1 and qc == 0 and h + 1 < DBG.get('nhB2', 4):
                load_kv(h + 1)
            if kb == 0:
                if m + 2 < len(chunks):
                    load_q(m + 2)
            sbk = n % 2
            for i in range(2):
                col = (sbk * 2 + i) * 512
                S.mm(P[:, col:col + 512], KD[h % 2][i * 64:(i + 1) * 64, kb * 128:(kb + 1) * 128],
                     QC[m % 3][i * 64:(i + 1) * 64, :], True, True, [b_KD[h % 2], b_QC[m % 3]], [b_pS[sbk][i]])

        def rec_rest(n):
            m, kb = steps[n]
            h, qc = chunks[m]
            sbk = n % 2
            ps = n % 3
            col = sbk * 1024
            a = m % 2
            par = kb % 2
            S.act(PT[ps][:, :], P[:, col:col + 1024], AF.Exp, b_pS[sbk], [b_PT[ps]], scale=DIFF_SCALE)
            for i in range(2):
                S.mm(PO[i], VD[h % 2][:, kb, :], PT[ps][:, i * 512:(i + 1) * 512], kb == 0, kb == NKB - 1,
                     [b_VD[h % 2], b_PT[ps]], [b_pO[i]])
            eng, acc_i, first = "dve", kb % 2, kb < 2
            if first:
                S.cp(eng, ACC[a][acc_i][:, :], PT[ps][:, :], [b_PT[ps]], [b_ACC[a][acc_i]])
            else:
                S.tt(eng, ACC[a][acc_i][:, :], ACC[a][acc_i][:, :], PT[ps][:, :], ALU.add,
                     [b_PT[ps], b_ACC[a][acc_i]], [b_ACC[a][acc_i]])
            if kb == NKB - 1:
                for i in range(2):
                    S.cp("dve", OS[a][i][:, :], PO[i], [b_pO[i]], [b_OS[a][i]])
                    dst = (od1T, od2T)[i]
                    S.dma(dst[h * 128:(h + 1) * 128, qc * 512:(qc + 1) * 512], OS[a][i][:, :], [b_OS[a][i]], (),
                          ch_O[a][i])
                for i in range(2):
                    for p2 in range(2):
                        S.mm(PSUMS[i], ones_f[:, 0:1], ACC[a][p2][:, i * 512:(i + 1) * 512], p2 == 0, p2 == 1,
                             [b_onesf, b_ACC[a][p2]], [b_pSum[i]])
                    S.cp("dve", SS[a][0:1, i * 512:(i + 1) * 512], PSUMS[i], [b_pSum[i]], [b_SS[a]])
                for i in range(2):
                    S.dma(sdS[2 * h + i:2 * h + i + 1, qc * 512:(qc + 1) * 512], SS[a][0:1, i * 512:(i + 1) * 512],
                          [b_SS[a]], (), ch_Ss[a][i])

        if steps:
            load_kv(0)
            load_q(0)
            if len(chunks) > 1:
                load_q(1)
            rec_S(0)
        for n in range(len(steps)):
            if n + 1 < len(steps):
                rec_S(n + 1)
            rec_rest(n)
        S.flush()


def phase_C(nc, S, sb, P, env):
    ident, ones_b, modT = env["ident"], env["ones_b"], env["modT"]
    gate_bc, lng_bc, lnb_bc, neglam, sublnS = env["gate_bc"], env["lng_bc"], env["lnb_bc"], env["neglam"], env["sublnS"]
    xk, wG, w_oa, w_ob, w_out, out = env["xk"], env["wG"], env["w_oa"], env["w_ob"], env["w_out"], env["out"]
    oaT, saS, od1T, od2T, sdS = env["oaT"], env["saS"], env["od1T"], env["od2T"], env["sdS"]
    NCH = DBG.get('nchC', T // W)
    NX = 3
    with ExitStack() as ph:
        WG, WOA, WOB, WOUT = env["WG"], env["WOA"], env["WOB"], env["WOUT"]
        xin = [sb(ph, f"cxin{i}", [128, 2, D], F32) for i in range(NX)]
        hT = [sb(ph, f"chT{i}", [128, 8, W], BF16) for i in range(2)]
        oa = sb(ph, "oa", [128, 4, W], F32)
        rab = sb(ph, "rab", [128, 4, W], F32)
        od1 = sb(ph, "od1", [128, 4, W], F32)
        od2 = sb(ph, "od2", [128, 4, W], F32)
        r1b = sb(ph, "r1b", [128, 4, W], F32)
        r2b = sb(ph, "r2b", [128, 4, W], F32)
        sil = sb(ph, "sil", [128, 8, W], F32)
        NTMP = 6
        tmp = [sb(ph, f"ctmp{i}", [128, W], F32) for i in range(NTMP)]
        dsq = sb(ph, "dsq", [128, 4, W], BF16)
        rn = sb(ph, "rn", [128, 4, W], F32)
        oga = [sb(ph, f"oga{i}", [128, 4, W], BF16) for i in range(2)]
        ogd = [sb(ph, f"ogd{i}", [128, 4, W], BF16) for i in range(2)]
        zT = sb(ph, "zT", [128, 8, W], BF16)
        tv = [sb(ph, f"tv{i}", [128, D], F32) for i in range(2)]
        ost = [sb(ph, f"ost{i}", [128, D], F32) for i in range(2)]
        stat = sb(ph, "stat", [128, 2, 6], F32)
        mv = sb(ph, "mv", [128, 4], F32)
        stg = tv
        eps_t = sb(ph, "eps_t", [128, 1], F32)
        b_eps = S.buf("eps")
        S.memset("pool", eps_t[:], SUBLN_EPS, [b_eps])
        rsb = ost

        b_ident, b_onesb, b_modT, b_gate, b_lng, b_lnb, b_neglam, b_subln = S.bufs_n(8, "cC")
        b_WG, b_WOA, b_WOB, b_WOUT = (S.bufs_n(2, n) for n in ("wG", "wOA", "wOB", "wOUT"))
        b_xin = S.bufs_n(NX, "cxin")
        b_hT = [S.bufs_n(8, f"chT{i}_") for i in range(2)]
        b_oa, b_rab, b_od1, b_od2, b_r1b, b_r2b = S.bufs_n(6, "o")
        b_sil = S.bufs_n(8, "sil")
        b_tmp = S.bufs_n(NTMP, "ctmp")
        b_dsq = S.buf("dsq")
        b_rn = S.bufs_n(4, "rn")
        b_oga = S.bufs_n(2, "oga")
        b_ogd = S.bufs_n(2, "ogd")
        b_zT = S.bufs_n(8, "zT")
        b_tv = S.bufs_n(2, "tv")
        b_stg = b_tv
        b_ost = S.bufs_n(2, "ost")
        b_stat, b_mv = S.bufs_n(2, "st")
        pbk = S.pbanks(8, "cpbk")
        ch_stg = [S.chan() for _ in range(2)]
        ch_x = [S.chan() for _ in range(NX)]
        ch_in = {n: S.chan() for n in ("oa", "rab0", "rab1", "od1", "od2", "r1b", "r2b")}
        ch_out = [S.chan() for _ in range(2)]

        if not env["cw_loaded"][0]:
            load_weight_bf16(S, WG, b_WG, wG, 8, 3072, stg, b_stg, ch_stg, piece=1024)
            load_weight_bf16(S, WOA, b_WOA, w_oa, 4, D, stg, b_stg, ch_stg, piece=1024)
            load_weight_bf16(S, WOB, b_WOB, w_ob, 4, D, stg, b_stg, ch_stg, piece=1024)
            load_weight_bf16(S, WOUT, b_WOUT, w_out, 8, D, stg, b_stg, ch_stg, piece=1024, engines=("dve",),
                             colscale=gate_bc, colscale_buf=b_gate)

        bank_rr = [0]
        tmp_rr = [0]

        def get_bank():
            f = 4 + bank_rr[0] % 4
            bank_rr[0] += 1
            return f

        def get_tmp():
            i = tmp_rr[0] % NTMP
            tmp_rr[0] += 1
            return i

        def Pb(f, half, m=128):
            return P[0:m, f * 512 + half * 256:f * 512 + (half + 1) * 256]

        def bc_src(t, row0, rstride, nrow, t0, nparts):
            ncol = t.shape[1]
            return bass.AP(t.tensor, row0 * ncol + t0, [[0, nparts], [rstride * ncol, nrow], [1, W]])

        def load_x(ci):
            s = ci % NX
            t0 = ci * W
            S.dma(xin[s][:], xk[CTX + t0:CTX + t0 + W, :].rearrange("(j p) d -> p j d", p=128), (), [b_xin[s]], ch_x[s])

        def load_o(ci):
            t0 = ci * W
            S.dma(oa[:], oaT[:, t0:t0 + W].rearrange("(c p) w -> p c w", p=128), (), [b_oa], ch_in["oa"])
            S.dma(od1[:], od1T[:, t0:t0 + W].rearrange("(c p) w -> p c w", p=128), (), [b_od1], ch_in["od1"])
            S.dma(od2[:], od2T[:, t0:t0 + W].rearrange("(c p) w -> p c w", p=128), (), [b_od2], ch_in["od2"])
            S.dma(rab[0:64, :, :], bc_src(saS, 0, 2, 4, t0, 64), (), [b_rab], ch_in["rab0"])
            S.dma(rab[64:128, :, :], bc_src(saS, 1, 2, 4, t0, 64), (), [b_rab], ch_in["rab1"])
            S.dma(r1b[:], bc_src(sdS, 0, 2, 4, t0, 128), [b_rd], [b_r1b], ch_in["r1b"])
            S.dma(r2b[:], bc_src(sdS, 1, 2, 4, t0, 128), [b_rd], [b_r2b], ch_in["r2b"])

        def flat(t):
            return t[:].rearrange("p c w -> p (c w)")

        def stage_X(ci, seg):
            sx = ci % NX
            s = ci % 2
            S.tag = f"X{seg}c{ci}"
            if seg == -1:
                stage_X0(ci, sx, s)
            elif seg == 0:
                stage_X1(ci, sx, s)
            elif seg == 1:
                stage_X2(ci, sx, s)
            else:
                stage_X3(ci, sx, s)

        def stage_X0(ci, sx, s):
            for dk in range(8):
                for j in range(2):
                    S.tr(P[:, dk * 256 + j * 128:dk * 256 + (j + 1) * 128], xin[sx][:, j, dk * 128:(dk + 1) * 128],
                         ident[:], [b_xin[sx], b_ident], [pbk[dk // 2]])
            for dk in range(8):
                S.act(hT[s][:, dk, :], Pb(dk // 2, dk % 2), AF.Identity, [pbk[dk // 2], b_modT], [b_hT[s][dk]],
                      bias=modT[:, dk, 0:1], scale=modT[:, 8 + dk, 0:1])
        def stage_X1(ci, sx, s):
            for mp in range(4):
                f = get_bank()
                for idx in range(2):
                    m = mp * 2 + idx
                    for k in range(8):
                        S.mm(Pb(f, idx), WG[:, k, m * 128:(m + 1) * 128], hT[s][:, k, :], k == 0, k == 7,
                             [b_WG, b_hT[s][k]], [pbk[f]])
                for idx in range(2):
                    m = mp * 2 + idx
                    t1 = get_tmp()
                    S.act(tmp[t1][:, :], Pb(f, idx), AF.Sigmoid, [pbk[f]], [b_tmp[t1]])
                    S.tt("dve", sil[:, m, :], Pb(f, idx), tmp[t1][:, :], ALU.mult, [pbk[f], b_tmp[t1]], [b_sil[m]])
        def stage_X2(ci, sx, s):
            for _ in gen_X2(ci, s):
                pass

        def gen_X2(ci, s):
            tg = f"X1c{ci}"
            S.tag = tg
            S.tt("pool", flat(oa), flat(oa), flat(rab), ALU.mult, [b_oa, b_rab], [b_oa])
            S.tt("pool", flat(od1), flat(od1), flat(r1b), ALU.mult, [b_od1, b_r1b], [b_od1])
            yield
            S.tag = tg
            S.tt("dve", flat(oga[s]), flat(oa), sil[:, 0:4, :].rearrange("p c w -> p (c w)"), ALU.mult,
                 [b_oa] + b_sil[0:4], [b_oga[s]])
            S.tt("pool", flat(od2), flat(od2), flat(r2b), ALU.mult, [b_od2, b_r2b], [b_od2])
            yield
            S.tag = tg
            S.stt(flat(od1), flat(od2), neglam[:, 0:1], flat(od1), ALU.mult, ALU.add, [b_od2, b_od1, b_neglam], [b_od1])
            yield
            S.tag = tg
            S.tt("pool", flat(dsq), flat(od1), flat(od1), ALU.mult, [b_od1], [b_dsq])
            yield

        def stage_X3(ci, sx, s):
            for cp_ in range(2):
                f = get_bank()
                for idx in range(2):
                    c = cp_ * 2 + idx
                    S.mm(Pb(f, idx), ones_b[:, :], dsq[:, c, :], True, True, [b_onesb, b_dsq], [pbk[f]])
                for idx in range(2):
                    c = cp_ * 2 + idx
                    S.act(rn[:, c, :], Pb(f, idx), AF.Ln, [pbk[f], b_eps], [b_rn[c]], bias=eps_t[:, 0:1], scale=1.0 / 128)
            S.act(flat(rn), flat(rn), AF.Exp, b_rn, b_rn, scale=-0.5)
            S.tt("pool", flat(od2), flat(od1), flat(rn), ALU.mult, [b_od1] + b_rn, [b_od2])
            S.stt(flat(ogd[s]), flat(od2), sublnS[:, 0:1], sil[:, 4:8, :].rearrange("p c w -> p (c w)"), ALU.mult,
                  ALU.mult, [b_od2, b_subln] + b_sil[4:8], [b_ogd[s]])

        def stage_Y(ci, seg, filler=None):
            sx = ci % NX
            s = ci % 2
            t0 = ci * W
            S.tag = f"Y{seg}c{ci}"
            if seg < 2:
                stage_Y1(ci, s, range(seg * 4, seg * 4 + 4), filler, f"Y{seg}c{ci}")
            else:
                stage_Y2(ci, sx, t0, seg - 2)

        def stage_Y1(ci, s, ms, filler=None, tg=None):
            for m in ms:
                if filler is not None and m != ms[0]:
                    next(filler, None)
                S.tag = tg
                fa = get_bank()
                for c in range(4):
                    S.mm(Pb(fa, 0), WOA[:, c, m * 128:(m + 1) * 128], oga[s][:, c, :], c == 0, c == 3,
                         [b_WOA, b_oga[s]], [pbk[fa]])
                for c in range(4):
                    S.mm(Pb(fa, 1), WOB[:, c, m * 128:(m + 1) * 128], ogd[s][:, c, :], c == 0, c == 3,
                         [b_WOB, b_ogd[s]], [pbk[fa]])
                fm = get_bank()
                for idx in range(2):
                    for k in range(8):
                        S.mm(Pb(fm, idx), WG[:, k, 1024 * (idx + 1) + m * 128:1024 * (idx + 1) + (m + 1) * 128],
                             hT[s][:, k, :], k == 0, k == 7, [b_WG, b_hT[s][k]], [pbk[fm]])
                t1 = get_tmp()
                S.act(tmp[t1][:, :], Pb(fm, 0), AF.Sigmoid, [pbk[fm]], [b_tmp[t1]])
                t2 = get_tmp()
                S.act(tmp[t2][:, :], Pb(fm, 1), AF.Sigmoid, [pbk[fm]], [b_tmp[t2]])
                S.tt("dve", tmp[t1][:, :], Pb(fa, 0), tmp[t1][:, :], ALU.mult, [pbk[fa], b_tmp[t1]], [b_tmp[t1]])
                S.tt("dve", tmp[t2][:, :], Pb(fa, 1), tmp[t2][:, :], ALU.mult, [pbk[fa], b_tmp[t2]], [b_tmp[t2]])
                S.tt("pool", zT[:, m, :], tmp[t1][:, :], tmp[t2][:, :], ALU.add, [b_tmp[t1], b_tmp[t2]], [b_zT[m]])
        def stage_Y2(ci, sx, t0, j):
            if True:
                o = j
                for half in range(2):
                    f = get_bank()
                    for m in range(8):
                        S.mm(P[:, f * 512:(f + 1) * 512], zT[:, m, j * 128:(j + 1) * 128],
                             WOUT[:, m, half * 512:(half + 1) * 512], m == 0, m == 7, [b_zT[m], b_WOUT],
                             [pbk[f]])
                    S.stt(tv[o][:, half * 512:(half + 1) * 512], xin[sx][:, j, half * 512:(half + 1) * 512], ALPHA,
                          P[:, f * 512:(f + 1) * 512], ALU.mult, ALU.add, [b_xin[sx], pbk[f]], [b_tv[o]])
                S.op("dve", lambda h, o=o: h.bn_stats(stat[:, 0, :], tv[o][:, 0:512]), [b_tv[o]], [b_stat])
                S.op("dve", lambda h, o=o: h.bn_stats(stat[:, 1, :], tv[o][:, 512:1024]), [b_tv[o]], [b_stat])
                S.op("dve", lambda h: h.bn_aggr(mv[:, 0:2], stat[:, :, :]), [b_stat], [b_mv])
                S.act(mv[:, 2:3], mv[:, 1:2], AF.Sqrt, [b_mv], [b_mv], bias=LN_EPS, scale=1.0)
                S.recip(mv[:, 2:3], mv[:, 2:3], [b_mv], [b_mv])
                S.ts("dve", mv[:, 3:4], mv[:, 0:1], -1.0, mv[:, 2:3], ALU.mult, ALU.mult, [b_mv], [b_mv])
                S.act(tv[o][:, :], tv[o][:, :], AF.Identity, [b_tv[o], b_mv], [b_tv[o]], bias=mv[:, 3:4], scale=mv[:, 2:3])
                S.tt("pool", tv[o][:, :], tv[o][:, :], lng_bc[:, :], ALU.mult, [b_tv[o], b_lng], [b_tv[o]])
                S.tt("pool", ost[o][:, :], tv[o][:, :], lnb_bc[:, :], ALU.add, [b_tv[o], b_lnb], [b_ost[o]])
                S.dma(out[t0 + j * 128:t0 + (j + 1) * 128, :], ost[o][:, :], [b_ost[o]], (), ch_out[o])

        b_rd = S.buf("rdram")
        b_rst = b_ost
        ch_rl = [S.chan() for _ in range(2)]
        ch_rs = [S.chan() for _ in range(2)]
        if 'nqcB' in DBG:
            for pc in range(DBG['nqcB']):
                i = pc % 2
                S.dma(rsb[i][0:8, 0:512], sdS[:, pc * 512:(pc + 1) * 512], (), [b_rst[i]], ch_rl[i])
                S.recip(rsb[i][0:8, 0:512], rsb[i][0:8, 0:512], [b_rst[i]], [b_rst[i]])
                S.dma(sdS[:, pc * 512:(pc + 1) * 512], rsb[i][0:8, 0:512], [b_rst[i]], [b_rd], ch_rs[i])
        else:
            sd_v = sdS.rearrange("r (s w) -> (r s) w", s=16)
            S.dma(rsb[0][:, 0:512], sd_v, (), [b_rst[0]], ch_rl[0])
            S.recip(rsb[0][:, 0:512], rsb[0][:, 0:512], [b_rst[0]], [b_rst[0]])
            S.dma(sd_v, rsb[0][:, 0:512], [b_rst[0]], [b_rd], ch_rs[0])
        load_x(0)
        if NCH > 1:
            load_x(1)
        load_o(0)
        for seg in range(-1, 3):
            stage_X(0, seg)
        for ci in range(NCH):
            if ci + 2 < NCH:
                load_x(ci + 2)
            nxt = ci + 1 < NCH
            if nxt:
                load_o(ci + 1)
                stage_X(ci + 1, -1)
            stage_Y(ci, 0)
            filler = None
            if nxt:
                stage_X(ci + 1, 0)
                filler = gen_X2(ci + 1, (ci + 1) % 2)
                next(filler, None)
            stage_Y(ci, 1, filler)
            if nxt:
                for _ in filler:
                    pass
                stage_X(ci + 1, 2)
            stage_Y(ci, 2)
            stage_Y(ci, 3)
        S.flush()


def _perm(n):
    q = n // 4
    p = np.concatenate([np.arange(0, q), np.arange(2 * q, 3 * q), np.arange(q, 2 * q), np.arange(3 * q, 4 * q)])
    ps = np.concatenate([p[n // 2:], p[:n // 2]])
    return p, ps


def _rope_table():
    tab = np.zeros((128, 4, NK), np.float32)
    tab[:, 0, :] = 1.0
    tab[:, 2, :] = 1.0
    t = np.arange(T)
    row = (t // 64).astype(np.float32)
    col = (t % 64).astype(np.float32)

    def cs(n):
        q = n // 4
        inv = (ROPE_THETA ** (-np.arange(q, dtype=np.float32) / np.float32(q))).astype(np.float32)
        c = np.zeros((n, T), np.float32)
        s = np.zeros((n, T), np.float32)
        for r in range(n):
            j = r % (n // 2)
            ang = (row * inv[j]) if j < q else (col * inv[j - q])
            ang = ang.astype(np.float32)
            c[r] = np.cos(ang)
            s[r] = -np.sin(ang) if r < n // 2 else np.sin(ang)
        return c, s

    c64, s64 = cs(64)
    c32, s32 = cs(32)
    tab[:, 0, CTX:] = np.concatenate([c64, c64], 0)
    tab[:, 1, CTX:] = np.concatenate([s64, s64], 0)
    tab[64:96, 2, CTX:] = c32
    tab[64:96, 3, CTX:] = s32
    return tab


def _prep_shared(w_mod, b_mod, w_in, mla_q_norm, mla_kv_norm, w_uq, w_ukv, diff_lambda, diff_subln, w_oa, w_ob,
                 w_out, ln_g, ln_b):
    f = lambda a: np.ascontiguousarray(a, dtype=np.float32)
    w_in = w_in[0]
    p32, p32s = _perm(32)
    p64, p64s = _perm(64)
    cols = [w_in[:, 0:640]]
    cols.append(w_in[:, 576:640])
    cols.append(w_in[:, 640:672][:, p32])
    cols.append(w_in[:, 576:640])
    cols.append(w_in[:, 640:672][:, p32s])
    for base in (1184, 1696):
        for pp in (p64, p64s):
            for blk in range(8):
                cols.append(w_in[:, base + blk * 64:base + (blk + 1) * 64][:, pp])
    cols.append(w_in[:, 2208:2720])
    wA = np.concatenate(cols, axis=1)
    assert wA.shape == (D, NA)
    wG = np.concatenate([w_in[:, 672:1184], w_in[:, 2720:3232], w_in[:, 3232:5280]], axis=1)
    uq = w_uq[0].reshape(384, 8, 96)
    uq_n = np.concatenate([uq[:, :, 0:64], uq[:, :, 64:96][:, :, p32]], axis=2).reshape(384, 768)
    uq_s = np.concatenate([uq[:, :, 0:64], uq[:, :, 64:96][:, :, p32s]], axis=2).reshape(384, 768)
    wuq2 = np.concatenate([uq_n, uq_s], axis=1)
    ukv = w_ukv[0].reshape(256, 8, 128)
    wukv = np.concatenate([ukv[:, :, 0:64].reshape(256, 512), ukv[:, :, 64:128].reshape(256, 512)], axis=1)
    return {
        "w_mod": f(w_mod[0]),
        "bmodT": f(b_mod[0].reshape(24, 128).T),
        "bgate": f(b_mod[0][2048:3072].reshape(1, D)),
        "wA": f(wA),
        "wG": f(wG),
        "wuq2": f(wuq2),
        "wukv": f(wukv),
        "qnT": f(mla_q_norm[0].reshape(3, 128).T),
        "kvnT": f(mla_kv_norm[0].reshape(2, 128).T),
        "lamb": f(diff_lambda[0].reshape(1, 256)),
        "sublnT": f(diff_subln[0].reshape(128, 1)),
        "w_oa": f(w_oa[0]),
        "w_ob": f(w_ob[0]),
        "w_out": f(w_out[0]),
        "ln_g": f(ln_g[0].reshape(1, D)),
        "ln_b": f(ln_b[0].reshape(1, D)),
        "rope_tab": _rope_table(),
        "ident": np.eye(128, dtype=np.float32),
    }


def make_in_maps(x, c, ctx, c_ctx, **w):
    x, c, ctx, c_ctx = (np.asarray(a, dtype=np.float32) for a in (x, c, ctx, c_ctx))
    shared = _prep_shared(**{k: np.asarray(v, dtype=np.float32) for k, v in w.items()})
    in_maps = []
    for b in range(x.shape[0]):
        m = dict(shared)
        m["xk"] = np.ascontiguousarray(np.concatenate([ctx[b], x[b]], axis=0))
        cv = np.stack([c[b].reshape(8, 128).T, c_ctx.reshape(8, 128).T], axis=2)
        m["cvec"] = np.ascontiguousarray(cv, dtype=np.float32)
        in_maps.append(m)
    return in_maps


_NC_CACHE = {}


def kernel(x, c, ctx, c_ctx, w_mod, b_mod, w_in, mla_q_norm, mla_kv_norm, w_uq, w_ukv, diff_lambda, diff_subln,
           w_oa, w_ob, w_out, ln_g, ln_b):
    in_maps = make_in_maps(x, c, ctx, c_ctx, w_mod=w_mod, b_mod=b_mod, w_in=w_in, mla_q_norm=mla_q_norm,
                           mla_kv_norm=mla_kv_norm, w_uq=w_uq, w_ukv=w_ukv, diff_lambda=diff_lambda,
                           diff_subln=diff_subln, w_oa=w_oa, w_ob=w_ob, w_out=w_out, ln_g=ln_g, ln_b=ln_b)
    if "nc" not in _NC_CACHE:
        _NC_CACHE["nc"] = build_nc()
    nc = _NC_CACHE["nc"]
    n = len(in_maps)
    res = run_bass_kernel_spmd(nc, in_maps, core_ids=list(range(n)))
    return np.stack([np.asarray(r["out"], dtype=np.float32) for r in res.results], axis=0)
```
